# Optimizing a Trainium2 kernel written in Bass

```python
import math
import jax, jax.numpy as jnp
from jax import lax
import numpy as np

D_MODEL = 1024
BATCH = 8
SEQ = 4096
DEPTH = 2

CHUNK = 64
N_EVEN = (DEPTH + 1) // 2
N_ODD = DEPTH // 2
NORM_EPS = 1e-6

RW_WIDTH = D_MODEL // 2
RW_HEAD = 64
RW_HEADS = RW_WIDTH // RW_HEAD
RW_DECAY_LORA = 64
RW_AAA_LORA = 64
RW_GATE_LORA = 128
RW_COLS = 3 * RW_WIDTH + RW_DECAY_LORA + RW_AAA_LORA + RW_GATE_LORA
RW_LN_EPS = 64e-5

MB_WIDTH = D_MODEL // 2
MB_HEAD = 64
MB_HEADS = MB_WIDTH // MB_HEAD
MB_GROUPS = 2
MB_STATE = 128
MB_CONV = 4
MB_CONV_CH = MB_WIDTH + 2 * MB_GROUPS * MB_STATE
MB_COLS = MB_WIDTH + MB_CONV_CH + MB_HEADS
MB_NORM_EPS = 1e-5

EVEN_COLS = RW_COLS + MB_COLS
EVEN_OUT = RW_WIDTH + MB_WIDTH

ML_HEADS = 8
ML_V_WIDTH = D_MODEL
ML_QK_WIDTH = D_MODEL // 2
ML_DV = ML_V_WIDTH // ML_HEADS
ML_DQK = ML_QK_WIDTH // ML_HEADS
ML_GATE_CAP = 15.0
ODD_COLS = 2 * ML_QK_WIDTH + 2 * ML_V_WIDTH + 2 * ML_HEADS

N_MEM = 256
XA_HEADS = 4
XA_HEAD = D_MODEL // XA_HEADS
D_FF = 4 * D_MODEL

kernel_name = 'hybrid_rwkv7_mamba2_mlstm_trunk'

F32 = jnp.float32


def rmsnorm(x, g, eps=NORM_EPS):
    x32 = x.astype(F32)
    y = x32 * lax.rsqrt(jnp.mean(x32 * x32, axis=-1, keepdims=True) + eps)
    return (y * g.astype(F32)).astype(x.dtype)


def token_shift(x):
    return jnp.pad(x, ((0, 0), (1, 0), (0, 0)))[:, :-1]


def causal_depthwise_conv(x, w, b):
    K = w.shape[0]
    y = lax.conv_general_dilated(x, w[:, None, :], window_strides=(1,), padding=[(K - 1, 0)],
                                 dimension_numbers=('NWC', 'WIO', 'NWC'),
                                 feature_group_count=x.shape[-1])
    return y + b


def softcap(x, cap=ML_GATE_CAP):
    return cap * jnp.tanh(x / cap)


def rwkv7_scan(r, w, k, v, a, b):
    def step(S, inp):
        r_t, w_t, k_t, v_t, a_t, b_t = inp
        sa = jnp.einsum('bhvk,bhk->bhv', S, a_t)
        S = S * w_t[:, :, None, :] + sa[..., None] * b_t[:, :, None, :] + v_t[..., None] * k_t[:, :, None, :]
        return S, jnp.einsum('bhvk,bhk->bhv', S, r_t)
    Bsz, T, H, N = r.shape
    xs = tuple(jnp.moveaxis(t, 1, 0) for t in (r, w, k, v, a, b))
    _, y = lax.scan(step, jnp.zeros((Bsz, H, N, N), F32), xs)
    return jnp.moveaxis(y, 0, 1)


def rwkv7_mix(p, mu, w0, w2, a0, a2, g2, k_k, k_a, r_k, ln_w, ln_b):
    dt_in = p.dtype
    p = p.astype(F32)
    Bsz, T, _ = p.shape
    p = p + (token_shift(p) - p) * mu.astype(F32)
    r, k, v, wd, ad, gd = jnp.split(p, [RW_WIDTH, 2 * RW_WIDTH, 3 * RW_WIDTH,
                                        3 * RW_WIDTH + RW_DECAY_LORA,
                                        3 * RW_WIDTH + RW_DECAY_LORA + RW_AAA_LORA], axis=-1)
    w_log = -jax.nn.softplus(-(w0.astype(F32) + jnp.tanh(wd) @ w2.astype(F32))) - 0.5
    decay = jnp.exp(-jnp.exp(w_log))
    a = jax.nn.sigmoid(a0.astype(F32) + ad @ a2.astype(F32))
    g = jax.nn.sigmoid(gd) @ g2.astype(F32)
    hd = lambda t: t.reshape(Bsz, T, RW_HEADS, RW_HEAD)
    kk = hd(k * k_k.astype(F32))
    kk = kk / jnp.maximum(jnp.sqrt(jnp.sum(kk * kk, axis=-1, keepdims=True)), 1e-12)
    k = k * (1.0 + (a - 1.0) * k_a.astype(F32))
    r_h, k_h, v_h, a_h = hd(r), hd(k), hd(v), hd(a)
    y = rwkv7_scan(r_h, hd(decay), k_h, v_h, -kk, kk * a_h)
    mean = jnp.mean(y, axis=-1, keepdims=True)
    var = jnp.mean(jnp.square(y - mean), axis=-1, keepdims=True)
    y = (y - mean) * lax.rsqrt(var + RW_LN_EPS) * ln_w.astype(F32).reshape(RW_HEADS, RW_HEAD) \
        + ln_b.astype(F32).reshape(RW_HEADS, RW_HEAD)
    y = y + jnp.sum(r_h * k_h * r_k.astype(F32), axis=-1, keepdims=True) * v_h
    return (y.reshape(Bsz, T, RW_WIDTH) * g).astype(dt_in)


def ssd_chunked(x, dt, A, Bm, Cm):
    Bsz, T, H, P = x.shape
    G, N = Bm.shape[2], Bm.shape[3]
    R = H // G
    nc = T // CHUNK
    x = x.reshape(Bsz, nc, CHUNK, G, R, P)
    dt = dt.reshape(Bsz, nc, CHUNK, G, R)
    Bm = Bm.reshape(Bsz, nc, CHUNK, G, N)
    Cm = Cm.reshape(Bsz, nc, CHUNK, G, N)
    acum = jnp.cumsum(dt * A.reshape(G, R), axis=2)
    acum_t = jnp.moveaxis(acum, 2, -1)
    seg = acum_t[..., :, None] - acum_t[..., None, :]
    causal = jnp.tril(jnp.ones((CHUNK, CHUNK), dtype=bool))
    Lmat = jnp.exp(jnp.where(causal, seg, -jnp.inf))
    xdt = x * dt[..., None]
    CB = jnp.einsum('bclgn,bcsgn->bcgls', Cm, Bm)
    y_diag = jnp.einsum('bcgrls,bcsgrp->bclgrp', CB[:, :, :, None] * Lmat, xdt)
    decay_to_end = jnp.exp(acum[:, :, -1:] - acum)
    chunk_states = jnp.einsum('bclgn,bclgr,bclgrp->bcgrpn', Bm, decay_to_end, xdt)
    chunk_decay = jnp.exp(acum[:, :, -1])

    def step(S, inp):
        st, dec = inp
        return S * dec[..., None, None] + st, S
    _, S_prev = lax.scan(step, jnp.zeros((Bsz, G, R, P, N), F32),
                         (jnp.moveaxis(chunk_states, 1, 0), jnp.moveaxis(chunk_decay, 1, 0)))
    S_prev = jnp.moveaxis(S_prev, 0, 1)
    y_off = jnp.einsum('bclgn,bcgrpn,bclgr->bclgrp', Cm, S_prev, jnp.exp(acum))
    return (y_diag + y_off).reshape(Bsz, T, H, P)


def mamba2_mix(p, conv_w, conv_b, dt_bias, A_log, D_skip, norm_w):
    dt_in = p.dtype
    p = p.astype(F32)
    Bsz, T, _ = p.shape
    z, xbc, dt = jnp.split(p, [MB_WIDTH, MB_WIDTH + MB_CONV_CH], axis=-1)
    xbc = jax.nn.silu(causal_depthwise_conv(xbc, conv_w.astype(F32), conv_b.astype(F32)))
    xs, Bm, Cm = jnp.split(xbc, [MB_WIDTH, MB_WIDTH + MB_GROUPS * MB_STATE], axis=-1)
    xs = xs.reshape(Bsz, T, MB_HEADS, MB_HEAD)
    dt = jax.nn.softplus(dt + dt_bias.astype(F32))
    A = -jnp.exp(A_log.astype(F32))
    y = ssd_chunked(xs, dt, A, Bm.reshape(Bsz, T, MB_GROUPS, MB_STATE),
                    Cm.reshape(Bsz, T, MB_GROUPS, MB_STATE))
    y = y + D_skip.astype(F32)[:, None] * xs
    y = (y.reshape(Bsz, T, MB_WIDTH) * jax.nn.silu(z)).reshape(Bsz, T, MB_GROUPS, MB_WIDTH // MB_GROUPS)
    y = y * lax.rsqrt(jnp.mean(y * y, axis=-1, keepdims=True) + MB_NORM_EPS)
    return (y.reshape(Bsz, T, MB_WIDTH) * norm_w.astype(F32)).astype(dt_in)


def mlstm_chunkwise(q, k, v, log_i, log_f):
    Bsz, T, H, DK = q.shape
    DV = v.shape[-1]
    nc = T // CHUNK
    q = q.reshape(Bsz, nc, CHUNK, H, DK)
    k = k.reshape(Bsz, nc, CHUNK, H, DK)
    v = v.reshape(Bsz, nc, CHUNK, H, DV)
    li = log_i.reshape(Bsz, nc, CHUNK, H)
    b = jnp.cumsum(log_f.reshape(Bsz, nc, CHUNK, H), axis=2)
    b_last = b[:, :, -1]
    g_end = b_last[:, :, None] - b + li
    m_loc = jnp.max(g_end, axis=2)
    w_end = jnp.exp(g_end - m_loc[:, :, None])
    C_loc = jnp.einsum('bcsh,bcshv,bcshk->bchvk', w_end, v, k)
    n_loc = jnp.einsum('bcsh,bcshk->bchk', w_end, k)

    def step(carry, inp):
        C, n, m = carry
        Cl, nl, ml, bl = inp
        m_new = jnp.maximum(bl + m, ml)
        s_old = jnp.exp(bl + m - m_new)
        s_loc = jnp.exp(ml - m_new)
        C_new = C * s_old[..., None, None] + Cl * s_loc[..., None, None]
        n_new = n * s_old[..., None] + nl * s_loc[..., None]
        return (C_new, n_new, m_new), (C, n, m)
    init = (jnp.zeros((Bsz, H, DV, DK), F32), jnp.zeros((Bsz, H, DK), F32), jnp.zeros((Bsz, H), F32))
    _, (C_prev, n_prev, m_prev) = lax.scan(
        step, init, tuple(jnp.moveaxis(t, 1, 0) for t in (C_loc, n_loc, m_loc, b_last)))
    C_prev = jnp.moveaxis(C_prev, 0, 1)
    n_prev = jnp.moveaxis(n_prev, 0, 1)
    m_prev = jnp.moveaxis(m_prev, 0, 1)
    inter = b + m_prev[:, :, None]
    Dlog = b[:, :, :, None, :] - b[:, :, None, :, :] + li[:, :, None, :, :]
    causal = jnp.tril(jnp.ones((CHUNK, CHUNK), dtype=bool))[None, None, :, :, None]
    Dlog = jnp.where(causal, Dlog, -jnp.inf)
    m_t = jnp.maximum(inter, jnp.max(Dlog, axis=3))
    S = jnp.einsum('bcthk,bcshk->bctsh', q, k) * jnp.exp(Dlog - m_t[:, :, :, None, :])
    scale_inter = jnp.exp(inter - m_t)
    num = jnp.einsum('bctsh,bcshv->bcthv', S, v) \
        + scale_inter[..., None] * jnp.einsum('bchvk,bcthk->bcthv', C_prev, q)
    den = jnp.sum(S, axis=3) + scale_inter * jnp.einsum('bchk,bcthk->bcth', n_prev, q)
    h = num / jnp.maximum(jnp.abs(den), jnp.exp(-m_t))[..., None]
    return h.reshape(Bsz, T, H, DV)


def mlstm_mix(p, b_gates, norm_w):
    dt_in = p.dtype
    p = p.astype(F32)
    Bsz, T, _ = p.shape
    q, k, v, o, gi, gf = jnp.split(p, [ML_QK_WIDTH, 2 * ML_QK_WIDTH, 2 * ML_QK_WIDTH + ML_V_WIDTH,
                                       2 * ML_QK_WIDTH + 2 * ML_V_WIDTH,
                                       2 * ML_QK_WIDTH + 2 * ML_V_WIDTH + ML_HEADS], axis=-1)
    q = q.reshape(Bsz, T, ML_HEADS, ML_DQK)
    k = k.reshape(Bsz, T, ML_HEADS, ML_DQK) * (ML_DQK ** -0.5)
    v = v.reshape(Bsz, T, ML_HEADS, ML_DV)
    bg = b_gates.astype(F32)
    log_i = softcap(gi + bg[:ML_HEADS])
    log_f = jax.nn.log_sigmoid(softcap(gf + bg[ML_HEADS:]))
    h = mlstm_chunkwise(q, k, v, log_i, log_f)
    h = h * lax.rsqrt(jnp.mean(h * h, axis=-1, keepdims=True) + NORM_EPS) \
        * norm_w.astype(F32).reshape(ML_HEADS, ML_DV)
    return (h.reshape(Bsz, T, ML_V_WIDTH) * jax.nn.sigmoid(o)).astype(dt_in)


def memory_cross_attention(h, mem_n, wq, wk, wv, wo):
    Bsz, T, _ = h.shape
    M = mem_n.shape[1]
    q = (h @ wq).reshape(Bsz, T, XA_HEADS, XA_HEAD)
    k = (mem_n @ wk).reshape(Bsz, M, XA_HEADS, XA_HEAD)
    v = (mem_n @ wv).reshape(Bsz, M, XA_HEADS, XA_HEAD)
    s = jnp.einsum('bthd,bmhd->bhtm', q, k).astype(F32) * (XA_HEAD ** -0.5)
    pr = jax.nn.softmax(s, axis=-1).astype(v.dtype)
    o = jnp.einsum('bhtm,bmhd->bthd', pr, v).reshape(Bsz, T, D_MODEL)
    return o @ wo


def squared_relu_mlp(h, w_up, w_down):
    return jnp.square(jax.nn.relu(h @ w_up)) @ w_down


def setup_inputs(seed: int = 0) -> dict:
    key = jax.random.key(seed)
    ks = iter(jax.random.split(key, 64))
    nrm = lambda shape, scale: jax.random.normal(next(ks), shape, F32) * scale
    gain = lambda shape: 1.0 + nrm(shape, 0.02)
    x = nrm((BATCH, SEQ, D_MODEL), 1.0)
    mem = nrm((BATCH, N_MEM, D_MODEL), 1.0)
    w0_base = -6.0 + 5.0 * (jnp.arange(RW_WIDTH, dtype=F32) / (RW_WIDTH - 1)) ** 0.85
    dt0 = jnp.exp(jax.random.uniform(next(ks), (N_EVEN, MB_HEADS), F32,
                                     minval=math.log(1e-3), maxval=math.log(1e-1)))
    A0 = jax.random.uniform(next(ks), (N_EVEN, MB_HEADS), F32, minval=1.0, maxval=16.0)
    b_i = nrm((N_ODD, ML_HEADS), 0.1)
    b_f = jnp.linspace(3.0, 6.0, ML_HEADS, dtype=F32)[None] + nrm((N_ODD, ML_HEADS), 0.1)
    return {
        'x': x,
        'mem': mem,
        'norm_mix_pre': gain((DEPTH, D_MODEL)),
        'norm_mix_post': gain((DEPTH, D_MODEL)),
        'norm_xa_pre': gain((DEPTH, D_MODEL)),
        'norm_xa_post': gain((DEPTH, D_MODEL)),
        'norm_mem': gain((DEPTH, D_MODEL)),
        'norm_ff_pre': gain((DEPTH, D_MODEL)),
        'norm_ff_post': gain((DEPTH, D_MODEL)),
        'xa_wq': nrm((DEPTH, D_MODEL, D_MODEL), D_MODEL ** -0.5),
        'xa_wk': nrm((DEPTH, D_MODEL, D_MODEL), D_MODEL ** -0.5),
        'xa_wv': nrm((DEPTH, D_MODEL, D_MODEL), D_MODEL ** -0.5),
        'xa_wo': nrm((DEPTH, D_MODEL, D_MODEL), D_MODEL ** -0.5),
        'ff_up': nrm((DEPTH, D_MODEL, D_FF), D_MODEL ** -0.5),
        'ff_down': nrm((DEPTH, D_FF, D_MODEL), D_FF ** -0.5),
        'ev_w_in': nrm((N_EVEN, D_MODEL, EVEN_COLS), D_MODEL ** -0.5),
        'ev_w_out': nrm((N_EVEN, EVEN_OUT, D_MODEL), EVEN_OUT ** -0.5),
        'rw_mu': jax.random.uniform(next(ks), (N_EVEN, RW_COLS), F32),
        'rw_w0': w0_base[None] + nrm((N_EVEN, RW_WIDTH), 0.1),
        'rw_w2': nrm((N_EVEN, RW_DECAY_LORA, RW_WIDTH), 0.5 * RW_DECAY_LORA ** -0.5),
        'rw_a0': nrm((N_EVEN, RW_WIDTH), 0.1),
        'rw_a2': nrm((N_EVEN, RW_AAA_LORA, RW_WIDTH), 0.5 * RW_AAA_LORA ** -0.5),
        'rw_g2': nrm((N_EVEN, RW_GATE_LORA, RW_WIDTH), RW_GATE_LORA ** -0.5),
        'rw_k_k': 0.85 + nrm((N_EVEN, RW_WIDTH), 0.02),
        'rw_k_a': 1.0 + nrm((N_EVEN, RW_WIDTH), 0.02),
        'rw_r_k': -0.04 + nrm((N_EVEN, RW_HEADS, RW_HEAD), 0.1),
        'rw_ln_w': gain((N_EVEN, RW_WIDTH)),
        'rw_ln_b': nrm((N_EVEN, RW_WIDTH), 0.02),
        'mb_conv_w': nrm((N_EVEN, MB_CONV, MB_CONV_CH), MB_CONV ** -0.5),
        'mb_conv_b': nrm((N_EVEN, MB_CONV_CH), 0.02),
        'mb_dt_bias': dt0 + jnp.log(-jnp.expm1(-dt0)),
        'mb_A_log': jnp.log(A0),
        'mb_D': 1.0 + nrm((N_EVEN, MB_HEADS), 0.05),
        'mb_norm_w': gain((N_EVEN, MB_WIDTH)),
        'ml_w_in': nrm((N_ODD, D_MODEL, ODD_COLS), D_MODEL ** -0.5),
        'ml_w_out': nrm((N_ODD, ML_V_WIDTH, D_MODEL), ML_V_WIDTH ** -0.5),
        'ml_b_gates': jnp.concatenate([b_i, b_f], axis=-1),
        'ml_norm_w': gain((N_ODD, ML_V_WIDTH)),
    }


def reference(x, mem, norm_mix_pre, norm_mix_post, norm_xa_pre, norm_xa_post, norm_mem,
              norm_ff_pre, norm_ff_post, xa_wq, xa_wk, xa_wv, xa_wo, ff_up, ff_down,
              ev_w_in, ev_w_out, rw_mu, rw_w0, rw_w2, rw_a0, rw_a2, rw_g2, rw_k_k, rw_k_a,
              rw_r_k, rw_ln_w, rw_ln_b, mb_conv_w, mb_conv_b, mb_dt_bias, mb_A_log, mb_D,
              mb_norm_w, ml_w_in, ml_w_out, ml_b_gates, ml_norm_w):
    for i in range(DEPTH):
        j = i // 2
        hn = rmsnorm(x, norm_mix_pre[i])
        if i % 2 == 0:
            proj = hn @ ev_w_in[j]
            ya = rwkv7_mix(proj[..., :RW_COLS], rw_mu[j], rw_w0[j], rw_w2[j], rw_a0[j], rw_a2[j],
                           rw_g2[j], rw_k_k[j], rw_k_a[j], rw_r_k[j], rw_ln_w[j], rw_ln_b[j])
            yb = mamba2_mix(proj[..., RW_COLS:], mb_conv_w[j], mb_conv_b[j], mb_dt_bias[j],
                            mb_A_log[j], mb_D[j], mb_norm_w[j])
            mix = jnp.concatenate([ya, yb], axis=-1) @ ev_w_out[j]
        else:
            proj = hn @ ml_w_in[j]
            mix = mlstm_mix(proj, ml_b_gates[j], ml_norm_w[j]) @ ml_w_out[j]
        x = x + rmsnorm(mix, norm_mix_post[i])
        mem_n = rmsnorm(mem, norm_mem[i])
        ca = memory_cross_attention(rmsnorm(x, norm_xa_pre[i]), mem_n,
                                    xa_wq[i], xa_wk[i], xa_wv[i], xa_wo[i])
        x = x + rmsnorm(ca, norm_xa_post[i])
        ff = squared_relu_mlp(rmsnorm(x, norm_ff_pre[i]), ff_up[i], ff_down[i])
        x = x + rmsnorm(ff, norm_ff_post[i])
    return x
```

```python
import os
import numpy as np
from contextlib import ExitStack
import concourse.bass as bass
import concourse.mybir as mybir
from concourse.alu_op_type import AluOpType as ALU
from concourse.bass_utils import run_bass_kernel_spmd

F32 = mybir.dt.float32
BF16 = mybir.dt.bfloat16
AF = mybir.ActivationFunctionType

D = 1024
T = 4096
NK = 8
NMEM = 256
DFF = 4096
EPS = 1e-6
N_DSEM = 12
DBG_STOP = float(os.environ.get('MK_STOP', '99'))
PE_SWITCH_NS = float(os.environ.get('MK_PESW', '400'))


class Op:
    __slots__ = ("eng", "fn", "deps", "dma", "sig", "waits", "signal", "clock")

    def __init__(self, eng, fn, deps, dma):
        self.eng = eng
        self.fn = fn
        self.deps = deps
        self.dma = dma
        self.sig = None
        self.waits = None
        self.signal = False
        self.clock = None


class Sched:
    ENGS = ("pe", "act", "dve", "pool", "sp")

    def __init__(self):
        self.ops = []
        self.res = {}
        self.pending = {e: set() for e in self.ENGS}
        self.last_op = {}
        self.dmas = []
        self._cap = None
        self.fin = []
        self.eng_free = {}
        self.pe_mode = None

    def add(self, eng, fn, r=(), w=(), dma=False, wacc=(), cost=500.0, mode=None):
        if self._cap is not None:
            self._cap.append((eng, fn, tuple(r), tuple(w), dma, tuple(wacc), cost, mode))
            return None
        idx = len(self.ops)
        deps = self.pending[eng]
        if deps:
            self.pending[eng] = set()
        else:
            deps = set()
        res = self.res
        for t in r:
            st = res.get(t)
            if st is None:
                st = res[t] = [None, []]
            if st[0] is not None:
                if isinstance(st[0], list):
                    deps.update(st[0])
                else:
                    deps.add(st[0])
            st[1].append(idx)
        for t in wacc:
            st = res.get(t)
            if st is None:
                st = res[t] = [None, []]
            deps.update(st[1])
            if isinstance(st[0], list):
                st[0].append(idx)
            else:
                if st[0] is not None:
                    deps.add(st[0])
                st[0] = [idx]
            st[1] = []
        for t in w:
            st = res.get(t)
            if st is None:
                st = res[t] = [None, []]
            if st[0] is not None:
                if isinstance(st[0], list):
                    deps.update(st[0])
                else:
                    deps.add(st[0])
            if st[1]:
                ops = self.ops
                lastc = {}
                for ri in st[1]:
                    rop = ops[ri] if ri < idx else None
                    if rop is None:
                        continue
                    if rop.dma:
                        deps.add(ri)
                    else:
                        lastc[rop.eng] = ri
                deps.update(lastc.values())
            st[0] = idx
            st[1] = []
        deps.discard(idx)
        self.ops.append(Op(eng, fn, deps, dma))
        fin = self.fin
        ready = 0.0
        for d in deps:
            fd = fin[d]
            if fd > ready:
                ready = fd
        if dma:
            fin.append(max(ready + 100.0, self.eng_free.get(eng, 0.0)) + cost)
            self.eng_free[eng] = max(ready, self.eng_free.get(eng, 0.0)) + 60.0
            self.dmas.append(idx)
        else:
            st_ = max(ready + 120.0, self.eng_free.get(eng, 0.0))
            if mode is not None:
                if mode != self.pe_mode:
                    st_ += PE_SWITCH_NS
                self.pe_mode = mode
            fin.append(st_ + cost)
            self.eng_free[eng] = st_ + cost
            self.last_op[eng] = idx
        return idx

    def peek_start(self, a):
        eng, _, r, w, dma, wacc, cost, mode = a
        res, fin = self.res, self.fin
        ready = 0.0
        for t in r:
            st = res.get(t)
            if st is not None and st[0] is not None:
                for d in (st[0] if isinstance(st[0], list) else (st[0],)):
                    if fin[d] > ready:
                        ready = fin[d]
        for t in tuple(w) + tuple(wacc):
            st = res.get(t)
            if st is not None:
                if st[0] is not None:
                    for d in (st[0] if isinstance(st[0], list) else (st[0],)):
                        if fin[d] > ready:
                            ready = fin[d]
                for d in st[1]:
                    if fin[d] > ready:
                        ready = fin[d]
        pen = PE_SWITCH_NS if (mode is not None and mode != self.pe_mode) else 0.0
        return max(ready + 120.0, self.eng_free.get(eng, 0.0)) + pen

    def merge_n(self, threads, prio=None):
        pos = [0] * len(threads)
        prio = prio or [0.0] * len(threads)
        while True:
            best, bi = None, -1
            for i, th in enumerate(threads):
                if pos[i] < len(th):
                    st = self.peek_start(th[pos[i]]) + prio[i]
                    if best is None or st < best:
                        best, bi = st, i
            if bi < 0:
                break
            a = threads[bi][pos[bi]]
            self.add(a[0], a[1], r=a[2], w=a[3], dma=a[4], wacc=a[5], cost=a[6], mode=a[7])
            pos[bi] += 1

    def merge_greedy(self, A, B, bias=0.0):
        ia = ib = 0
        while ia < len(A) or ib < len(B):
            if ib >= len(B):
                pick = 0
            elif ia >= len(A):
                pick = 1
            else:
                sa = self.peek_start(A[ia])
                sb_ = self.peek_start(B[ib])
                pick = 0 if sa <= sb_ + bias else 1
            a = A[ia] if pick == 0 else B[ib]
            self.add(a[0], a[1], r=a[2], w=a[3], dma=a[4], wacc=a[5], cost=a[6], mode=a[7])
            if pick == 0:
                ia += 1
            else:
                ib += 1

    def capture(self, fn, *args):
        prev = self._cap
        self._cap = []
        fn(*args)
        caps, self._cap = self._cap, prev
        return caps

    def merge(self, A, B):
        nb = 0
        for i, a in enumerate(A):
            self.add(*a[:2], r=a[2], w=a[3], dma=a[4], wacc=a[5], cost=a[6], mode=a[7])
            tgt = ((i + 1) * len(B)) // max(1, len(A))
            while nb < tgt:
                b = B[nb]
                self.add(*b[:2], r=b[2], w=b[3], dma=b[4], wacc=b[5], cost=b[6], mode=b[7])
                nb += 1
        while nb < len(B):
            b = B[nb]
            self.add(*b[:2], r=b[2], w=b[3], dma=b[4], wacc=b[5], cost=b[6], mode=b[7])
            nb += 1

    def barrier(self):
        s = set(self.last_op.values()) | set(self.dmas)
        self.dmas = []
        for e in self.ENGS:
            self.pending[e] |= s

    def emit(self, nc, es):
        ops = self.ops
        n = len(ops)
        for op in ops:
            for d in op.deps:
                dop = ops[d]
                if dop.eng == "pe" and op.eng == "pe" and not dop.dma:
                    continue
                dop.signal = True
        sems = {}
        for e in ("pe", "act", "dve", "pool"):
            sems[e] = es.enter_context(nc.semaphore("s_" + e))
        for q in ("sp", "pool", "act"):
            for i in range(N_DSEM):
                sems[("d", q, i)] = es.enter_context(nc.semaphore("d_%s_%d" % (q, i)))
        cnt = {e: 0 for e in self.ENGS}
        dcnt = {e: 0 for e in self.ENGS}
        know = {e: {} for e in self.ENGS}
        nwaits = 0
        for op in ops:
            K = know[op.eng]
            waits = {}
            if op.dma:
                k = dcnt[op.eng]
                dcnt[op.eng] += 1
                key = ("d", op.eng, k % N_DSEM)
                op.sig = (key, 16 * (k // N_DSEM + 1))
                op.signal = True
                if k >= N_DSEM and K.get(key, 0) < 16 * (k // N_DSEM):
                    waits[key] = 16 * (k // N_DSEM)
                    K[key] = 16 * (k // N_DSEM)
            elif op.signal:
                cnt[op.eng] += 1
                op.sig = (op.eng, cnt[op.eng])
            for d in sorted(op.deps):
                dop = ops[d]
                if dop.eng == "pe" and op.eng == "pe" and not dop.dma:
                    continue
                key, val = dop.sig
                if K.get(key, 0) >= val:
                    continue
                if waits.get(key, 0) < val:
                    waits[key] = val
                for kk, vv in dop.clock.items():
                    if K.get(kk, 0) < vv:
                        K[kk] = vv
            op.waits = list(waits.items())
            nwaits += len(op.waits)
            if op.signal:
                c = dict(K)
                c[op.sig[0]] = op.sig[1]
                op.clock = c
        self.stats = dict(n_ops=n, n_waits=nwaits, cnt=dict(cnt), dcnt=dict(dcnt))
        self.check()
        engmap = {"pe": "tensor", "act": "scalar", "dve": "vector", "pool": "gpsimd", "sp": "sync"}
        with nc.Block() as block:
            for e in self.ENGS:
                mine = [op for op in ops if op.eng == e]
                if not mine:
                    continue

                def body(eng, mine=mine):
                    for op in mine:
                        for key, val in op.waits:
                            eng.wait_ge(sems[key], val)
                        ins = op.fn(eng)
                        if op.signal:
                            ins.then_inc(sems[op.sig[0]], 16 if op.dma else 1)

                getattr(block, engmap[e])(body)


def _sched_check(self):
    per = {e: [op for op in self.ops if op.eng == e] for e in self.ENGS}
    ptr = {e: 0 for e in self.ENGS}
    val = {}
    progress = True
    while progress:
        progress = False
        for e in self.ENGS:
            while ptr[e] < len(per[e]):
                op = per[e][ptr[e]]
                if all(val.get(k, 0) >= v for k, v in op.waits):
                    if op.signal:
                        val[op.sig[0]] = val.get(op.sig[0], 0) + (16 if op.dma else 1)
                        assert val[op.sig[0]] == op.sig[1], (op.sig, val[op.sig[0]])
                    ptr[e] += 1
                    progress = True
                else:
                    break
    stuck = {e: (ptr[e], len(per[e])) for e in self.ENGS if ptr[e] < len(per[e])}
    assert not stuck, "sync deadlock: %s" % stuck


Sched.check = _sched_check


class Ctx:
    def __init__(self, nc):
        self.nc = nc
        self.s = Sched()
        self.uid = 0

    def name(self, base):
        self.uid += 1
        return "%s_%d" % (base, self.uid)

    @staticmethod
    def _n(ap):
        n = 1
        for d in ap.shape[1:]:
            n *= int(d)
        return n

    def _cost(self, eng, ap):
        n = self._n(ap)
        if eng == "dve":
            return n / 0.96 + 150.0
        if eng == "act":
            return n / 1.2 + 250.0
        if eng == "pool":
            return n * 2.4 + 200.0
        return 500.0

    def mm(self, out, lhsT, rhs, start, stop, r, w, **kw):
        n = max(64, self._n(rhs))
        cost = (n / 2.4 + 12.0) * (4.0 if rhs.dtype == F32 else 1.0)
        rnd = lambda v: 32 if v <= 32 else (64 if v <= 64 else 128)
        mode = (rnd(int(lhsT.shape[0])), rnd(self._n(lhsT)), rhs.dtype == F32)
        self.s.add("pe", lambda e: e.matmul(out, lhsT, rhs, start=start, stop=stop, **kw), r=r, w=w, cost=cost, mode=mode)

    def tr(self, out, in_, ident, r, w):
        rnd = lambda v: 32 if v <= 32 else (64 if v <= 64 else 128)
        mode = ("T", rnd(int(in_.shape[0])), rnd(self._n(in_)), in_.dtype == F32)
        self.s.add("pe", lambda e: e.transpose(out, in_, ident), r=r, w=w, cost=70.0, mode=mode)

    def act(self, out, in_, func, r, w, bias=None, scale=None, accum_out=None, eng="act"):
        kw = {}
        if bias is not None:
            kw["bias"] = bias
        if scale is not None:
            kw["scale"] = scale
        if accum_out is not None:
            kw["accum_out"] = accum_out
        self.s.add(eng, lambda e: e.activation(out, in_, func, **kw), r=r, w=w, cost=self._cost(eng, out))

    def tt(self, out, in0, in1, op, r, w, eng="dve"):
        self.s.add(eng, lambda e: e.tensor_tensor(out, in0, in1, op), r=r, w=w, cost=self._cost(eng, out))

    def ts(self, out, in0, s1, op0, r, w, s2=None, op1=None, eng="dve"):
        if op1 is None:
            self.s.add(eng, lambda e: e.tensor_scalar(out, in0, s1, None, op0), r=r, w=w, cost=self._cost(eng, out))
        else:
            self.s.add(eng, lambda e: e.tensor_scalar(out, in0, s1, s2, op0, op1), r=r, w=w, cost=self._cost(eng, out))

    def stt(self, out, in0, scalar, in1, op0, op1, r, w):
        self.s.add("dve", lambda e: e.scalar_tensor_tensor(out, in0, scalar, in1, op0, op1), r=r, w=w, cost=self._cost("dve", out))

    def cp(self, out, in_, r, w, eng="dve"):
        if eng == "act":
            self.s.add("act", lambda e: e.copy(out, in_), r=r, w=w, cost=self._cost("act", out))
        else:
            self.s.add(eng, lambda e: e.tensor_copy(out, in_), r=r, w=w, cost=self._cost(eng, out))

    def recip(self, out, in_, r, w):
        self.s.add("dve", lambda e: e.reciprocal(out, in_), r=r, w=w, cost=self._cost("dve", out))

    def memset(self, ap, val, w, eng="pool"):
        self.s.add(eng, lambda e: e.memset(ap, val), w=w)

    def dma(self, out, in_, r, w, q="sp", wacc=(), **kw):
        self.s.add(q, lambda e: e.dma_start(out=out, in_=in_, **kw), r=r, w=w, dma=True, wacc=wacc,
                   cost=2500.0 + self._n(out) * 128 * 4 / 150.0)


def load_weight_bf16(c, dst, src, rows_tok, cols, wtok, kchunks, tokfn=None):
    srcv = src.rearrange("(k p) n -> p k n", p=128)
    step = 2048
    for k in range(kchunks):
        for c0 in range(0, cols, step):
            c1 = min(cols, c0 + step)
            c.dma(dst[:, k, c0:c1], srcv[:, k, c0:c1], r=(), w=(), wacc=(wtok if tokfn is None else tokfn(k, c0),), q="pool")


class Common:
    def __init__(self, c, es, TT, lite=False):
        nc = c.nc
        self.c = c
        self.TT = TT
        sb = lambda name, shape, dt: es.enter_context(nc.sbuf_tensor(c.name(name), shape, dt))
        self.sb = sb
        self.xs = [sb("x", [128, NK, TT], F32) for _ in range(2)]
        self.hn = [sb("hn", [128, NK, TT], BF16) for _ in range(2)]
        self.sq = sb("sq", [128, NK, TT], BF16)
        self.rstd = [sb("rstd", [128, TT], F32) for _ in range(2)]
        if not lite:
            self.tmp = [sb("tmp", [128, TT], F32) for _ in range(2)]
            self.yy = sb("yy", [128, NK, TT], F32)
        self.nrm = 0
        self.pb = 0
        self.gpb = {}

    def bank(self, group=None):
        if group is None:
            self.pb = (self.pb + 1) % 7
            return self.pb
        lo, n = group
        k = self.gpb.get(group, 0)
        self.gpb[group] = k + 1
        return lo + k % n

    def load_x(self, it, x_in):
        c, TT = self.c, self.TT
        b = it % 2
        t0 = it * TT
        c.dma(self.xs[b][:], x_in[0][:, :, t0:t0 + TT].rearrange("k p t -> p k t"),
              r=tuple(("dram", x_in[1], j) for j in range(t0 // 64, (t0 + TT) // 64)), w=(("x", b),))
        return self.xs[b], ("x", b)

    def store_x(self, it, x_out):
        c, TT = self.c, self.TT
        b = it % 2
        t0 = it * TT
        c.dma(x_out[0][:, :, t0:t0 + TT].rearrange("k p t -> p k t"), self.xs[b][:], r=(("x", b),),
              w=tuple(("dram", x_out[1], j) for j in range(t0 // 64, (t0 + TT) // 64)))

    def stats(self, src, tok_src, n=None):
        c = self.c
        n = n or self.TT
        P = c.psum
        self.nrm += 1
        rb = self.nrm % 2
        c.act(self.sq[:, :, 0:n], src, AF.Square, r=(tok_src,), w=("sq",))
        for k in range(NK):
            c.mm(P[7][:, 0:n], c.ones, self.sq[:, k, 0:n], start=(k == 0), stop=(k == NK - 1), r=("sq",), w=(("ps", 7),))
        rs = self.rstd[rb][:, 0:n]
        tok = ("rstd", rb)
        c.act(rs, P[7][:, 0:n], AF.Ln, r=(("ps", 7),), w=(tok,), bias=c.eps_ap, scale=1.0 / D)
        c.act(rs, rs, AF.Exp, r=(tok,), w=(tok,), scale=-0.5)
        return rs, tok

    def prenorm(self, it, g):
        c = self.c
        b = it % 2
        X, tx = self.xs[b], ("x", b)
        rs, tok = self.stats(X[:], tx)
        for k in range(NK):
            c.stt(self.hn[b][:, k, :], X[:, k, :], g[:, k:k + 1], rs, ALU.mult, ALU.mult, r=(tx, tok), w=(("hn", b),))
        return self.hn[b], ("hn", b)

    def get_hn(self, it, g, x_in):
        if getattr(self, "pref", None) == it:
            return self.hn[it % 2], ("hn", it % 2)
        self.load_x(it, x_in)
        return self.prenorm(it, g)

    def prefetch(self, it, g, x_in, NT):
        if it < NT:
            self.load_x(it, x_in)
            self.prenorm(it, g)
            self.pref = it

    def post_residual(self, it, g):
        c = self.c
        b = it % 2
        X, tx = self.xs[b], ("x", b)
        rs, tok = self.stats(self.yy[:], "yy")
        for k in range(NK):
            tb = k % 2
            c.stt(self.tmp[tb][:], self.yy[:, k, :], g[:, k:k + 1], rs, ALU.mult, ALU.mult, r=("yy", tok), w=(("tmp", tb),))
            c.tt(X[:, k, :], self.tmp[tb][:], X[:, k, :], ALU.add, r=(("tmp", tb), tx), w=(tx,),
                 eng=("dve" if os.environ.get("MK_VAR2", "") == "E" else "pool"))


def phase_mlp(c, layer, x_in, x_out, w_up, w_down, cst):
    nc = c.nc
    TT = 256
    NT = T // TT
    NF = DFF // 128
    with ExitStack() as es:
        cm = Common(c, es, TT)
        sb = cm.sb
        wu = sb("wu", [128, NK, DFF], BF16)
        wd = sb("wd", [128, NF, D], BF16)
        hh = sb("hh", [128, NF, TT], BF16)
        rr = [sb("rr", [128, TT], BF16) for _ in range(2)]
        P = c.psum
        g1 = cst["norm_ff_pre"][layer]
        g2 = cst["norm_ff_post"][layer]
        load_weight_bf16(c, wu, w_up, D, DFF, "wu", NK, tokfn=lambda k, c0: ("wu", c0 // 2048))
        load_weight_bf16(c, wd, w_down, DFF, D, "wd", NF, tokfn=lambda k, c0: ("wd", k))
        for it in range(NT):
            hn, thn = cm.get_hn(it, g1, x_in)
            for f in range(NF):
                pb = cm.bank()
                for k in range(NK):
                    c.mm(P[pb][:, 0:TT], wu[:, k, f * 128:(f + 1) * 128], hn[:, k, :], start=(k == 0), stop=(k == NK - 1),
                         r=(("wu", f // 16), thn), w=(("ps", pb),))
                rb = f % 2
                c.act(rr[rb][:], P[pb][:, 0:TT], AF.Relu, r=(("ps", pb),), w=(("rr", rb),))
                c.tt(hh[:, f, :], P[pb][:, 0:TT], rr[rb][:], ALU.mult, r=(("ps", pb), ("rr", rb)), w=(("hh", f),))
            cm.prefetch(it + 1, g1, x_in, NT)
            for o in range(NK):
                pb = cm.bank()
                for f in range(NF):
                    c.mm(P[pb][:, 0:TT], wd[:, f, o * 128:(o + 1) * 128], hh[:, f, :], start=(f == 0), stop=(f == NF - 1),
                         r=(("wd", f), ("hh", f)), w=(("ps", pb),))
                c.cp(cm.yy[:, o, :], P[pb][:, 0:TT], r=(("ps", pb),), w=("yy",), eng="act")
            cm.post_residual(it, g2)
            cm.store_x(it, x_out)
        c.s.barrier()


def phase_xattn(c, layer, x_in, x_out, memT, wq_d, wk_d, wv_d, wo_d, cst):
    nc = c.nc
    TT = 512
    NT = T // TT
    with ExitStack() as es:
        cm = Common(c, es, TT)
        sb = cm.sb
        wq = sb("wq", [128, NK, D], BF16)
        wo = sb("wo", [128, NK, D], BF16)
        kT = sb("kT", [128, NK, NMEM], BF16)
        V = sb("V", [128, 2, D], BF16)
        qT = sb("qT", [128, NK, TT], BF16)
        oT = sb("oT", [128, NK, TT], BF16)
        E = [sb("E", [128, 2, TT], BF16) for _ in range(2)]
        rden = [sb("rden", [128, TT], F32) for _ in range(2)]
        P = c.psum
        g_pre = cst["norm_xa_pre"][layer]
        g_post = cst["norm_xa_post"][layer]
        g_mem = cst["norm_mem"][layer]
        with ExitStack() as es2:
            sb2 = lambda name, shape, dt: es2.enter_context(nc.sbuf_tensor(c.name(name), shape, dt))
            wk = sb2("wk", [128, NK, D], BF16)
            wv = sb2("wv", [128, NK, D], BF16)
            load_weight_bf16(c, wk, wk_d, D, D, "wk", NK)
            load_weight_bf16(c, wv, wv_d, D, D, "wv", NK)
            load_weight_bf16(c, wq, wq_d, D, D, "wq", NK)
            load_weight_bf16(c, wo, wo_d, D, D, "wo", NK)
            mx = cm.xs[1]
            c.dma(mx[:, :, 0:NMEM], memT.rearrange("k p t -> p k t"), r=(), w=(("x", 1),))
            rs, tok = cm.stats(mx[:, :, 0:NMEM], ("x", 1), n=NMEM)
            mn = cm.hn[1]
            for k in range(NK):
                c.stt(mn[:, k, 0:NMEM], mx[:, k, 0:NMEM], g_mem[:, k:k + 1], rs, ALU.mult, ALU.mult,
                      r=(("x", 1), tok), w=(("hn", 1),))
            for cc in range(NK):
                pb = cm.bank()
                for k in range(NK):
                    c.mm(P[pb][:, 0:NMEM], wk[:, k, cc * 128:(cc + 1) * 128], mn[:, k, 0:NMEM], start=(k == 0), stop=(k == NK - 1),
                         r=("wk", ("hn", 1)), w=(("ps", pb),))
                c.cp(kT[:, cc, :], P[pb][:, 0:NMEM], r=(("ps", pb),), w=("kT",), eng="act")
            for mc in range(2):
                for hf in range(2):
                    pb = cm.bank()
                    for k in range(NK):
                        c.mm(P[pb][:, :], mn[:, k, mc * 128:(mc + 1) * 128], wv[:, k, hf * 512:(hf + 1) * 512],
                             start=(k == 0), stop=(k == NK - 1), r=("wv", ("hn", 1)), w=(("ps", pb),))
                    c.cp(V[:, mc, hf * 512:(hf + 1) * 512], P[pb][:, :], r=(("ps", pb),), w=("V",), eng="dve")
            c.s.barrier()
        for it in range(NT):
            hn, thn = cm.get_hn(it, g_pre, x_in)
            for cc in range(NK):
                pb = cm.bank()
                for k in range(NK):
                    c.mm(P[pb][:, :], wq[:, k, cc * 128:(cc + 1) * 128], hn[:, k, :], start=(k == 0), stop=(k == NK - 1),
                         r=("wq", thn), w=(("ps", pb),))
                c.act(qT[:, cc, :], P[pb][:, :], AF.Copy, r=(("ps", pb),), w=(("qT", cc),), scale=1.0 / 16.0)
            cm.prefetch(it + 1, g_pre, x_in, NT)
            def scores(h):
                eb = h % 2
                for mc in range(2):
                    pb = cm.bank()
                    for ci in range(2):
                        cc = 2 * h + ci
                        c.mm(P[pb][:, :], kT[:, cc, mc * 128:(mc + 1) * 128], qT[:, cc, :], start=(ci == 0), stop=(ci == 1),
                             r=("kT", ("qT", cc)), w=(("ps", pb),))
                    c.act(E[eb][:, mc, :], P[pb][:, :], AF.Exp, r=(("ps", pb),), w=(("E", eb, mc),))

            def attend(h):
                eb = h % 2
                pd = cm.bank()
                for mc in range(2):
                    c.mm(P[pd][:, :], c.ones, E[eb][:, mc, :], start=(mc == 0), stop=(mc == 1), r=(("E", eb, mc),), w=(("ps", pd),))
                c.recip(rden[eb][:], P[pd][:, :], r=(("ps", pd),), w=(("rden", eb),))
                for ci in range(2):
                    cc = 2 * h + ci
                    pb = cm.bank()
                    for mc in range(2):
                        c.mm(P[pb][:, :], V[:, mc, cc * 128:(cc + 1) * 128], E[eb][:, mc, :], start=(mc == 0), stop=(mc == 1),
                             r=("V", ("E", eb, mc)), w=(("ps", pb),))
                    c.tt(oT[:, cc, :], P[pb][:, :], rden[eb][:], ALU.mult, r=(("ps", pb), ("rden", eb)), w=(("oT", cc),))

            scores(0)
            for h in range(4):
                if h + 1 < 4:
                    scores(h + 1)
                attend(h)
            for o in range(NK):
                pb = cm.bank()
                for cc in range(NK):
                    c.mm(P[pb][:, :], wo[:, cc, o * 128:(o + 1) * 128], oT[:, cc, :], start=(cc == 0), stop=(cc == NK - 1),
                         r=("wo", ("oT", cc)), w=(("ps", pb),))
                c.cp(cm.yy[:, o, :], P[pb][:, :], r=(("ps", pb),), w=("yy",), eng="act")
            cm.post_residual(it, g_post)
            cm.store_x(it, x_out)
        c.s.barrier()


def phase_mlstm(c, layer, j, x_in, x_out, w_in_d, w_out_d, cst):
    nc = c.nc
    TT = 256
    NT = T // TT
    NC_ = TT // 64
    NG = 2
    GH = 8 // NG
    with ExitStack() as es:
        cm = Common(c, es, TT)
        sb = cm.sb
        P = c.psum
        win = sb("mlwin", [128, NK, 3088], BF16)
        wout = sb("mlwout", [128, NK, D], BF16)
        load_weight_bf16(c, win, w_in_d, D, 3088, "win", NK)
        load_weight_bf16(c, wout, w_out_d, D, D, "wout", NK)
        g_pre = cst["norm_mix_pre"][layer]
        g_post = cst["norm_mix_post"][layer]
        nw = cst["ml_norm_w"][j]
        bi = cst["ml_b_i"][j]
        bf_ = cst["ml_b_f"][j]
        selh = cst["selh"][0]
        ident = cst["ident"][0]
        mask = cst["mask_incl"][0]
        reset = cst["reset"][0]
        gsm = lambda nm: sb(nm, [8, TT], F32)
        b15 = sb("b15", [8, 2], F32)
        one1 = sb("one1", [8, 1], F32)
        onesb = sb("onesb", [64, 128], BF16)
        th_i, th_f, e1, l1, bneg, arg, arg2 = [gsm(n) for n in ("thi", "thf", "e1", "l1", "bneg", "arg", "arg2")]
        eb, ek, eend = gsm("eb"), gsm("ek"), gsm("eend")
        EB = sb("EB", [64, 8, TT], F32)
        EK = sb("EK", [64, 8, TT], F32)
        qt = sb("qt", [64, 8, TT], BF16)
        kt = sb("kt", [64, 8, TT], BF16)
        eT = [sb("eT", [64, 8], F32) for _ in range(NC_)]
        KhT = [sb("KhT", [64, 8, 64], BF16) for _ in range(NC_)]
        Va = [sb("Va", [64, 8, 128], BF16) for _ in range(NC_)]
        ST = [sb("ST", [64, 8, 64], BF16) for _ in range(2)]
        CT = sb("CT", [64, 8, 128], F32)
        CTb = sb("CTb", [64, 8, 128], BF16)
        nst = sb("nst", [64, 8], F32)
        nrep = sb("nrep", [64, 8, 128], BF16)
        dmax = sb("dmax", [128, 512], F32)
        hT = [sb("hT", [128, 8, TT], F32) for _ in range(2)]
        rsh = [sb("rsh", [128, TT], F32) for _ in range(2 * NG)]
        sg = [sb("sg", [128, TT], F32) for _ in range(2 * NG)]
        t1 = [sb("t1", [128, TT], F32) for _ in range(2 * NG)]
        mix = sb("mix", [128, 8, TT], BF16)
        c.memset(CT[:], 0.0, w=tuple(("CT", g) for g in range(NG)))
        c.memset(CTb[:], 0.0, w=tuple(("CTb", g) for g in range(NG)))
        c.memset(nst[:], 0.0, w=tuple(("nst", g) for g in range(NG)))
        c.memset(nrep[:], 0.0, w=tuple(("nrep", g) for g in range(NG)))
        c.memset(onesb[:], 1.0, w=("onesb",))
        c.memset(one1[:], 1.0, w=("one1",))
        c.ts(b15[:, 0:1], bi[0:8, :], 1.0 / 15.0, ALU.mult, r=(), w=("b15",))
        c.ts(b15[:, 1:2], bf_[0:8, :], 1.0 / 15.0, ALU.mult, r=(), w=("b15",))
        V3 = lambda ap, d: ap.rearrange("p (h d) -> p h d", d=d)
        GBANK = [(0, 3), (3, 3)]
        CB_ = (6, 1)

        def head(it):
            hn, thn = cm.get_hn(it, g_pre, x_in)
            pgi, pgf = cm.bank(CB_), cm.bank(GBANK[0])
            for (pg, c0) in ((pgi, 3072), (pgf, 3080)):
                for k in range(NK):
                    c.mm(P[pg][0:8, 0:TT], win[:, k, c0:c0 + 8], hn[:, k, :], start=(k == 0), stop=(k == NK - 1),
                         r=("win", thn), w=(("ps", pg),))
            c.act(th_i[:], P[pgi][0:8, 0:TT], AF.Tanh, r=(("ps", pgi), "b15"), w=("thi",), bias=b15[:, 0:1], scale=1.0 / 15.0)
            c.act(th_f[:], P[pgf][0:8, 0:TT], AF.Tanh, r=(("ps", pgf), "b15"), w=("thf",), bias=b15[:, 1:2], scale=1.0 / 15.0)
            c.act(e1[:], th_f[:], AF.Exp, r=("thf",), w=("e1",), scale=-15.0)
            c.act(l1[:], e1[:], AF.Ln, r=("e1", "one1"), w=("l1",), bias=one1[:], scale=1.0)
            c.s.add("dve", lambda e: e.tensor_tensor_scan(bneg[:], reset[0:8, 0:TT], l1[:], 0.0, ALU.mult, ALU.add),
                    r=("l1",), w=("bneg",))
            c.act(eb[:], bneg[:], AF.Exp, r=("bneg",), w=("eb",), scale=-1.0)
            c.stt(arg[:], th_i[:], 15.0, bneg[:], ALU.mult, ALU.add, r=("thi", "bneg"), w=("arg",))
            c.act(ek[:], arg[:], AF.Exp, r=("arg",), w=("ek",))
            a3 = arg[:].rearrange("p (c l) -> p c l", l=64)
            bL = bneg[:].rearrange("p (c l) -> p c l", l=64)[:, :, 63:64].broadcast_to([8, NC_, 64])
            c.tt(arg2[:].rearrange("p (c l) -> p c l", l=64), a3, bL, ALU.subtract, r=("arg", "bneg"), w=("arg2",))
            c.act(eend[:], arg2[:], AF.Exp, r=("arg2",), w=("eend",))
            for jc in range(NC_):
                tsl = slice(jc * 64, (jc + 1) * 64)
                pt = cm.bank(CB_)
                c.tr(P[pt][0:64, 0:8], eend[0:8, tsl], ident[0:8, 0:8], r=("eend",), w=(("ps", pt),))
                c.ts(eT[jc][:], P[pt][0:64, 0:8], 0.125, ALU.mult, r=(("ps", pt),), w=(("eT", jc),))

        def group(it, g):
            b = it % 2
            G = GBANK[g]
            hn, thn = cm.hn[b], ("hn", b)
            hsl = slice(g * GH, (g + 1) * GH)
            for h2 in range(g * GH // 2, (g + 1) * GH // 2):
                pe_, pk_ = cm.bank(G), cm.bank(G)
                for i in range(2):
                    h = 2 * h2 + i
                    cs = slice(i * TT, (i + 1) * TT)
                    c.mm(P[pe_][0:64, cs], selh[0:8, h * 64:(h + 1) * 64], eb[:], start=True, stop=True, r=("eb",), w=(("ps", pe_),))
                    c.mm(P[pk_][0:64, cs], selh[0:8, h * 64:(h + 1) * 64], ek[:], start=True, stop=True, r=("ek",), w=(("ps", pk_),))
                hs2 = slice(2 * h2, 2 * h2 + 2)
                c.cp(EB[:, hs2, :], V3(P[pe_][0:64, :], TT), r=(("ps", pe_),), w=(("EB", h2),), eng="act")
                c.cp(EK[:, hs2, :], V3(P[pk_][0:64, :], TT), r=(("ps", pk_),), w=(("EK", h2),), eng="act")
                pq, pk2 = cm.bank(G), cm.bank(G)
                for i in range(2):
                    h = 2 * h2 + i
                    cs = slice(i * TT, (i + 1) * TT)
                    for k in range(NK):
                        c.mm(P[pq][0:64, cs], win[:, k, h * 64:(h + 1) * 64], hn[:, k, :], start=(k == 0), stop=(k == NK - 1),
                             r=("win", thn), w=(("ps", pq),))
                    for k in range(NK):
                        c.mm(P[pk2][0:64, cs], win[:, k, 512 + h * 64:512 + (h + 1) * 64], hn[:, k, :], start=(k == 0),
                             stop=(k == NK - 1), r=("win", thn), w=(("ps", pk2),))
                c.tt(qt[:, hs2, :], V3(P[pq][0:64, :], TT), EB[:, hs2, :], ALU.mult,
                     r=(("ps", pq), ("EB", h2)), w=(("qt", h2),))
                c.stt(kt[:, hs2, :], V3(P[pk2][0:64, :], TT), 0.125, EK[:, hs2, :], ALU.mult, ALU.mult,
                      r=(("ps", pk2), ("EK", h2)), w=(("kt", h2),))
            for jc in range(NC_):
                tsl = slice(jc * 64, (jc + 1) * 64)
                pkk = cm.bank(G)
                for k in range(NK):
                    c.mm(P[pkk][0:64, 0:GH * 64], hn[:, k, tsl], win[:, k, 512 + g * GH * 64:512 + (g + 1) * GH * 64], start=(k == 0),
                         stop=(k == NK - 1), r=("win", thn), w=(("ps", pkk),))
                c.tt(KhT[jc][:, hsl, :], V3(P[pkk][0:64, 0:GH * 64], 64),
                     eT[jc][:, hsl].unsqueeze(2).broadcast_to([64, GH, 64]),
                     ALU.mult, r=(("ps", pkk), ("eT", jc)), w=(("KhT", jc, g),))
                pv = cm.bank(G)
                for k in range(NK):
                    c.mm(P[pv][0:64, 0:GH * 128], hn[:, k, tsl], win[:, k, 1024 + g * GH * 128:1024 + (g + 1) * GH * 128], start=(k == 0),
                         stop=(k == NK - 1), r=("win", thn), w=(("ps", pv),))
                c.cp(Va[jc][:, hsl, :], V3(P[pv][0:64, 0:GH * 128], 128), r=(("ps", pv),), w=(("Va", jc, g),), eng="act")
            for jc in range(NC_):
                cols = slice(jc * 64, (jc + 1) * 64)
                jb = jc % 2
                pS = cm.bank(G)
                for hi in range(GH):
                    h = g * GH + hi
                    c.mm(P[pS][0:64, hi * 64:(hi + 1) * 64], kt[:, h, cols], qt[:, h, cols], start=True, stop=True,
                         r=(("kt", h // 2), ("qt", h // 2)), w=(("ps", pS),))
                c.tt(ST[jb][:, hsl, :], V3(P[pS][0:64, 0:GH * 64], 64),
                     mask[0:64, :].unsqueeze(1).broadcast_to([64, GH, 64]), ALU.mult, r=(("ps", pS),), w=(("ST", jb, g),))
                pN, pD = cm.bank(G), cm.bank(G)
                for hi in range(GH):
                    h = g * GH + hi
                    hs = slice(hi * 64, (hi + 1) * 64)
                    c.mm(P[pN][:, hs], CTb[:, h, :], qt[:, h, cols], start=True, stop=False,
                         r=(("CTb", g), ("qt", h // 2)), w=(("ps", pN),))
                    c.mm(P[pN][:, hs], Va[jc][:, h, :], ST[jb][:, h, :], start=False, stop=True,
                         r=(("Va", jc, g), ("ST", jb, g)), w=(("ps", pN),))
                    c.mm(P[pD][:, hs], nrep[:, h, :], qt[:, h, cols], start=True, stop=False,
                         r=(("nrep", g), ("qt", h // 2)), w=(("ps", pD),))
                    c.mm(P[pD][:, hs], onesb[:], ST[jb][:, h, :], start=False, stop=True,
                         r=("onesb", ("ST", jb, g)), w=(("ps", pD),))
                dm = dmax[:, g * GH * 64:(g + 1) * GH * 64]
                c.ts(dm, P[pD][:, 0:GH * 64], -1.0, ALU.mult, r=(("ps", pD),), w=(("dmax", g),), s2=1.0, op1=ALU.max)
                c.tt(dm, P[pD][:, 0:GH * 64], dm, ALU.max, r=(("ps", pD), ("dmax", g)), w=(("dmax", g),))
                c.recip(dm, dm, r=(("dmax", g),), w=(("dmax", g),))
                c.tt(hT[b][:, hsl, cols], V3(P[pN][:, 0:GH * 64], 64), V3(dm, 64), ALU.mult,
                     r=(("ps", pN), ("dmax", g)), w=(("hT", b, g),))
                pU = cm.bank(G)
                pUn = cm.bank(G)
                for hi in range(GH):
                    h = g * GH + hi
                    c.mm(P[pU][0:64, hi * 128:(hi + 1) * 128], KhT[jc][:, h, :], Va[jc][:, h, :],
                         start=True, stop=True, r=(("KhT", jc, g), ("Va", jc, g)), w=(("ps", pU),))
                    c.mm(P[pUn][0:64, hi:hi + 1], KhT[jc][:, h, :], onesb[:, 0:1], start=True, stop=True,
                         r=(("KhT", jc, g), "onesb"), w=(("ps", pUn),))
                ebl = EB[:, hsl, jc * 64 + 63:jc * 64 + 64]
                ebl_toks = tuple(("EB", q) for q in range(g * GH // 2, (g + 1) * GH // 2))
                c.tt(CT[:, hsl, :], CT[:, hsl, :], ebl.broadcast_to([64, GH, 128]), ALU.mult, r=(("CT", g),) + ebl_toks, w=(("CT", g),))
                c.tt(CT[:, hsl, :], CT[:, hsl, :], V3(P[pU][0:64, 0:GH * 128], 128), ALU.add, r=(("CT", g), ("ps", pU)), w=(("CT", g),))
                c.cp(CTb[:, hsl, :], CT[:, hsl, :], r=(("CT", g),), w=(("CTb", g),), eng="act")
                c.tt(nst[:, hsl], nst[:, hsl], ebl.rearrange("p h o -> p (h o)"), ALU.mult, r=(("nst", g),) + ebl_toks, w=(("nst", g),))
                c.tt(nst[:, hsl], nst[:, hsl], P[pUn][0:64, 0:GH], ALU.add, r=(("nst", g), ("ps", pUn)), w=(("nst", g),))
                c.cp(nrep[:, hsl, :], nst[:, hsl].unsqueeze(2).broadcast_to([64, GH, 128]), r=(("nst", g),), w=(("nrep", g),), eng="pool")
            H = hT[b]
            c.act(cm.sq[:, hsl, :], H[:, hsl, :], AF.Square, r=(("hT", b, g),), w=(("sqg", g),))
            for hi in range(GH):
                h = g * GH + hi
                hb2 = g * 2 + hi % 2
                pb = cm.bank(G)
                c.mm(P[pb][:, 0:TT], c.ones, cm.sq[:, h, :], start=True, stop=True, r=(("sqg", g),), w=(("ps", pb),))
                c.act(rsh[hb2][:], P[pb][:, 0:TT], AF.Ln, r=(("ps", pb),), w=(("rsh", hb2),), bias=c.eps_ap, scale=1.0 / 128.0)
                c.act(rsh[hb2][:], rsh[hb2][:], AF.Exp, r=(("rsh", hb2),), w=(("rsh", hb2),), scale=-0.5)
                po = cm.bank(G)
                for k in range(NK):
                    c.mm(P[po][:, 0:TT], win[:, k, 2048 + h * 128:2048 + (h + 1) * 128], hn[:, k, :], start=(k == 0), stop=(k == NK - 1),
                         r=("win", thn), w=(("ps", po),))
                c.act(sg[hb2][:], P[po][:, 0:TT], AF.Sigmoid, r=(("ps", po),), w=(("sg", hb2),))
                c.stt(t1[hb2][:], H[:, h, :], nw[:, h:h + 1], rsh[hb2][:], ALU.mult, ALU.mult, r=(("hT", b, g), ("rsh", hb2)), w=(("t1", hb2),))
                c.tt(mix[:, h, :], t1[hb2][:], sg[hb2][:], ALU.mult, r=(("t1", hb2), ("sg", hb2)), w=(("mix", h),), eng="pool")

        def tail(it):
            for o in range(NK):
                pb = cm.bank(CB_) if o % 2 == 0 else cm.bank(GBANK[0])
                for h in range(8):
                    c.mm(P[pb][:, 0:TT], wout[:, h, o * 128:(o + 1) * 128], mix[:, h, :], start=(h == 0), stop=(h == 7),
                         r=("wout", ("mix", h)), w=(("ps", pb),))
                c.cp(cm.yy[:, o, :], P[pb][:, 0:TT], r=(("ps", pb),), w=("yy",), eng="act")
            cm.post_residual(it, g_post)
            cm.store_x(it, x_out)

        S = c.s
        for it in range(NT):
            head(it)
            thr = [S.capture(group, it, g) for g in range(NG)]
            if os.environ.get("MK_NOIL", "0") == "1":
                for t_ in thr:
                    S.merge(t_, [])
            else:
                S.merge_greedy(thr[0], thr[1])
            cm.prefetch(it + 1, g_pre, x_in, NT)
            tail(it)
        c.s.barrier()


def phase_mamba(c, layer, j, x_in, x_out, ya_d, w_in_d, w_outa_d, w_outb_d, cst):
    nc = c.nc
    TT = 256
    NT = T // TT
    NC_ = TT // 64
    with ExitStack() as es:
        cm = Common(c, es, TT)
        sb = cm.sb
        P = c.psum
        win = sb("mbwin", [128, NK, 1544], BF16)
        woa = sb("mbwoa", [128, 4, D], BF16)
        wob = sb("mbwob", [64, 8, D], BF16)
        load_weight_bf16(c, win, w_in_d, D, 1544, "win", NK)
        load_weight_bf16(c, woa, w_outa_d, 512, D, "woa", 4)
        for h in range(8):
            c.dma(wob[:, h, :], w_outb_d[h], r=(), w=(), wacc=("wob",), q="pool")
        g_pre = cst["norm_mix_pre"][layer]
        g_post = cst["norm_mix_post"][layer]
        cwx = [cst["mb_cwx%d" % i][j] for i in range(4)]
        cwb = [cst["mb_cwb%d" % i][j] for i in range(4)]
        cbx, cbb = cst["mb_cbx"][j], cst["mb_cbb"][j]
        dtb, alog = cst["mb_dt_bias"][j], cst["mb_A_log"][j]
        Dbc, nwm = cst["mb_D"][j], cst["mb_norm_w"][j]
        sel128, selh, ident = cst["sel128"][0], cst["selh"][0], cst["ident"][0]
        negm, reset = cst["neg_mask"][0], cst["reset"][0]
        gsm = lambda nm: sb(nm, [8, TT], F32)
        one1 = sb("one1", [8, 1], F32)
        eps5 = sb("eps5", [64, 1], F32)
        negA = sb("negA", [8, 1], F32)
        ones64 = sb("ones64", [64, 64], BF16)
        XR = sb("XR", [64, 8, TT + 3], F32)
        BR = sb("BR", [128, 4, TT + 3], F32)
        acc = sb("acc", [64, 8, TT], F32)
        accb = sb("accb", [128, 4, TT], F32)
        XS = sb("XS", [64, 8, TT], F32)
        BCf = sb("BCf", [128, 4, TT], F32)
        BCb = sb("BCb", [128, 4, TT], BF16)
        SZ = sb("SZ", [64, 8, TT], F32)
        e_dt, dt_, dA, acum, ea, warg, wend = [gsm(n) for n in ("edt", "dt", "dA", "acum", "ea", "warg", "wend")]
        Ct = sb("Ct", [128, 8, TT], BF16)
        eaL = sb("eaL", [128, 8, NC_], F32)
        ACb = sb("ACb", [64, 8, TT], F32)
        sc = [sb("sc", [64, 24], F32) for _ in range(2)]
        xdtT = [sb("xdtT", [64, 8, 64], BF16) for _ in range(2)]
        xhT = [sb("xhT", [64, 8, 64], BF16) for _ in range(2)]
        BT = [sb("BT", [64, 2, 128], BF16) for _ in range(2)]
        CBs = [sb("CBs", [64, 2, 64], F32) for _ in range(2)]
        arg = [sb("arg", [64, 8, 64], F32) for _ in range(2)]
        G = [sb("G", [64, 8, 64], BF16) for _ in range(2)]
        STs = sb("STs", [128, 8, 64], F32)
        STb = sb("STb", [128, 8, 64], BF16)
        Y = sb("Y", [64, 8, TT], F32)
        rsg = [sb("rsg", [64, TT], F32) for _ in range(2)]
        yb = sb("yb", [64, 8, TT], BF16)
        yab = sb("yab", [128, 4, TT], BF16)
        c.memset(one1[:], 1.0, w=("one1",))
        c.memset(eps5[:], 1e-5, w=("eps5",))
        c.memset(ones64[:], 1.0, w=("ones64",))
        c.memset(XR[:], 0.0, w=("XR",))
        c.memset(BR[:], 0.0, w=("BR",))
        c.memset(STs[:], 0.0, w=("STs",))
        c.memset(STb[:], 0.0, w=("STb",))
        c.act(negA[:], alog[0:8, :], AF.Exp, r=(), w=("negA",))
        for it in range(NT):
            t0 = it * TT
            c.dma(yab[:], ya_d[0][:, :, t0:t0 + TT].rearrange("k p t -> p k t"),
                  r=tuple(("dram", ya_d[1], q) for q in range(t0 // 64, (t0 + TT) // 64)), w=("yab",))
            hn, thn = cm.get_hn(it, g_pre, x_in)
            for h2 in range(4):
                pz, px = cm.bank(), cm.bank()
                for i in range(2):
                    h = 2 * h2 + i
                    cs = slice(i * TT, (i + 1) * TT)
                    for k in range(NK):
                        c.mm(P[pz][0:64, cs], win[:, k, h * 64:(h + 1) * 64], hn[:, k, :], start=(k == 0), stop=(k == NK - 1),
                             r=("win", thn), w=(("ps", pz),))
                    for k in range(NK):
                        c.mm(P[px][0:64, cs], win[:, k, 512 + h * 64:512 + (h + 1) * 64], hn[:, k, :], start=(k == 0),
                             stop=(k == NK - 1), r=("win", thn), w=(("ps", px),))
                hs2 = slice(2 * h2, 2 * h2 + 2)
                c.act(SZ[:, hs2, :], P[pz][0:64, :].rearrange("p (h t) -> p h t", t=TT), AF.Silu, r=(("ps", pz),), w=("SZ",))
                c.cp(XR[:, hs2, 3:3 + TT], P[px][0:64, :].rearrange("p (h t) -> p h t", t=TT), r=(("ps", px),), w=("XR",), eng="act")
            for q in range(4):
                pb = cm.bank()
                for k in range(NK):
                    c.mm(P[pb][:, 0:TT], win[:, k, 1024 + q * 128:1024 + (q + 1) * 128], hn[:, k, :], start=(k == 0), stop=(k == NK - 1),
                         r=("win", thn), w=(("ps", pb),))
                c.cp(BR[:, q, 3:3 + TT], P[pb][:, 0:TT], r=(("ps", pb),), w=("BR",), eng="act")
            for h in range(8):
                c.ts(acc[:, h, :], XR[:, h, 3:3 + TT], cwx[3][0:64, h:h + 1], ALU.mult, r=("XR",), w=(("acc", h),),
                     s2=cbx[0:64, h:h + 1], op1=ALU.add, eng="pool")
                for i in range(3):
                    c.stt(acc[:, h, :], XR[:, h, i:i + TT], cwx[i][0:64, h:h + 1], acc[:, h, :], ALU.mult, ALU.add,
                          r=("XR", ("acc", h)), w=(("acc", h),))
            for q in range(4):
                c.ts(accb[:, q, :], BR[:, q, 3:3 + TT], cwb[3][:, q:q + 1], ALU.mult, r=("BR",), w=(("accb", q),),
                     s2=cbb[:, q:q + 1], op1=ALU.add, eng="pool")
                for i in range(3):
                    c.stt(accb[:, q, :], BR[:, q, i:i + TT], cwb[i][:, q:q + 1], accb[:, q, :], ALU.mult, ALU.add,
                          r=("BR", ("accb", q)), w=(("accb", q),))
            acc_t = tuple(("acc", h) for h in range(8))
            accb_t = tuple(("accb", q) for q in range(4))
            c.act(XS[:], acc[:], AF.Silu, r=acc_t, w=("XS",))
            c.act(BCf[:], accb[:], AF.Silu, r=accb_t, w=("BCf",))
            c.cp(BCb[:], BCf[:], r=("BCf",), w=("BCb",), eng="pool")
            c.cp(XR[:, :, 0:3], XR[:, :, TT:TT + 3], r=("XR",), w=("XR",), eng="pool")
            c.cp(BR[:, :, 0:3], BR[:, :, TT:TT + 3], r=("BR",), w=("BR",), eng="pool")
            if DBG_STOP <= 1:
                cm.store_x(it, x_out)
                continue
            if os.environ.get("MK_PFPOS", "0") == "1":
                cm.prefetch(it + 1, g_pre, x_in, NT)
            pd = cm.bank()
            for k in range(NK):
                c.mm(P[pd][0:8, 0:TT], win[:, k, 1536:1544], hn[:, k, :], start=(k == 0), stop=(k == NK - 1),
                     r=("win", thn), w=(("ps", pd),))
            c.act(e_dt[:], P[pd][0:8, 0:TT], AF.Exp, r=(("ps", pd),), w=("edt",), bias=dtb[0:8, :])
            c.act(dt_[:], e_dt[:], AF.Ln, r=("edt", "one1"), w=("dt",), bias=one1[:])
            c.ts(dA[:], dt_[:], negA[:, 0:1], ALU.mult, r=("dt", "negA"), w=("dA",), s2=-1.0, op1=ALU.mult)
            c.s.add("dve", lambda e: e.tensor_tensor_scan(acum[:], reset[0:8, 0:TT], dA[:], 0.0, ALU.mult, ALU.add),
                    r=("dA",), w=("acum",))
            c.act(ea[:], acum[:], AF.Exp, r=("acum",), w=("ea",))
            aL = acum[:].rearrange("p (c l) -> p c l", l=64)[:, :, 63:64].broadcast_to([8, NC_, 64])
            c.tt(warg[:].rearrange("p (c l) -> p c l", l=64), aL, acum[:].rearrange("p (c l) -> p c l", l=64), ALU.subtract,
                 r=("acum",), w=("warg",))
            c.act(wend[:], warg[:], AF.Exp, r=("warg",), w=("wend",))
            c.tt(wend[:], wend[:], dt_[:], ALU.mult, r=("wend", "dt"), w=("wend",))
            for h2 in range(4):
                pe_, pa_ = cm.bank(), cm.bank()
                for i in range(2):
                    h = 2 * h2 + i
                    cs = slice(i * TT, (i + 1) * TT)
                    c.mm(P[pe_][:, cs], sel128[0:8, h * 128:(h + 1) * 128], ea[:], start=True, stop=True, r=("ea",), w=(("ps", pe_),))
                    c.mm(P[pa_][0:64, cs], selh[0:8, h * 64:(h + 1) * 64], acum[:], start=True, stop=True, r=("acum",), w=(("ps", pa_),))
                hs2 = slice(2 * h2, 2 * h2 + 2)
                g = h2 // 2
                pe3 = P[pe_][:, :].rearrange("p (h t) -> p h t", t=TT)
                c.tt(Ct[:, hs2, :], pe3, BCf[:, 2 + g, :].unsqueeze(1).broadcast_to([128, 2, TT]), ALU.mult,
                     r=(("ps", pe_), "BCf"), w=("Ct",))
                c.cp(eaL[:, hs2, :], P[pe_][:, :].rearrange("p (h c l) -> p h c l", h=2, l=64)[:, :, :, 63], r=(("ps", pe_),),
                     w=("eaL",), eng="act")
                c.cp(ACb[:, hs2, :], P[pa_][0:64, :].rearrange("p (h t) -> p h t", t=TT), r=(("ps", pa_),), w=("ACb",), eng="act")
            if DBG_STOP <= 2:
                cm.store_x(it, x_out)
                continue
            if os.environ.get("MK_PFPOS", "0") == "0":
                cm.prefetch(it + 1, g_pre, x_in, NT)
            for jc in range(NC_):
                cols = slice(jc * 64, (jc + 1) * 64)
                jb = jc % 2
                pt = cm.bank()
                for qi, src in enumerate((acum, dt_, wend)):
                    c.tr(P[pt][0:64, qi * 8:(qi + 1) * 8], src[0:8, cols], ident[0:8, 0:8], r=("acum", "dt", "wend"), w=(("ps", pt),))
                c.cp(sc[jb][:], P[pt][0:64, 0:24], r=(("ps", pt),), w=(("sc", jb),), eng="act")
                px_ = cm.bank()
                for h in range(8):
                    c.tr(P[px_][0:64, h * 64:(h + 1) * 64], XS[:, h, cols], ident[0:64, 0:64], r=("XS",), w=(("ps", px_),))
                px3 = P[px_][0:64, :].rearrange("p (h d) -> p h d", d=64)
                c.tt(xdtT[jb][:], px3, sc[jb][:, 8:16].unsqueeze(2).broadcast_to([64, 8, 64]), ALU.mult,
                     r=(("ps", px_), ("sc", jb)), w=(("xdtT", jb),))
                c.tt(xhT[jb][:], px3, sc[jb][:, 16:24].unsqueeze(2).broadcast_to([64, 8, 64]), ALU.mult,
                     r=(("ps", px_), ("sc", jb)), w=(("xhT", jb),))
                pbt = cm.bank()
                for g in range(2):
                    c.tr(P[pbt][0:64, g * 128:(g + 1) * 128], BCf[:, g, cols], ident[:, :], r=("BCf",), w=(("ps", pbt),))
                c.cp(BT[jb][:], P[pbt][0:64, 0:256].rearrange("p (g n) -> p g n", n=128), r=(("ps", pbt),), w=(("BT", jb),), eng="act")
                pcb = cm.bank()
                for g in range(2):
                    c.mm(P[pcb][0:64, g * 64:(g + 1) * 64], BCb[:, g, cols], BCb[:, 2 + g, cols], start=True, stop=True,
                         r=("BCb",), w=(("ps", pcb),))
                c.cp(CBs[jb][:], P[pcb][0:64, 0:128].rearrange("p (g l) -> p g l", l=64), r=(("ps", pcb),), w=(("CBs", jb),), eng="act")
                c.tt(arg[jb][:], ACb[:, :, cols], sc[jb][:, 0:8].unsqueeze(2).broadcast_to([64, 8, 64]), ALU.subtract,
                     r=("ACb", ("sc", jb)), w=(("arg", jb),))
                c.tt(arg[jb][:], arg[jb][:], negm[0:64, :].unsqueeze(1).broadcast_to([64, 8, 64]), ALU.add,
                     r=(("arg", jb),), w=(("arg", jb),), eng="pool")
                c.act(arg[jb][:], arg[jb][:], AF.Exp, r=(("arg", jb),), w=(("arg", jb),))
                for g in range(2):
                    c.tt(G[jb][:, 4 * g:4 * g + 4, :], arg[jb][:, 4 * g:4 * g + 4, :],
                         CBs[jb][:, g, :].unsqueeze(1).broadcast_to([64, 4, 64]), ALU.mult,
                         r=(("arg", jb), ("CBs", jb)), w=(("G", jb),))
                py = cm.bank()
                for h in range(8):
                    hs = slice(h * 64, (h + 1) * 64)
                    c.mm(P[py][0:64, hs], STb[:, h, :], Ct[:, h, cols], start=True, stop=False, r=("STb", "Ct"), w=(("ps", py),))
                    c.mm(P[py][0:64, hs], xdtT[jb][:, h, :], G[jb][:, h, :], start=False, stop=True,
                         r=(("xdtT", jb), ("G", jb)), w=(("ps", py),))
                c.cp(Y[:, :, cols], P[py][0:64, :].rearrange("p (h d) -> p h d", d=64), r=(("ps", py),), w=("Y",), eng="act")
                pst = cm.bank()
                for h in range(8):
                    c.mm(P[pst][:, h * 64:(h + 1) * 64], BT[jb][:, h // 4, :], xhT[jb][:, h, :], start=True, stop=True,
                         r=(("BT", jb), ("xhT", jb)), w=(("ps", pst),))
                c.tt(STs[:], STs[:], eaL[:, :, jc:jc + 1].broadcast_to([128, 8, 64]), ALU.mult, r=("STs", "eaL"), w=("STs",))
                c.tt(STs[:], STs[:], P[pst][:, :].rearrange("p (h d) -> p h d", d=64), ALU.add, r=("STs", ("ps", pst)), w=("STs",))
                c.cp(STb[:], STs[:], r=("STs",), w=("STb",), eng="act")
            if DBG_STOP <= 3:
                cm.store_x(it, x_out)
                continue
            if os.environ.get("MK_PFPOS", "0") == "2":
                cm.prefetch(it + 1, g_pre, x_in, NT)
            c.tt(acc[:], XS[:], Dbc[0:64, :].unsqueeze(2).broadcast_to([64, 8, TT]), ALU.mult, r=("XS",) + acc_t, w=acc_t, eng="pool")
            c.tt(Y[:], Y[:], acc[:], ALU.add, r=("Y",) + acc_t, w=("Y",))
            c.tt(Y[:], Y[:], SZ[:], ALU.mult, r=("Y", "SZ"), w=("Y",))
            if DBG_STOP <= 3.3:
                cm.store_x(it, x_out)
                continue
            sq = cm.sq[0:64, :, :]
            c.act(sq, Y[:], AF.Square, r=("Y",), w=("sq",))
            for g in range(2):
                pb = cm.bank()
                for i in range(4):
                    c.mm(P[pb][0:64, 0:TT], ones64[:], sq[:, 4 * g + i, :], start=(i == 0), stop=(i == 3), r=("sq", "ones64"),
                         w=(("ps", pb),))
                c.act(rsg[g][:], P[pb][0:64, 0:TT], AF.Ln, r=(("ps", pb), "eps5"), w=(("rsg", g),), bias=eps5[:], scale=1.0 / 256.0)
                c.act(rsg[g][:], rsg[g][:], AF.Exp, r=(("rsg", g),), w=(("rsg", g),), scale=-0.5)
                for i in range(4):
                    h = 4 * g + i
                    c.stt(yb[:, h, :], Y[:, h, :], nwm[0:64, h:h + 1], rsg[g][:], ALU.mult, ALU.mult, r=("Y", ("rsg", g)), w=(("yb", h),))
            if DBG_STOP <= 3.6:
                cm.store_x(it, x_out)
                continue
            for o in range(NK):
                pb = cm.bank()
                VAR = os.environ.get("MK_VAR", "")
                for cc in range(4):
                    if VAR in ("B", "C"):
                        break
                    c.mm(P[pb][:, 0:TT], woa[:, cc, o * 128:(o + 1) * 128], yab[:, cc, :], start=(cc == 0), stop=(VAR == "A" and cc == 3),
                         r=("woa", "yab"), w=(("ps", pb),))
                for h in range(8):
                    if VAR in ("A", "C"):
                        break
                    c.mm(P[pb][:, 0:TT], wob[:, h, o * 128:(o + 1) * 128], yb[:, h, :], start=(VAR == "B" and h == 0), stop=(h == 7),
                         r=("wob", ("yb", h)), w=(("ps", pb),))
                if VAR == "C":
                    v3 = os.environ.get("MK_VAR3", "act")
                    if v3 != "none":
                        c.cp(cm.yy[:, o, :], cm.xs[it % 2][:, o, :], r=(("x", it % 2),), w=("yy",), eng=v3)
                else:
                    c.cp(cm.yy[:, o, :], P[pb][:, 0:TT], r=(("ps", pb),), w=("yy",), eng="dve")
            if VAR != "C" or os.environ.get("MK_VAR2", "") != "D":
                cm.post_residual(it, g_post)
            cm.store_x(it, x_out)
        c.s.barrier()


def phase_rwkv(c, layer, j, x_in, ya_out, w_in_d, w2_d, a2_d, g2_d, cst):
    nc = c.nc
    TT = 128
    NT = T // TT
    NC_ = TT // 64
    NG = 2
    GH = 8 // NG
    GMAIN = [(0, 2), (2, 2)]
    GPRE = [(4, 1), (5, 1)]
    GSH = (6, 1)
    with ExitStack() as es:
        cm = Common(c, es, TT, lite=True)
        sb = cm.sb
        P = c.psum
        win = sb("rwwin", [128, NK, 1792], BF16)
        w2b = sb("w2b", [64, 512], BF16)
        a2b = sb("a2b", [64, 512], BF16)
        g2b = sb("g2b", [128, 512], BF16)
        load_weight_bf16(c, win, w_in_d, D, 1792, "win", NK)
        c.dma(w2b[:], w2_d, r=(), w=("w2b",), q="pool")
        c.dma(a2b[:], a2_d, r=(), w=("a2b",), q="pool")
        c.dma(g2b[:], g2_d, r=(), w=("g2b",), q="pool")
        g_pre = cst["norm_mix_pre"][layer]
        C8 = lambda nm: cst[nm][j][0:64, :]
        mu = [C8("rw_mu_r"), C8("rw_mu_k"), C8("rw_mu_v")]
        mul = cst["rw_mu_l"][j]
        w0, a0, kkc, kac, rkc, lnw, lnb = [C8(n) for n in ("rw_w0", "rw_a0", "rw_k_k", "rw_k_a", "rw_r_k", "rw_ln_w", "rw_ln_b")]
        ident = cst["ident"][0]
        m_strict, m_incl, m_low = cst["mask_strict"][0], cst["mask_incl"][0], cst["mask_low"][0]
        reset = cst["reset"][0]
        f32t = lambda nm: sb(nm, [64, 8, TT], F32)
        b16t = lambda nm: sb(nm, [64, 8, TT], BF16)
        ones64 = sb("ones64", [64, 64], BF16)
        lneps = sb("lneps", [64, 1], F32)
        RAW = [sb("RAW", [64, 8, TT + 1], F32) for _ in range(3)]
        RAWL = sb("RAWL", [128, 3, TT + 1], F32)
        X0, X1 = f32t("X0"), f32t("X1")
        XL = sb("XL", [128, 3, TT], F32)
        twd = sb("twd", [64, TT], BF16)
        adb = sb("adb", [64, TT], BF16)
        sgd = sb("sgd", [128, TT], BF16)
        LW, CL, E_, tE, KK, Kmod, Bsc = [f32t(n) for n in ("LW", "CL", "E", "tE", "KK", "Kmod", "Bsc")]
        sqh = b16t("sqh")
        Xv2 = [f32t("Xv") for _ in range(2)]
        Bh2 = [f32t("Bh") for _ in range(2)]
        Kh2 = [f32t("Kh") for _ in range(2)]
        Epos2 = [f32t("Epos") for _ in range(2)]
        BON2 = [f32t("BON") for _ in range(2)]
        GG2 = [f32t("GG") for _ in range(2)]
        At2 = [b16t("At") for _ in range(2)]
        Rt2 = [b16t("Rt") for _ in range(2)]
        Bt2 = [b16t("Bt") for _ in range(2)]
        Kt2 = [b16t("Kt") for _ in range(2)]
        YY = f32t("YY")
        sqh2 = b16t("sqh2")
        NR2 = f32t("NR2")
        YA = sqh2
        m64 = lambda nm, dt=BF16: sb(nm, [64, 8, 64], dt)
        Aab, AabT, Aak, Arb, Ark = [m64(n) for n in ("Aab", "AabT", "Aak", "Arb", "Ark")]
        Tm = [m64("Tm") for _ in range(2)]
        Ak = [m64("Ak") for _ in range(2)]
        AkT = [m64("AkT") for _ in range(2)]
        VT, BhT, KhT, XTs, UTs = [m64(n) for n in ("VT", "BhT", "KhT", "XTs", "UTs")]
        S0 = m64("S0", F32)
        S0b = m64("S0b")
        allg = lambda nm: tuple((nm, g) for g in range(NG))
        c.memset(ones64[:], 1.0, w=("ones64",))
        c.memset(lneps[:], 64e-5, w=("lneps",))
        c.memset(S0[:], 0.0, w=allg("S0"))
        c.memset(S0b[:], 0.0, w=allg("S0b"))
        for q in range(3):
            c.memset(RAW[q][:], 0.0, w=tuple(("RAW", q, g) for g in range(NG)))
        c.memset(RAWL[:], 0.0, w=("RAWL",))
        ya_v = ya_out[0].rearrange("c (two p) t -> p (c two) t", two=2)
        V3 = lambda ap, d=64: ap.rearrange("p (h d) -> p h d", d=d)

        def shared(it):
            hn, thn = cm.get_hn(it, g_pre, x_in)
            pb = cm.bank(GSH)
            for (li, c0, mrows) in ((0, 1536, 64), (1, 1600, 64), (2, 1664, 128)):
                for k in range(NK):
                    c.mm(P[pb][0:mrows, li * TT:(li + 1) * TT], win[:, k, c0:c0 + mrows], hn[:, k, :], start=(k == 0), stop=(k == NK - 1),
                         r=("win", thn), w=(("ps", pb),))
            c.cp(RAWL[0:64, 0:2, 1:1 + TT], V3(P[pb][0:64, 0:2 * TT], TT), r=(("ps", pb),), w=("RAWL",), eng="dve")
            c.cp(RAWL[:, 2, 1:1 + TT], P[pb][:, 2 * TT:3 * TT], r=(("ps", pb),), w=("RAWL",), eng="dve")
            c.tt(XL[:], RAWL[:, :, 0:TT], RAWL[:, :, 1:1 + TT], ALU.subtract, r=("RAWL",), w=("XL",), eng="pool")
            c.tt(XL[:], XL[:], mul[:, :].unsqueeze(2).broadcast_to([128, 3, TT]), ALU.mult, r=("XL",), w=("XL",))
            c.tt(XL[:], XL[:], RAWL[:, :, 1:1 + TT], ALU.add, r=("XL", "RAWL"), w=("XL",))
            c.cp(RAWL[:, :, 0:1], RAWL[:, :, TT:TT + 1], r=("RAWL",), w=("RAWL",), eng="pool")
            c.act(twd[:], XL[0:64, 0, :], AF.Tanh, r=("XL",), w=("twd",))
            c.act(adb[:], XL[0:64, 1, :], AF.Copy, r=("XL",), w=("adb",), scale=1.0)
            c.act(sgd[:], XL[:, 2, :], AF.Sigmoid, r=("XL",), w=("sgd",))

        def pre(it, g):
            b = it % 2
            GB = GPRE[g]
            hsl = slice(g * GH, (g + 1) * GH)
            HTg = GH * TT
            Xv, Bh, Kh, Epos, BON, GG = [t_[b][:, hsl, :] for t_ in (Xv2, Bh2, Kh2, Epos2, BON2, GG2)]
            At, Rt, Bt, Kt = [t_[b][:, hsl, :] for t_ in (At2, Rt2, Bt2, Kt2)]
            tXv, tBh, tKh, tEpos, tBON, tGG = [(n, b, g) for n in ("Xv", "Bh", "Kh", "Epos", "BON", "GG")]
            tAt, tRt, tBt, tKt = [(n, b, g) for n in ("At", "Rt", "Bt", "Kt")]
            hn, thn = cm.hn[b], ("hn", b)
            Xr, Xk = X0[:, hsl, :], X1[:, hsl, :]
            tXr, tXk = ("X0", g), ("X1", g)
            Xs, tXs = [Xr, Xk, Xv], [tXr, tXk, tXv]
            LWg, CLg, Eg, tEg, KKg, Kmodg, Bscg, sqhg = [t_[:, hsl, :] for t_ in (LW, CL, E_, tE, KK, Kmod, Bsc, sqh)]
            tLW, tCL, tEb, ttE, tKK, tKmod, tBsc, tsqh = [(n, g) for n in ("LW", "CL", "E", "tE", "KK", "Kmod", "Bsc", "sqh")]
            bcg = lambda ap: ap[:, hsl].unsqueeze(2).broadcast_to([64, GH, TT])
            fl = lambda ap: ap.rearrange("p h t -> p (h t)")
            for q in range(3):
                pb = cm.bank(GB)
                for i in range(GH):
                    h = g * GH + i
                    for k in range(NK):
                        c.mm(P[pb][0:64, i * TT:(i + 1) * TT], win[:, k, q * 512 + h * 64:q * 512 + (h + 1) * 64], hn[:, k, :],
                             start=(k == 0), stop=(k == NK - 1), r=("win", thn), w=(("ps", pb),))
                c.act(RAW[q][:, hsl, 1:1 + TT], V3(P[pb][0:64, 0:HTg], TT), AF.Copy, r=(("ps", pb),), w=(("RAW", q, g),), scale=1.0)
            for q in range(3):
                R_ = RAW[q]
                c.tt(tEg, R_[:, hsl, 0:TT], R_[:, hsl, 1:1 + TT], ALU.subtract, r=(("RAW", q, g),), w=(ttE,), eng="pool")
                c.tt(tEg, tEg, bcg(mu[q]), ALU.mult, r=(ttE,), w=(ttE,))
                c.tt(Xs[q], tEg, R_[:, hsl, 1:1 + TT], ALU.add, r=(ttE, ("RAW", q, g)), w=(tXs[q],), eng="pool")
                c.cp(R_[:, hsl, 0:1], R_[:, hsl, TT:TT + 1], r=(("RAW", q, g),), w=(("RAW", q, g),), eng="pool")
            AA = tEg
            pw = cm.bank(GB)
            for i in range(GH):
                h = g * GH + i
                c.mm(P[pw][0:64, i * TT:(i + 1) * TT], w2b[:, h * 64:(h + 1) * 64], twd[:], start=True, stop=True, r=("w2b", "twd"), w=(("ps", pw),))
            for i in range(GH):
                h = g * GH + i
                c.act(LW[:, h, :], P[pw][0:64, i * TT:(i + 1) * TT], AF.Sigmoid, r=(("ps", pw),), w=(tLW,), bias=w0[:, h:h + 1])
            pa = cm.bank(GB)
            for i in range(GH):
                h = g * GH + i
                c.mm(P[pa][0:64, i * TT:(i + 1) * TT], a2b[:, h * 64:(h + 1) * 64], adb[:], start=True, stop=True, r=("a2b", "adb"), w=(("ps", pa),))
            for i in range(GH):
                h = g * GH + i
                c.act(tE[:, h, :], P[pa][0:64, i * TT:(i + 1) * TT], AF.Sigmoid, r=(("ps", pa),), w=(ttE,), bias=a0[:, h:h + 1])
            pg = cm.bank(GB)
            for i in range(GH):
                h = g * GH + i
                c.mm(P[pg][0:64, i * TT:(i + 1) * TT], g2b[:, h * 64:(h + 1) * 64], sgd[:], start=True, stop=True, r=("g2b", "sgd"), w=(("ps", pg),))
            c.cp(GG, V3(P[pg][0:64, 0:HTg], TT), r=(("ps", pg),), w=(tGG,), eng="dve")
            c.act(LWg, LWg, AF.Copy, r=(tLW,), w=(tLW,), scale=-0.6065306597126334)
            c.s.add("dve", lambda e: e.tensor_tensor_scan(fl(CLg), reset[0:64, 0:HTg], fl(LWg), 0.0, ALU.mult, ALU.add),
                    r=(tLW,), w=(tCL,), cost=2 * HTg / 0.96 + 150)
            c.act(Epos, CLg, AF.Exp, r=(tCL,), w=(tEpos,))
            NR = Eg
            c.tt(KKg, Xk, bcg(kkc), ALU.mult, r=(tXk,), w=(tKK,))
            c.act(sqhg, KKg, AF.Square, r=(tKK,), w=(tsqh,))
            pb = cm.bank(GB)
            for i in range(GH):
                h = g * GH + i
                c.mm(P[pb][0:64, i * TT:(i + 1) * TT], ones64[:], sqh[:, h, :], start=True, stop=True, r=(tsqh, "ones64"), w=(("ps", pb),))
            c.act(NR, V3(P[pb][0:64, 0:HTg], TT), AF.Sqrt, r=(("ps", pb),), w=(tEb,))
            c.act(c.dummy2[g], c.dummy, AF.Exp, r=(), w=(("dummy", g),))
            c.ts(NR, NR, 1e-12, ALU.max, r=(tEb,), w=(tEb,))
            c.recip(NR, NR, r=(tEb,), w=(tEb,))
            c.tt(KKg, KKg, NR, ALU.mult, r=(tKK, tEb), w=(tKK,))
            c.stt(Kmodg, AA, 1.0, bcg(kac), ALU.subtract, ALU.mult, r=(ttE,), w=(tKmod,))
            c.stt(Kmodg, Kmodg, 1.0, Xk, ALU.add, ALU.mult, r=(tKmod, tXk), w=(tKmod,))
            c.tt(Bscg, KKg, AA, ALU.mult, r=(tKK, ttE), w=(tBsc,), eng="pool")
            c.tt(Rt, Xr, Epos, ALU.mult, r=(tXr, tEpos), w=(tRt,), eng="pool")
            c.act(Eg, CLg, AF.Exp, r=(tCL, tEb), w=(tEb,), scale=-1.0)
            c.tt(Bt, Bscg, Eg, ALU.mult, r=(tBsc, tEb), w=(tBt,))
            c.tt(Kt, Kmodg, Eg, ALU.mult, r=(tKmod, tEb), w=(tKt,), eng="pool")
            c.tt(Eg, CLg, LWg, ALU.subtract, r=(tCL, tLW, tEb), w=(tEb,))
            c.act(Eg, Eg, AF.Exp, r=(tEb,), w=(tEb,))
            c.stt(At, KKg, -1.0, Eg, ALU.mult, ALU.mult, r=(tKK, tEb), w=(tAt,))
            CL4 = CLg.rearrange("p h (c l) -> p h c l", l=64)
            c.tt(Eg.rearrange("p h (c l) -> p h c l", l=64), CL4[:, :, :, 63:64].broadcast_to([64, GH, NC_, 64]), CL4, ALU.subtract,
                 r=(tCL, tEb), w=(tEb,), eng="pool")
            c.act(Eg, Eg, AF.Exp, r=(tEb,), w=(tEb,))
            c.tt(Bh, Bscg, Eg, ALU.mult, r=(tBsc, tEb), w=(tBh,))
            c.tt(Kh, Kmodg, Eg, ALU.mult, r=(tKmod, tEb), w=(tKh,), eng="pool")
            c.tt(tEg, Xr, Kmodg, ALU.mult, r=(tXr, tKmod, ttE), w=(ttE,), eng="pool")
            c.tt(sqhg, tEg, bcg(rkc), ALU.mult, r=(ttE,), w=(tsqh,))
            pb = cm.bank(GB)
            for i in range(GH):
                h = g * GH + i
                c.mm(P[pb][0:64, i * TT:(i + 1) * TT], ones64[:], sqh[:, h, :], start=True, stop=True, r=(tsqh, "ones64"), w=(("ps", pb),))
            c.tt(BON, V3(P[pb][0:64, 0:HTg], TT), Xv, ALU.mult, r=(("ps", pb), tXv), w=(tBON,))

        def main(it, g):
            b = it % 2
            t0 = it * TT
            GA = GMAIN[g]
            hsl = slice(g * GH, (g + 1) * GH)
            HTg = GH * TT
            W = GH * 64
            Xv, Bh, Kh, Epos, BON, GG = [t_[b] for t_ in (Xv2, Bh2, Kh2, Epos2, BON2, GG2)]
            At, Rt, Bt, Kt = [t_[b] for t_ in (At2, Rt2, Bt2, Kt2)]
            tXv, tBh, tKh, tEpos, tBON, tGG = [(n, b, g) for n in ("Xv", "Bh", "Kh", "Epos", "BON", "GG")]
            tAt, tRt, tBt, tKt = [(n, b, g) for n in ("At", "Rt", "Bt", "Kt")]
            T_ = lambda nm: (nm, g)
            bcm = lambda ap: ap[:, hsl].unsqueeze(2).broadcast_to([64, GH, TT])
            mk = lambda m_: m_[0:64, 0:64].unsqueeze(1).broadcast_to([64, GH, 64])
            hs_of = lambda i: slice(i * 64, (i + 1) * 64)
            for jc in range(NC_):
                cols = slice(jc * 64, (jc + 1) * 64)
                for (src, stok, dst, tok) in ((Xv, tXv, VT, "VT"), (Bh, tBh, BhT, "BhT"), (Kh, tKh, KhT, "KhT")):
                    pb = cm.bank(GA)
                    for i in range(GH):
                        c.tr(P[pb][0:64, hs_of(i)], src[:, g * GH + i, cols], ident[0:64, 0:64], r=(stok,), w=(("ps", pb),))
                    c.act(dst[:, hsl, :], V3(P[pb][0:64, 0:W]), AF.Copy, r=(("ps", pb),), w=(T_(tok),), scale=1.0)
                for (lt, ltok, rt, rtok, dst, tok, msk) in ((Bt, tBt, At, tAt, Aab, "Aab", m_strict), (At, tAt, Bt, tBt, AabT, "AabT", m_low),
                                                            (Kt, tKt, At, tAt, Aak, "Aak", m_strict),
                                                            (Bt, tBt, Rt, tRt, Arb, "Arb", m_incl), (Kt, tKt, Rt, tRt, Ark, "Ark", m_incl)):
                    pb = cm.bank(GA)
                    for i in range(GH):
                        h = g * GH + i
                        c.mm(P[pb][0:64, hs_of(i)], lt[:, h, cols], rt[:, h, cols], start=True, stop=True, r=(ltok, rtok), w=(("ps", pb),))
                    c.tt(dst[:, hsl, :], V3(P[pb][0:64, 0:W]), mk(msk), ALU.mult, r=(("ps", pb),), w=(T_(tok),))
                c.tt(Tm[0][:, hsl, :], Aab[:, hsl, :], mk(ident), ALU.add, r=(T_("Aab"),), w=(("Tm", 0, g),), eng="pool")
                A_c, AT_c, tokA, tokAT = Aab, AabT, T_("Aab"), T_("AabT")
                tcur = 0
                for lvl, kpow in enumerate((1, 2, 4, 8, 16, 32)):
                    nb = lvl % 2
                    if kpow >= 2:
                        pz = cm.bank(GA)
                        for i in range(GH):
                            h = g * GH + i
                            c.mm(P[pz][0:64, hs_of(i)], AT_c[:, h, :], Tm[tcur][:, h, :], start=True, stop=True,
                                 r=(tokAT, ("Tm", tcur, g)), w=(("ps", pz),))
                        c.tt(Tm[1 - tcur][:, hsl, :], V3(P[pz][0:64, 0:W]), Tm[tcur][:, hsl, :], ALU.add,
                             r=(("ps", pz), ("Tm", tcur, g)), w=(("Tm", 1 - tcur, g),))
                        tcur = 1 - tcur
                    if kpow <= 16:
                        py = cm.bank(GA)
                        for i in range(GH):
                            h = g * GH + i
                            c.mm(P[py][0:64, hs_of(i)], A_c[:, h, :], AT_c[:, h, :], start=True, stop=True, r=(tokA, tokAT), w=(("ps", py),))
                        if kpow <= 8:
                            px_ = cm.bank(GA)
                            for i in range(GH):
                                h = g * GH + i
                                c.mm(P[px_][0:64, hs_of(i)], AT_c[:, h, :], A_c[:, h, :], start=True, stop=True, r=(tokA, tokAT), w=(("ps", px_),))
                            c.act(Ak[nb][:, hsl, :], V3(P[px_][0:64, 0:W]), AF.Copy, r=(("ps", px_),), w=(("Ak", nb, g),), scale=1.0)
                        c.cp(AkT[nb][:, hsl, :], V3(P[py][0:64, 0:W]), r=(("ps", py),), w=(("AkT", nb, g),), eng="dve")
                        A_c, AT_c, tokA, tokAT = Ak[nb], AkT[nb], ("Ak", nb, g), ("AkT", nb, g)
                Tf, tokT = Tm[tcur], ("Tm", tcur, g)
                pb = cm.bank(GA)
                for i in range(GH):
                    h = g * GH + i
                    c.mm(P[pb][0:64, hs_of(i)], At[:, h, cols], S0b[:, h, :], start=True, stop=False, r=(tAt, T_("S0b")), w=(("ps", pb),))
                    c.mm(P[pb][0:64, hs_of(i)], Aak[:, h, :], VT[:, h, :], start=False, stop=True, r=(T_("Aak"), T_("VT")), w=(("ps", pb),))
                c.act(XTs[:, hsl, :], V3(P[pb][0:64, 0:W]), AF.Copy, r=(("ps", pb),), w=(T_("XTs"),), scale=1.0)
                pb = cm.bank(GA)
                for i in range(GH):
                    h = g * GH + i
                    c.mm(P[pb][0:64, hs_of(i)], Tf[:, h, :], XTs[:, h, :], start=True, stop=True, r=(tokT, T_("XTs")), w=(("ps", pb),))
                c.cp(UTs[:, hsl, :], V3(P[pb][0:64, 0:W]), r=(("ps", pb),), w=(T_("UTs"),), eng="dve")
                pb = cm.bank(GA)
                for i in range(GH):
                    h = g * GH + i
                    c.mm(P[pb][0:64, hs_of(i)], S0b[:, h, :], Rt[:, h, cols], start=True, stop=False, r=(T_("S0b"), tRt), w=(("ps", pb),))
                    c.mm(P[pb][0:64, hs_of(i)], UTs[:, h, :], Arb[:, h, :], start=False, stop=False, r=(T_("UTs"), T_("Arb")), w=(("ps", pb),))
                    c.mm(P[pb][0:64, hs_of(i)], VT[:, h, :], Ark[:, h, :], start=False, stop=True, r=(T_("VT"), T_("Ark")), w=(("ps", pb),))
                c.act(YY[:, hsl, cols], V3(P[pb][0:64, 0:W]), AF.Copy, r=(("ps", pb),), w=(T_("YY"),), scale=1.0)
                pb = cm.bank(GA)
                for i in range(GH):
                    h = g * GH + i
                    c.mm(P[pb][0:64, hs_of(i)], BhT[:, h, :], UTs[:, h, :], start=True, stop=False, r=(T_("BhT"), T_("UTs")), w=(("ps", pb),))
                    c.mm(P[pb][0:64, hs_of(i)], KhT[:, h, :], VT[:, h, :], start=False, stop=True, r=(T_("KhT"), T_("VT")), w=(("ps", pb),))
                c.tt(S0[:, hsl, :], S0[:, hsl, :], Epos[:, hsl, jc * 64 + 63:jc * 64 + 64].broadcast_to([64, GH, 64]), ALU.mult,
                     r=(T_("S0"), tEpos), w=(T_("S0"),))
                c.tt(S0[:, hsl, :], S0[:, hsl, :], V3(P[pb][0:64, 0:W]), ALU.add, r=(T_("S0"), ("ps", pb)), w=(T_("S0"),))
                c.act(S0b[:, hsl, :], S0[:, hsl, :], AF.Copy, r=(T_("S0"),), w=(T_("S0b"),), scale=1.0)
            YYg, sq2g, NR2g = YY[:, hsl, :], sqh2[:, hsl, :], NR2[:, hsl, :]
            c.act(sq2g, YYg, AF.Copy, r=(T_("YY"),), w=(T_("sqh2"),), scale=1.0)
            pb = cm.bank(GA)
            for i in range(GH):
                h = g * GH + i
                c.mm(P[pb][0:64, i * TT:(i + 1) * TT], ones64[:], sqh2[:, h, :], start=True, stop=True, r=(T_("sqh2"), "ones64"), w=(("ps", pb),))
            c.stt(YYg, V3(P[pb][0:64, 0:HTg], TT), -1.0 / 64.0, YYg, ALU.mult, ALU.add, r=(("ps", pb), T_("YY")), w=(T_("YY"),))
            c.act(sq2g, YYg, AF.Square, r=(T_("YY"),), w=(T_("sqh2"),))
            pb = cm.bank(GA)
            for i in range(GH):
                h = g * GH + i
                c.mm(P[pb][0:64, i * TT:(i + 1) * TT], ones64[:], sqh2[:, h, :], start=True, stop=True, r=(T_("sqh2"), "ones64"), w=(("ps", pb),))
            c.act(NR2g, V3(P[pb][0:64, 0:HTg], TT), AF.Ln, r=(("ps", pb), "lneps"), w=(T_("NR2"),), bias=lneps[:], scale=1.0 / 64.0)
            c.act(NR2g, NR2g, AF.Exp, r=(T_("NR2"),), w=(T_("NR2"),), scale=-0.5)
            c.tt(YYg, YYg, NR2g, ALU.mult, r=(T_("YY"), T_("NR2")), w=(T_("YY"),))
            c.tt(YYg, YYg, bcm(lnw), ALU.mult, r=(T_("YY"),), w=(T_("YY"),), eng="pool")
            c.tt(YYg, YYg, bcm(lnb), ALU.add, r=(T_("YY"),), w=(T_("YY"),), eng="pool")
            c.tt(YYg, YYg, BON[:, hsl, :], ALU.add, r=(T_("YY"), tBON), w=(T_("YY"),))
            c.tt(sq2g, YYg, GG[:, hsl, :], ALU.mult, r=(T_("YY"), tGG), w=(T_("sqh2"),))
            c.dma(ya_v[:, hsl, t0:t0 + TT], sq2g, r=(T_("sqh2"),), w=(),
                  wacc=tuple(("dram", ya_out[1], q) for q in range(t0 // 64, (t0 + TT) // 64)))

        S = c.s
        shared(0)
        cm.prefetch(1, g_pre, x_in, NT)
        for g in range(NG):
            pre(0, g)
        noil = os.environ.get("MK_NOIL", "0") == "1"
        pr = float(os.environ.get("MK_PRIO", "-300"))
        for it in range(NT):
            if it + 1 < NT:
                shared(it + 1)
                cm.prefetch(it + 2, g_pre, x_in, NT)
            thr = [S.capture(main, it, g) for g in range(NG)]
            if it + 1 < NT:
                thr += [S.capture(pre, it + 1, g) for g in range(NG)]
            if noil:
                for t_ in thr:
                    S.merge(t_, [])
            else:
                S.merge_n(thr, prio=[0.0, 0.0, pr, pr][:len(thr)])
        c.s.barrier()


def pack_consts(inputs):
    cols = []
    index = {}

    def add(name, arr2d):
        off = sum(a.shape[1] for a in cols)
        cols.append(np.ascontiguousarray(arr2d, dtype=np.float32))
        index[name] = (off, arr2d.shape[1])

    for nm in ("norm_mix_pre", "norm_mix_post", "norm_xa_pre", "norm_xa_post", "norm_mem", "norm_ff_pre", "norm_ff_post"):
        a = inputs[nm]
        for l in range(a.shape[0]):
            add((nm, l), a[l].reshape(NK, 128).T)
    a = inputs["ml_norm_w"]
    for l in range(a.shape[0]):
        add(("ml_norm_w", l), a[l].reshape(8, 128).T)
    a = inputs["ml_b_gates"]
    for l in range(a.shape[0]):
        z = np.zeros((128, 1), np.float32)
        z[0:8, 0] = a[l][0:8]
        add(("ml_b_i", l), z)
        z = np.zeros((128, 1), np.float32)
        z[0:8, 0] = a[l][8:16]
        add(("ml_b_f", l), z)
    def hl(v512):
        z = np.zeros((128, 8), np.float32)
        z[0:64] = np.asarray(v512).reshape(8, 64).T
        return z
    for l in range(inputs["rw_mu"].shape[0]):
        mu_ = inputs["rw_mu"][l]
        add(("rw_mu_r", l), hl(mu_[0:512]))
        add(("rw_mu_k", l), hl(mu_[512:1024]))
        add(("rw_mu_v", l), hl(mu_[1024:1536]))
        z = np.zeros((128, 3), np.float32)
        z[0:64, 0] = mu_[1536:1600]
        z[0:64, 1] = mu_[1600:1664]
        z[:, 2] = mu_[1664:1792]
        add(("rw_mu_l", l), z)
        for nm in ("rw_w0", "rw_a0", "rw_k_k", "rw_k_a", "rw_ln_w", "rw_ln_b"):
            add((nm, l), hl(inputs[nm][l]))
        add(("rw_r_k", l), hl(inputs["rw_r_k"][l].reshape(512)))
    for l in range(inputs["mb_conv_w"].shape[0]):
        cw = inputs["mb_conv_w"][l]
        cb = inputs["mb_conv_b"][l]
        for i in range(4):
            z = np.zeros((128, 8), np.float32)
            z[0:64] = cw[i, 0:512].reshape(8, 64).T
            add(("mb_cwx%d" % i, l), z)
            add(("mb_cwb%d" % i, l), cw[i, 512:1024].reshape(4, 128).T)
        z = np.zeros((128, 8), np.float32)
        z[0:64] = cb[0:512].reshape(8, 64).T
        add(("mb_cbx", l), z)
        add(("mb_cbb", l), cb[512:1024].reshape(4, 128).T)
        for nm in ("mb_dt_bias", "mb_A_log"):
            z = np.zeros((128, 1), np.float32)
            z[0:8, 0] = inputs[nm][l]
            add((nm, l), z)
        add(("mb_D", l), np.broadcast_to(inputs["mb_D"][l][None, :], (128, 8)))
        z = np.zeros((128, 8), np.float32)
        z[0:64] = inputs["mb_norm_w"][l].reshape(8, 64).T
        add(("mb_norm_w", l), z)
    p = np.arange(128)[:, None]
    t = np.arange(64)[None, :]
    add(("neg_mask", 0), np.where((p % 64) <= t, 0.0, -30000.0).astype(np.float32))
    sel128 = np.zeros((128, 8 * 128), np.float32)
    for h in range(8):
        sel128[h, h * 128:(h + 1) * 128] = 1.0
    add(("sel128", 0), sel128)
    add(("mask_incl", 0), ((p % 64) <= t).astype(np.float32))
    add(("mask_strict", 0), ((p % 64) < t).astype(np.float32))
    add(("mask_low", 0), ((p % 64) > t).astype(np.float32))
    add(("ident", 0), np.eye(128, dtype=np.float32))
    selm = np.zeros((128, 4 * 128), np.float32)
    for h in range(8):
        hp = h // 2
        selm[h, hp * 128 + (h % 2) * 64: hp * 128 + (h % 2) * 64 + 64] = 1.0
    selh = np.zeros((128, 8 * 64), np.float32)
    for h in range(8):
        selh[h, h * 64:(h + 1) * 64] = 1.0
    add(("selh", 0), selh)
    rst = np.ones((128, 1024), np.float32)
    rst[:, ::64] = 0.0
    add(("reset", 0), rst)
    return np.concatenate(cols, axis=1), index


def build(phases, consts_np, cindex):
    nc = bass.Bass("TRN2", target_bir_lowering=False)
    c = Ctx(nc)
    dr = {}

    def din(name, shape):
        dr[name] = nc.dram_tensor(name, list(shape), F32, kind="ExternalInput")
        return dr[name]

    ncst = consts_np.shape[1]
    x0 = din("xT", [NK, 128, T])
    cst_d = din("consts", [128, ncst])
    out = nc.dram_tensor("outT", [NK, 128, T], F32, kind="ExternalOutput")
    with ExitStack() as es:
        cst_sb = es.enter_context(nc.sbuf_tensor("cst", [128, ncst], F32))
        ones = es.enter_context(nc.sbuf_tensor("ones", [128, 128], BF16))
        epst = es.enter_context(nc.sbuf_tensor("epst", [128, 1], F32))
        c.psum = [es.enter_context(nc.psum_tensor("ps%d" % i, [128, 512], F32)) for i in range(8)]
        c.ones = ones[:]
        c.eps_ap = epst[:]
        dmy = es.enter_context(nc.sbuf_tensor("dmy", [128, 1], F32))
        c.dummy = dmy[:]
        c.memset(dmy[:], 0.0, w=("dummy",))
        dmy2 = es.enter_context(nc.sbuf_tensor("dmy2", [128, 4], F32))
        c.dummy2 = [dmy2[:, i:i + 1] for i in range(4)]
        c.memset(dmy2[:], 0.0, w=tuple(("dummy", i) for i in range(4)))
        c.dma(cst_sb[:], cst_d.ap(), r=(), w=("cst",))
        c.memset(ones[:], 1.0, w=("ones",))
        c.memset(epst[:], EPS, w=("eps",))
        c.s.barrier()
        cst = {}
        for (nm, l), (off, n) in cindex.items():
            cst.setdefault(nm, {})[l] = cst_sb[:, off:off + n]
        cur = (x0.ap(), "xT")
        for pi_, ph in enumerate(phases):
            last = pi_ == len(phases) - 1
            if last:
                dst = (out.ap(), "outT")
            elif ph[0] == "rwkv":
                dst = None
            else:
                dst = (nc.dram_tensor("scr%d" % pi_, [NK, 128, T], F32, kind="Internal").ap(), "scr%d" % pi_)
            kind = ph[0]
            if kind == "mlp":
                l = ph[1]
                wu = din("ff_up%d" % l, [D, DFF])
                wdn = din("ff_down%d" % l, [DFF, D])
                phase_mlp(c, l, cur, dst, wu.ap(), wdn.ap(), cst)
            elif kind == "rwkv":
                l = ph[1]
                wi = din("ev_w_in_rw", [D, 1792])
                w2_ = din("rw_w2", [64, 512])
                a2_ = din("rw_a2", [64, 512])
                g2_ = din("rw_g2", [128, 512])
                if len(ph) > 2 and ph[2] == "out":
                    ya_t = (nc.dram_tensor("ya_out", [4, 128, T], BF16, kind="ExternalOutput").ap(), "ya_out")
                else:
                    ya_t = (nc.dram_tensor("ya_scr%d" % l, [4, 128, T], BF16, kind="Internal").ap(), "ya_scr%d" % l)
                    dr["ya_scr"] = ya_t
                phase_rwkv(c, l, l // 2, cur, ya_t, wi.ap(), w2_.ap(), a2_.ap(), g2_.ap(), cst)
                dst = cur
            elif kind == "mamba":
                l = ph[1]
                ya_t = dr.get("ya_scr")
                if ya_t is None:
                    ya_t = (nc.dram_tensor("ya_in", [4, 128, T], BF16, kind="ExternalInput").ap(), "ya_in")
                wi = din("ev_w_in_mb", [D, 1544])
                woa_ = din("ev_w_out_a", [512, D])
                wob_ = din("ev_w_out_b", [8, 64, D])
                phase_mamba(c, l, l // 2, cur, dst, ya_t, wi.ap(), woa_.ap(), wob_.ap(), cst)
            elif kind == "mlstm":
                l = ph[1]
                wi = din("ml_w_in", [D, 3088])
                wo_ = din("ml_w_out", [D, D])
                phase_mlstm(c, l, l // 2, cur, dst, wi.ap(), wo_.ap(), cst)
            elif kind == "xattn":
                l = ph[1]
                if "memT" not in dr:
                    din("memT", [NK, 128, NMEM])
                ws = [din("xa_w%s%d" % (nm, l), [D, D]).ap() for nm in "qkvo"]
                phase_xattn(c, l, cur, dst, dr["memT"].ap(), ws[0], ws[1], ws[2], ws[3], cst)
            cur = dst
        fin = [("dram", "outT", it) for it in range(64)] + [("dram", "ya_out", it) for it in range(64)]
        c.s.add("sp", lambda e: e.nop(), r=fin)
        c.s.emit(nc, es)
    return nc, c


def phase_inputs(phases, inputs, b):
    m = {}
    for ph in phases:
        if ph[0] == "mlp":
            l = ph[1]
            m["ff_up%d" % l] = np.ascontiguousarray(inputs["ff_up"][l])
            m["ff_down%d" % l] = np.ascontiguousarray(inputs["ff_down"][l])
        elif ph[0] == "rwkv":
            jj = ph[1] // 2
            m["ev_w_in_rw"] = np.ascontiguousarray(inputs["ev_w_in"][jj][:, 0:1792])
            m["rw_w2"] = np.ascontiguousarray(inputs["rw_w2"][jj])
            m["rw_a2"] = np.ascontiguousarray(inputs["rw_a2"][jj])
            m["rw_g2"] = np.ascontiguousarray(inputs["rw_g2"][jj])
        elif ph[0] == "mamba":
            jj = ph[1] // 2
            m["ev_w_in_mb"] = np.ascontiguousarray(inputs["ev_w_in"][jj][:, 1792:3336])
            m["ev_w_out_a"] = np.ascontiguousarray(inputs["ev_w_out"][jj][0:512])
            m["ev_w_out_b"] = np.ascontiguousarray(inputs["ev_w_out"][jj][512:1024].reshape(8, 64, D))
        elif ph[0] == "mlstm":
            jj = ph[1] // 2
            m["ml_w_in"] = np.ascontiguousarray(inputs["ml_w_in"][jj])
            m["ml_w_out"] = np.ascontiguousarray(inputs["ml_w_out"][jj])
        elif ph[0] == "xattn":
            l = ph[1]
            m["memT"] = np.ascontiguousarray(inputs["mem"][b].T.reshape(NK, 128, NMEM))
            xa = {"q": inputs["xa_wq"], "k": inputs["xa_wk"], "v": inputs["xa_wv"], "o": inputs["xa_wo"]}
            for nm in "qkvo":
                m["xa_w%s%d" % (nm, l)] = np.ascontiguousarray(xa[nm][l])
    return m


FULL_PHASES = [("rwkv", 0), ("mamba", 0), ("xattn", 0), ("mlp", 0), ("mlstm", 1), ("xattn", 1), ("mlp", 1)]
_CACHE = {}


def kernel(**inputs):
    inputs = {k: np.asarray(v) for k, v in inputs.items()}
    consts, cindex = pack_consts(inputs)
    key = consts.shape
    if key not in _CACHE:
        _CACHE[key] = build(FULL_PHASES, consts, cindex)[0]
    nc = _CACHE[key]
    B = inputs["x"].shape[0]
    shared = phase_inputs(FULL_PHASES, inputs, 0)
    in_maps = []
    for b in range(B):
        m = dict(shared)
        m["xT"] = np.ascontiguousarray(inputs["x"][b].T.reshape(NK, 128, T))
        m["memT"] = np.ascontiguousarray(inputs["mem"][b].T.reshape(NK, 128, NMEM))
        m["consts"] = consts
        in_maps.append(m)
    res = run_bass_kernel_spmd(nc, in_maps, core_ids=list(range(B)))
    out = np.empty((B, T, D), np.float32)
    for b in range(B):
        out[b] = np.asarray(res.results[b]["outT"]).reshape(D, T).T
    return out
```

```python
import os
import numpy as np
from contextlib import ExitStack
import concourse.bass as bass
import concourse.mybir as mybir
from concourse.alu_op_type import AluOpType as ALU
from concourse.bass_utils import run_bass_kernel_spmd

F32 = mybir.dt.float32
BF16 = mybir.dt.bfloat16
AF = mybir.ActivationFunctionType

D = 1024
T = 4096
NK = 8
NMEM = 256
DFF = 4096
EPS = 1e-6
N_DSEM = 12
DBG_STOP = float(os.environ.get('MK_STOP', '99'))
PE_SWITCH_NS = float(os.environ.get('MK_PESW', '400'))


class Op:
    __slots__ = ("eng", "fn", "deps", "dma", "sig", "waits", "signal", "clock")

    def __init__(self, eng, fn, deps, dma):
        self.eng = eng
        self.fn = fn
        self.deps = deps
        self.dma = dma
        self.sig = None
        self.waits = None
        self.signal = False
        self.clock = None


class Sched:
    ENGS = ("pe", "act", "dve", "pool", "sp")

    def __init__(self):
        self.ops = []
        self.res = {}
        self.pending = {e: set() for e in self.ENGS}
        self.last_op = {}
        self.dmas = []
        self._cap = None
        self.fin = []
        self.eng_free = {}
        self.pe_mode = None

    def add(self, eng, fn, r=(), w=(), dma=False, wacc=(), cost=500.0, mode=None):
        if self._cap is not None:
            self._cap.append((eng, fn, tuple(r), tuple(w), dma, tuple(wacc), cost, mode))
            return None
        idx = len(self.ops)
        deps = self.pending[eng]
        if deps:
            self.pending[eng] = set()
        else:
            deps = set()
        res = self.res
        for t in r:
            st = res.get(t)
            if st is None:
                st = res[t] = [None, []]
            if st[0] is not None:
                if isinstance(st[0], list):
                    deps.update(st[0])
                else:
                    deps.add(st[0])
            st[1].append(idx)
        for t in wacc:
            st = res.get(t)
            if st is None:
                st = res[t] = [None, []]
            deps.update(st[1])
            if isinstance(st[0], list):
                st[0].append(idx)
            else:
                if st[0] is not None:
                    deps.add(st[0])
                st[0] = [idx]
            st[1] = []
        for t in w:
            st = res.get(t)
            if st is None:
                st = res[t] = [None, []]
            if st[0] is not None:
                if isinstance(st[0], list):
                    deps.update(st[0])
                else:
                    deps.add(st[0])
            if st[1]:
                ops = self.ops
                lastc = {}
                for ri in st[1]:
                    rop = ops[ri] if ri < idx else None
                    if rop is None:
                        continue
                    if rop.dma:
                        deps.add(ri)
                    else:
                        lastc[rop.eng] = ri
                deps.update(lastc.values())
            st[0] = idx
            st[1] = []
        deps.discard(idx)
        self.ops.append(Op(eng, fn, deps, dma))
        fin = self.fin
        ready = 0.0
        for d in deps:
            fd = fin[d]
            if fd > ready:
                ready = fd
        if dma:
            fin.append(max(ready + 100.0, self.eng_free.get(eng, 0.0)) + cost)
            self.eng_free[eng] = max(ready, self.eng_free.get(eng, 0.0)) + 60.0
            self.dmas.append(idx)
        else:
            st_ = max(ready + 120.0, self.eng_free.get(eng, 0.0))
            if mode is not None:
                if mode != self.pe_mode:
                    st_ += PE_SWITCH_NS
                self.pe_mode = mode
            fin.append(st_ + cost)
            self.eng_free[eng] = st_ + cost
            self.last_op[eng] = idx
        return idx

    def peek_start(self, a):
        eng, _, r, w, dma, wacc, cost, mode = a
        res, fin = self.res, self.fin
        ready = 0.0
        for t in r:
            st = res.get(t)
            if st is not None and st[0] is not None:
                for d in (st[0] if isinstance(st[0], list) else (st[0],)):
                    if fin[d] > ready:
                        ready = fin[d]
        for t in tuple(w) + tuple(wacc):
            st = res.get(t)
            if st is not None:
                if st[0] is not None:
                    for d in (st[0] if isinstance(st[0], list) else (st[0],)):
                        if fin[d] > ready:
                            ready = fin[d]
                for d in st[1]:
                    if fin[d] > ready:
                        ready = fin[d]
        pen = PE_SWITCH_NS if (mode is not None and mode != self.pe_mode) else 0.0
        return max(ready + 120.0, self.eng_free.get(eng, 0.0)) + pen

    def merge_n(self, threads, prio=None):
        pos = [0] * len(threads)
        prio = prio or [0.0] * len(threads)
        while True:
            best, bi = None, -1
            for i, th in enumerate(threads):
                if pos[i] < len(th):
                    st = self.peek_start(th[pos[i]]) + prio[i]
                    if best is None or st < best:
                        best, bi = st, i
            if bi < 0:
                break
            a = threads[bi][pos[bi]]
            self.add(a[0], a[1], r=a[2], w=a[3], dma=a[4], wacc=a[5], cost=a[6], mode=a[7])
            pos[bi] += 1

    def merge_greedy(self, A, B, bias=0.0):
        ia = ib = 0
        while ia < len(A) or ib < len(B):
            if ib >= len(B):
                pick = 0
            elif ia >= len(A):
                pick = 1
            else:
                sa = self.peek_start(A[ia])
                sb_ = self.peek_start(B[ib])
                pick = 0 if sa <= sb_ + bias else 1
            a = A[ia] if pick == 0 else B[ib]
            self.add(a[0], a[1], r=a[2], w=a[3], dma=a[4], wacc=a[5], cost=a[6], mode=a[7])
            if pick == 0:
                ia += 1
            else:
                ib += 1

    def capture(self, fn, *args):
        prev = self._cap
        self._cap = []
        fn(*args)
        caps, self._cap = self._cap, prev
        return caps

    def merge(self, A, B):
        nb = 0
        for i, a in enumerate(A):
            self.add(*a[:2], r=a[2], w=a[3], dma=a[4], wacc=a[5], cost=a[6], mode=a[7])
            tgt = ((i + 1) * len(B)) // max(1, len(A))
            while nb < tgt:
                b = B[nb]
                self.add(*b[:2], r=b[2], w=b[3], dma=b[4], wacc=b[5], cost=b[6], mode=b[7])
                nb += 1
        while nb < len(B):
            b = B[nb]
            self.add(*b[:2], r=b[2], w=b[3], dma=b[4], wacc=b[5], cost=b[6], mode=b[7])
            nb += 1

    def barrier(self):
        s = set(self.last_op.values()) | set(self.dmas)
        self.dmas = []
        for e in self.ENGS:
            self.pending[e] |= s

    def emit(self, nc, es):
        ops = self.ops
        n = len(ops)
        for op in ops:
            for d in op.deps:
                dop = ops[d]
                if dop.eng == "pe" and op.eng == "pe" and not dop.dma:
                    continue
                dop.signal = True
        sems = {}
        for e in ("pe", "act", "dve", "pool"):
            sems[e] = es.enter_context(nc.semaphore("s_" + e))
        for q in ("sp", "pool", "act"):
            for i in range(N_DSEM):
                sems[("d", q, i)] = es.enter_context(nc.semaphore("d_%s_%d" % (q, i)))
        cnt = {e: 0 for e in self.ENGS}
        dcnt = {e: 0 for e in self.ENGS}
        know = {e: {} for e in self.ENGS}
        nwaits = 0
        for op in ops:
            K = know[op.eng]
            waits = {}
            if op.dma:
                k = dcnt[op.eng]
                dcnt[op.eng] += 1
                key = ("d", op.eng, k % N_DSEM)
                op.sig = (key, 16 * (k // N_DSEM + 1))
                op.signal = True
                if k >= N_DSEM and K.get(key, 0) < 16 * (k // N_DSEM):
                    waits[key] = 16 * (k // N_DSEM)
                    K[key] = 16 * (k // N_DSEM)
            elif op.signal:
                cnt[op.eng] += 1
                op.sig = (op.eng, cnt[op.eng])
            for d in sorted(op.deps):
                dop = ops[d]
                if dop.eng == "pe" and op.eng == "pe" and not dop.dma:
                    continue
                key, val = dop.sig
                if K.get(key, 0) >= val:
                    continue
                if waits.get(key, 0) < val:
                    waits[key] = val
                for kk, vv in dop.clock.items():
                    if K.get(kk, 0) < vv:
                        K[kk] = vv
            op.waits = list(waits.items())
            nwaits += len(op.waits)
            if op.signal:
                c = dict(K)
                c[op.sig[0]] = op.sig[1]
                op.clock = c
        self.stats = dict(n_ops=n, n_waits=nwaits, cnt=dict(cnt), dcnt=dict(dcnt))
        self.check()
        engmap = {"pe": "tensor", "act": "scalar", "dve": "vector", "pool": "gpsimd", "sp": "sync"}
        with nc.Block() as block:
            for e in self.ENGS:
                mine = [op for op in ops if op.eng == e]
                if not mine:
                    continue

                def body(eng, mine=mine):
                    for op in mine:
                        for key, val in op.waits:
                            eng.wait_ge(sems[key], val)
                        ins = op.fn(eng)
                        if op.signal:
                            ins.then_inc(sems[op.sig[0]], 16 if op.dma else 1)

                getattr(block, engmap[e])(body)


def _sched_check(self):
    per = {e: [op for op in self.ops if op.eng == e] for e in self.ENGS}
    ptr = {e: 0 for e in self.ENGS}
    val = {}
    progress = True
    while progress:
        progress = False
        for e in self.ENGS:
            while ptr[e] < len(per[e]):
                op = per[e][ptr[e]]
                if all(val.get(k, 0) >= v for k, v in op.waits):
                    if op.signal:
                        val[op.sig[0]] = val.get(op.sig[0], 0) + (16 if op.dma else 1)
                        assert val[op.sig[0]] == op.sig[1], (op.sig, val[op.sig[0]])
                    ptr[e] += 1
                    progress = True
                else:
                    break
    stuck = {e: (ptr[e], len(per[e])) for e in self.ENGS if ptr[e] < len(per[e])}
    assert not stuck, "sync deadlock: %s" % stuck


Sched.check = _sched_check


class Ctx:
    def __init__(self, nc):
        self.nc = nc
        self.s = Sched()
        self.uid = 0

    def name(self, base):
        self.uid += 1
        return "%s_%d" % (base, self.uid)

    @staticmethod
    def _n(ap):
        n = 1
        for d in ap.shape[1:]:
            n *= int(d)
        return n

    def _cost(self, eng, ap):
        n = self._n(ap)
        if eng == "dve":
            return n / 0.96 + 150.0
        if eng == "act":
            return n / 1.2 + 250.0
        if eng == "pool":
            return n * 2.4 + 200.0
        return 500.0

    def mm(self, out, lhsT, rhs, start, stop, r, w, **kw):
        n = max(64, self._n(rhs))
        cost = (n / 2.4 + 12.0) * (4.0 if rhs.dtype == F32 else 1.0)
        rnd = lambda v: 32 if v <= 32 else (64 if v <= 64 else 128)
        mode = (rnd(int(lhsT.shape[0])), rnd(self._n(lhsT)), rhs.dtype == F32)
        self.s.add("pe", lambda e: e.matmul(out, lhsT, rhs, start=start, stop=stop, **kw), r=r, w=w, cost=cost, mode=mode)

    def tr(self, out, in_, ident, r, w):
        rnd = lambda v: 32 if v <= 32 else (64 if v <= 64 else 128)
        mode = ("T", rnd(int(in_.shape[0])), rnd(self._n(in_)), in_.dtype == F32)
        self.s.add("pe", lambda e: e.transpose(out, in_, ident), r=r, w=w, cost=70.0, mode=mode)

    def act(self, out, in_, func, r, w, bias=None, scale=None, accum_out=None, eng="act"):
        kw = {}
        if bias is not None:
            kw["bias"] = bias
        if scale is not None:
            kw["scale"] = scale
        if accum_out is not None:
            kw["accum_out"] = accum_out
        self.s.add(eng, lambda e: e.activation(out, in_, func, **kw), r=r, w=w, cost=self._cost(eng, out))

    def tt(self, out, in0, in1, op, r, w, eng="dve"):
        self.s.add(eng, lambda e: e.tensor_tensor(out, in0, in1, op), r=r, w=w, cost=self._cost(eng, out))

    def ts(self, out, in0, s1, op0, r, w, s2=None, op1=None, eng="dve"):
        if op1 is None:
            self.s.add(eng, lambda e: e.tensor_scalar(out, in0, s1, None, op0), r=r, w=w, cost=self._cost(eng, out))
        else:
            self.s.add(eng, lambda e: e.tensor_scalar(out, in0, s1, s2, op0, op1), r=r, w=w, cost=self._cost(eng, out))

    def stt(self, out, in0, scalar, in1, op0, op1, r, w):
        self.s.add("dve", lambda e: e.scalar_tensor_tensor(out, in0, scalar, in1, op0, op1), r=r, w=w, cost=self._cost("dve", out))

    def cp(self, out, in_, r, w, eng="dve"):
        if eng == "act":
            self.s.add("act", lambda e: e.copy(out, in_), r=r, w=w, cost=self._cost("act", out))
        else:
            self.s.add(eng, lambda e: e.tensor_copy(out, in_), r=r, w=w, cost=self._cost(eng, out))

    def recip(self, out, in_, r, w):
        self.s.add("dve", lambda e: e.reciprocal(out, in_), r=r, w=w, cost=self._cost("dve", out))

    def memset(self, ap, val, w, eng="pool"):
        self.s.add(eng, lambda e: e.memset(ap, val), w=w)

    def dma(self, out, in_, r, w, q="sp", wacc=(), **kw):
        self.s.add(q, lambda e: e.dma_start(out=out, in_=in_, **kw), r=r, w=w, dma=True, wacc=wacc,
                   cost=2500.0 + self._n(out) * 128 * 4 / 150.0)


def load_weight_bf16(c, dst, src, rows_tok, cols, wtok, kchunks, tokfn=None):
    srcv = src.rearrange("(k p) n -> p k n", p=128)
    step = 2048
    for k in range(kchunks):
        for c0 in range(0, cols, step):
            c1 = min(cols, c0 + step)
            c.dma(dst[:, k, c0:c1], srcv[:, k, c0:c1], r=(), w=(), wacc=(wtok if tokfn is None else tokfn(k, c0),), q="pool")


class Common:
    def __init__(self, c, es, TT, lite=False):
        nc = c.nc
        self.c = c
        self.TT = TT
        sb = lambda name, shape, dt: es.enter_context(nc.sbuf_tensor(c.name(name), shape, dt))
        self.sb = sb
        self.xs = [sb("x", [128, NK, TT], F32) for _ in range(2)]
        self.hn = [sb("hn", [128, NK, TT], BF16) for _ in range(2)]
        self.sq = sb("sq", [128, NK, TT], BF16)
        self.rstd = [sb("rstd", [128, TT], F32) for _ in range(2)]
        if not lite:
            self.tmp = [sb("tmp", [128, TT], F32) for _ in range(2)]
            self.yy = sb("yy", [128, NK, TT], F32)
        self.nrm = 0
        self.pb = 0
        self.gpb = {}

    def bank(self, group=None):
        if group is None:
            self.pb = (self.pb + 1) % 7
            return self.pb
        lo, n = group
        k = self.gpb.get(group, 0)
        self.gpb[group] = k + 1
        return lo + k % n

    def load_x(self, it, x_in):
        c, TT = self.c, self.TT
        b = it % 2
        t0 = it * TT
        c.dma(self.xs[b][:], x_in[0][:, :, t0:t0 + TT].rearrange("k p t -> p k t"),
              r=tuple(("dram", x_in[1], j) for j in range(t0 // 64, (t0 + TT) // 64)), w=(("x", b),))
        return self.xs[b], ("x", b)

    def store_x(self, it, x_out):
        c, TT = self.c, self.TT
        b = it % 2
        t0 = it * TT
        c.dma(x_out[0][:, :, t0:t0 + TT].rearrange("k p t -> p k t"), self.xs[b][:], r=(("x", b),),
              w=tuple(("dram", x_out[1], j) for j in range(t0 // 64, (t0 + TT) // 64)))

    def stats(self, src, tok_src, n=None):
        c = self.c
        n = n or self.TT
        P = c.psum
        self.nrm += 1
        rb = self.nrm % 2
        c.act(self.sq[:, :, 0:n], src, AF.Square, r=(tok_src,), w=("sq",) + tuple(getattr(self, "sq_alias", ())))
        for k in range(NK):
            c.mm(P[7][:, 0:n], c.ones, self.sq[:, k, 0:n], start=(k == 0), stop=(k == NK - 1), r=("sq",), w=(("ps", 7),))
        rs = self.rstd[rb][:, 0:n]
        tok = ("rstd", rb)
        c.act(rs, P[7][:, 0:n], AF.Ln, r=(("ps", 7),), w=(tok,), bias=c.eps_ap, scale=1.0 / D)
        c.act(rs, rs, AF.Exp, r=(tok,), w=(tok,), scale=-0.5)
        return rs, tok

    def prenorm(self, it, g):
        c = self.c
        b = it % 2
        X, tx = self.xs[b], ("x", b)
        rs, tok = self.stats(X[:], tx)
        for k in range(NK):
            c.stt(self.hn[b][:, k, :], X[:, k, :], g[:, k:k + 1], rs, ALU.mult, ALU.mult, r=(tx, tok), w=(("hn", b),))
        return self.hn[b], ("hn", b)

    def get_hn(self, it, g, x_in):
        if getattr(self, "pref", None) == it:
            return self.hn[it % 2], ("hn", it % 2)
        self.load_x(it, x_in)
        return self.prenorm(it, g)

    def prefetch(self, it, g, x_in, NT):
        if it < NT:
            self.load_x(it, x_in)
            self.prenorm(it, g)
            self.pref = it

    def post_residual(self, it, g):
        c = self.c
        b = it % 2
        X, tx = self.xs[b], ("x", b)
        rs, tok = self.stats(self.yy[:], "yy")
        for k in range(NK):
            tb = k % 2
            c.stt(self.tmp[tb][:], self.yy[:, k, :], g[:, k:k + 1], rs, ALU.mult, ALU.mult, r=("yy", tok), w=(("tmp", tb),))
            c.tt(X[:, k, :], self.tmp[tb][:], X[:, k, :], ALU.add, r=(("tmp", tb), tx), w=(tx,),
                 eng=("dve" if os.environ.get("MK_VAR2", "") == "E" else "pool"))


def phase_mlp(c, layer, x_in, x_out, w_up, w_down, cst):
    nc = c.nc
    TT = 256
    NT = T // TT
    NF = DFF // 128
    with ExitStack() as es:
        cm = Common(c, es, TT)
        sb = cm.sb
        wu = sb("wu", [128, NK, DFF], BF16)
        wd = sb("wd", [128, NF, D], BF16)
        hh = sb("hh", [128, NF, TT], BF16)
        rr = [sb("rr", [128, TT], BF16) for _ in range(2)]
        P = c.psum
        g1 = cst["norm_ff_pre"][layer]
        g2 = cst["norm_ff_post"][layer]
        load_weight_bf16(c, wu, w_up, D, DFF, "wu", NK, tokfn=lambda k, c0: ("wu", c0 // 2048))
        load_weight_bf16(c, wd, w_down, DFF, D, "wd", NF, tokfn=lambda k, c0: ("wd", k))
        for it in range(NT):
            hn, thn = cm.get_hn(it, g1, x_in)
            for f in range(NF):
                pb = cm.bank()
                for k in range(NK):
                    c.mm(P[pb][:, 0:TT], wu[:, k, f * 128:(f + 1) * 128], hn[:, k, :], start=(k == 0), stop=(k == NK - 1),
                         r=(("wu", f // 16), thn), w=(("ps", pb),))
                rb = f % 2
                c.act(rr[rb][:], P[pb][:, 0:TT], AF.Relu, r=(("ps", pb),), w=(("rr", rb),))
                c.tt(hh[:, f, :], P[pb][:, 0:TT], rr[rb][:], ALU.mult, r=(("ps", pb), ("rr", rb)), w=(("hh", f),))
            cm.prefetch(it + 1, g1, x_in, NT)
            for o in range(NK):
                pb = cm.bank()
                for f in range(NF):
                    c.mm(P[pb][:, 0:TT], wd[:, f, o * 128:(o + 1) * 128], hh[:, f, :], start=(f == 0), stop=(f == NF - 1),
                         r=(("wd", f), ("hh", f)), w=(("ps", pb),))
                c.cp(cm.yy[:, o, :], P[pb][:, 0:TT], r=(("ps", pb),), w=("yy",), eng="act")
            cm.post_residual(it, g2)
            cm.store_x(it, x_out)
        c.s.barrier()


def phase_xattn(c, layer, x_in, x_out, memT, wq_d, wk_d, wv_d, wo_d, cst):
    nc = c.nc
    TT = 512
    NT = T // TT
    with ExitStack() as es:
        cm = Common(c, es, TT)
        sb = cm.sb
        wq = sb("wq", [128, NK, D], BF16)
        wo = sb("wo", [128, NK, D], BF16)
        kT = sb("kT", [128, NK, NMEM], BF16)
        V = sb("V", [128, 2, D], BF16)
        qT = sb("qT", [128, NK, TT], BF16)
        oT = sb("oT", [128, NK, TT], BF16)
        E = [sb("E", [128, 2, TT], BF16) for _ in range(2)]
        rden = [sb("rden", [128, TT], F32) for _ in range(2)]
        P = c.psum
        g_pre = cst["norm_xa_pre"][layer]
        g_post = cst["norm_xa_post"][layer]
        g_mem = cst["norm_mem"][layer]
        with ExitStack() as es2:
            sb2 = lambda name, shape, dt: es2.enter_context(nc.sbuf_tensor(c.name(name), shape, dt))
            wk = sb2("wk", [128, NK, D], BF16)
            wv = sb2("wv", [128, NK, D], BF16)
            load_weight_bf16(c, wk, wk_d, D, D, "wk", NK)
            load_weight_bf16(c, wv, wv_d, D, D, "wv", NK)
            load_weight_bf16(c, wq, wq_d, D, D, "wq", NK)
            load_weight_bf16(c, wo, wo_d, D, D, "wo", NK)
            mx = cm.xs[1]
            c.dma(mx[:, :, 0:NMEM], memT.rearrange("k p t -> p k t"), r=(), w=(("x", 1),))
            rs, tok = cm.stats(mx[:, :, 0:NMEM], ("x", 1), n=NMEM)
            mn = cm.hn[1]
            for k in range(NK):
                c.stt(mn[:, k, 0:NMEM], mx[:, k, 0:NMEM], g_mem[:, k:k + 1], rs, ALU.mult, ALU.mult,
                      r=(("x", 1), tok), w=(("hn", 1),))
            for cc in range(NK):
                pb = cm.bank()
                for k in range(NK):
                    c.mm(P[pb][:, 0:NMEM], wk[:, k, cc * 128:(cc + 1) * 128], mn[:, k, 0:NMEM], start=(k == 0), stop=(k == NK - 1),
                         r=("wk", ("hn", 1)), w=(("ps", pb),))
                c.cp(kT[:, cc, :], P[pb][:, 0:NMEM], r=(("ps", pb),), w=("kT",), eng="act")
            for mc in range(2):
                for hf in range(2):
                    pb = cm.bank()
                    for k in range(NK):
                        c.mm(P[pb][:, :], mn[:, k, mc * 128:(mc + 1) * 128], wv[:, k, hf * 512:(hf + 1) * 512],
                             start=(k == 0), stop=(k == NK - 1), r=("wv", ("hn", 1)), w=(("ps", pb),))
                    c.cp(V[:, mc, hf * 512:(hf + 1) * 512], P[pb][:, :], r=(("ps", pb),), w=("V",), eng="dve")
            c.s.barrier()
        for it in range(NT):
            hn, thn = cm.get_hn(it, g_pre, x_in)
            for cc in range(NK):
                pb = cm.bank()
                for k in range(NK):
                    c.mm(P[pb][:, :], wq[:, k, cc * 128:(cc + 1) * 128], hn[:, k, :], start=(k == 0), stop=(k == NK - 1),
                         r=("wq", thn), w=(("ps", pb),))
                c.act(qT[:, cc, :], P[pb][:, :], AF.Copy, r=(("ps", pb),), w=(("qT", cc),), scale=1.0 / 16.0)
            cm.prefetch(it + 1, g_pre, x_in, NT)
            def scores(h):
                eb = h % 2
                for mc in range(2):
                    pb = cm.bank()
                    for ci in range(2):
                        cc = 2 * h + ci
                        c.mm(P[pb][:, :], kT[:, cc, mc * 128:(mc + 1) * 128], qT[:, cc, :], start=(ci == 0), stop=(ci == 1),
                             r=("kT", ("qT", cc)), w=(("ps", pb),))
                    c.act(E[eb][:, mc, :], P[pb][:, :], AF.Exp, r=(("ps", pb),), w=(("E", eb, mc),))

            def attend(h):
                eb = h % 2
                pd = cm.bank()
                for mc in range(2):
                    c.mm(P[pd][:, :], c.ones, E[eb][:, mc, :], start=(mc == 0), stop=(mc == 1), r=(("E", eb, mc),), w=(("ps", pd),))
                c.recip(rden[eb][:], P[pd][:, :], r=(("ps", pd),), w=(("rden", eb),))
                for ci in range(2):
                    cc = 2 * h + ci
                    pb = cm.bank()
                    for mc in range(2):
                        c.mm(P[pb][:, :], V[:, mc, cc * 128:(cc + 1) * 128], E[eb][:, mc, :], start=(mc == 0), stop=(mc == 1),
                             r=("V", ("E", eb, mc)), w=(("ps", pb),))
                    c.tt(oT[:, cc, :], P[pb][:, :], rden[eb][:], ALU.mult, r=(("ps", pb), ("rden", eb)), w=(("oT", cc),))

            scores(0)
            for h in range(4):
                if h + 1 < 4:
                    scores(h + 1)
                attend(h)
            for o in range(NK):
                pb = cm.bank()
                for cc in range(NK):
                    c.mm(P[pb][:, :], wo[:, cc, o * 128:(o + 1) * 128], oT[:, cc, :], start=(cc == 0), stop=(cc == NK - 1),
                         r=("wo", ("oT", cc)), w=(("ps", pb),))
                c.cp(cm.yy[:, o, :], P[pb][:, :], r=(("ps", pb),), w=("yy",), eng="act")
            cm.post_residual(it, g_post)
            cm.store_x(it, x_out)
        c.s.barrier()


def phase_mlstm(c, layer, j, x_in, x_out, w_in_d, w_out_d, cst):
    nc = c.nc
    TT = 256
    NT = T // TT
    NC_ = TT // 64
    NG = 2
    GH = 8 // NG
    with ExitStack() as es:
        cm = Common(c, es, TT)
        sb = cm.sb
        P = c.psum
        win = sb("mlwin", [128, NK, 3088], BF16)
        wout = sb("mlwout", [128, NK, D], BF16)
        load_weight_bf16(c, win, w_in_d, D, 3088, "win", NK)
        load_weight_bf16(c, wout, w_out_d, D, D, "wout", NK)
        g_pre = cst["norm_mix_pre"][layer]
        g_post = cst["norm_mix_post"][layer]
        nw = cst["ml_norm_w"][j]
        bi = cst["ml_b_i"][j]
        bf_ = cst["ml_b_f"][j]
        selh = cst["selh"][0]
        ident = cst["ident"][0]
        mask = cst["mask_incl"][0]
        reset = cst["reset"][0]
        gsm = lambda nm: sb(nm, [8, TT], F32)
        b15 = sb("b15", [8, 2], F32)
        one1 = sb("one1", [8, 1], F32)
        onesb = sb("onesb", [64, 128], BF16)
        th_i, th_f, e1, l1, bneg, arg, arg2 = [gsm(n) for n in ("thi", "thf", "e1", "l1", "bneg", "arg", "arg2")]
        eb, ek, eend = gsm("eb"), gsm("ek"), gsm("eend")
        EB = sb("EB", [64, 8, TT], F32)
        EK = sb("EK", [64, 8, TT], F32)
        qt = sb("qt", [64, 8, TT], BF16)
        kt = sb("kt", [64, 8, TT], BF16)
        eT = [sb("eT", [64, 8], F32) for _ in range(NC_)]
        KhT = [sb("KhT", [64, 8, 64], BF16) for _ in range(NC_)]
        Va = [sb("Va", [64, 8, 128], BF16) for _ in range(NC_)]
        ST = [sb("ST", [64, 8, 64], BF16) for _ in range(2)]
        CT = sb("CT", [64, 8, 128], F32)
        CTb = sb("CTb", [64, 8, 128], BF16)
        nst = sb("nst", [64, 8], F32)
        nrep = sb("nrep", [64, 8, 128], BF16)
        dmax = sb("dmax", [128, 512], F32)
        hT = [sb("hT", [128, 8, TT], F32) for _ in range(2)]
        rsh = [sb("rsh", [128, TT], F32) for _ in range(2 * NG)]
        sg = [sb("sg", [128, TT], F32) for _ in range(2 * NG)]
        t1 = [sb("t1", [128, TT], F32) for _ in range(2 * NG)]
        mix = sb("mix", [128, 8, TT], BF16)
        c.memset(CT[:], 0.0, w=tuple(("CT", g) for g in range(NG)))
        c.memset(CTb[:], 0.0, w=tuple(("CTb", g) for g in range(NG)))
        c.memset(nst[:], 0.0, w=tuple(("nst", g) for g in range(NG)))
        c.memset(nrep[:], 0.0, w=tuple(("nrep", g) for g in range(NG)))
        c.memset(onesb[:], 1.0, w=("onesb",))
        c.memset(one1[:], 1.0, w=("one1",))
        c.ts(b15[:, 0:1], bi[0:8, :], 1.0 / 15.0, ALU.mult, r=(), w=("b15",))
        c.ts(b15[:, 1:2], bf_[0:8, :], 1.0 / 15.0, ALU.mult, r=(), w=("b15",))
        V3 = lambda ap, d: ap.rearrange("p (h d) -> p h d", d=d)
        GBANK = [(0, 3), (3, 3)]
        CB_ = (6, 1)
        cm.sq_alias = tuple(("sqg", g) for g in range(NG))

        def head(it):
            hn, thn = cm.get_hn(it, g_pre, x_in)
            pgi, pgf = cm.bank(CB_), cm.bank(GBANK[0])
            for (pg, c0) in ((pgi, 3072), (pgf, 3080)):
                for k in range(NK):
                    c.mm(P[pg][0:8, 0:TT], win[:, k, c0:c0 + 8], hn[:, k, :], start=(k == 0), stop=(k == NK - 1),
                         r=("win", thn), w=(("ps", pg),))
            c.act(th_i[:], P[pgi][0:8, 0:TT], AF.Tanh, r=(("ps", pgi), "b15"), w=("thi",), bias=b15[:, 0:1], scale=1.0 / 15.0)
            c.act(th_f[:], P[pgf][0:8, 0:TT], AF.Tanh, r=(("ps", pgf), "b15"), w=("thf",), bias=b15[:, 1:2], scale=1.0 / 15.0)
            c.act(e1[:], th_f[:], AF.Exp, r=("thf",), w=("e1",), scale=-15.0)
            c.act(l1[:], e1[:], AF.Ln, r=("e1", "one1"), w=("l1",), bias=one1[:], scale=1.0)
            c.s.add("dve", lambda e: e.tensor_tensor_scan(bneg[:], reset[0:8, 0:TT], l1[:], 0.0, ALU.mult, ALU.add),
                    r=("l1",), w=("bneg",))
            c.act(eb[:], bneg[:], AF.Exp, r=("bneg",), w=("eb",), scale=-1.0)
            c.stt(arg[:], th_i[:], 15.0, bneg[:], ALU.mult, ALU.add, r=("thi", "bneg"), w=("arg",))
            c.act(ek[:], arg[:], AF.Exp, r=("arg",), w=("ek",))
            a3 = arg[:].rearrange("p (c l) -> p c l", l=64)
            bL = bneg[:].rearrange("p (c l) -> p c l", l=64)[:, :, 63:64].broadcast_to([8, NC_, 64])
            c.tt(arg2[:].rearrange("p (c l) -> p c l", l=64), a3, bL, ALU.subtract, r=("arg", "bneg"), w=("arg2",))
            c.act(eend[:], arg2[:], AF.Exp, r=("arg2",), w=("eend",))
            for jc in range(NC_):
                tsl = slice(jc * 64, (jc + 1) * 64)
                pt = cm.bank(CB_)
                c.tr(P[pt][0:64, 0:8], eend[0:8, tsl], ident[0:8, 0:8], r=("eend",), w=(("ps", pt),))
                c.ts(eT[jc][:], P[pt][0:64, 0:8], 0.125, ALU.mult, r=(("ps", pt),), w=(("eT", jc),))

        def group(it, g):
            b = it % 2
            G = GBANK[g]
            hn, thn = cm.hn[b], ("hn", b)
            hsl = slice(g * GH, (g + 1) * GH)
            for h2 in range(g * GH // 2, (g + 1) * GH // 2):
                pe_, pk_ = cm.bank(G), cm.bank(G)
                for i in range(2):
                    h = 2 * h2 + i
                    cs = slice(i * TT, (i + 1) * TT)
                    c.mm(P[pe_][0:64, cs], selh[0:8, h * 64:(h + 1) * 64], eb[:], start=True, stop=True, r=("eb",), w=(("ps", pe_),))
                    c.mm(P[pk_][0:64, cs], selh[0:8, h * 64:(h + 1) * 64], ek[:], start=True, stop=True, r=("ek",), w=(("ps", pk_),))
                hs2 = slice(2 * h2, 2 * h2 + 2)
                c.cp(EB[:, hs2, :], V3(P[pe_][0:64, :], TT), r=(("ps", pe_),), w=(("EB", h2),), eng="act")
                c.cp(EK[:, hs2, :], V3(P[pk_][0:64, :], TT), r=(("ps", pk_),), w=(("EK", h2),), eng="act")
                pq, pk2 = cm.bank(G), cm.bank(G)
                for i in range(2):
                    h = 2 * h2 + i
                    cs = slice(i * TT, (i + 1) * TT)
                    for k in range(NK):
                        c.mm(P[pq][0:64, cs], win[:, k, h * 64:(h + 1) * 64], hn[:, k, :], start=(k == 0), stop=(k == NK - 1),
                             r=("win", thn), w=(("ps", pq),))
                    for k in range(NK):
                        c.mm(P[pk2][0:64, cs], win[:, k, 512 + h * 64:512 + (h + 1) * 64], hn[:, k, :], start=(k == 0),
                             stop=(k == NK - 1), r=("win", thn), w=(("ps", pk2),))
                c.tt(qt[:, hs2, :], V3(P[pq][0:64, :], TT), EB[:, hs2, :], ALU.mult,
                     r=(("ps", pq), ("EB", h2)), w=(("qt", h2),))
                c.stt(kt[:, hs2, :], V3(P[pk2][0:64, :], TT), 0.125, EK[:, hs2, :], ALU.mult, ALU.mult,
                      r=(("ps", pk2), ("EK", h2)), w=(("kt", h2),))
            for jc in range(NC_):
                tsl = slice(jc * 64, (jc + 1) * 64)
                pkk = cm.bank(G)
                for k in range(NK):
                    c.mm(P[pkk][0:64, 0:GH * 64], hn[:, k, tsl], win[:, k, 512 + g * GH * 64:512 + (g + 1) * GH * 64], start=(k == 0),
                         stop=(k == NK - 1), r=("win", thn), w=(("ps", pkk),))
                c.tt(KhT[jc][:, hsl, :], V3(P[pkk][0:64, 0:GH * 64], 64),
                     eT[jc][:, hsl].unsqueeze(2).broadcast_to([64, GH, 64]),
                     ALU.mult, r=(("ps", pkk), ("eT", jc)), w=(("KhT", jc, g),))
                pv = cm.bank(G)
                for k in range(NK):
                    c.mm(P[pv][0:64, 0:GH * 128], hn[:, k, tsl], win[:, k, 1024 + g * GH * 128:1024 + (g + 1) * GH * 128], start=(k == 0),
                         stop=(k == NK - 1), r=("win", thn), w=(("ps", pv),))
                c.cp(Va[jc][:, hsl, :], V3(P[pv][0:64, 0:GH * 128], 128), r=(("ps", pv),), w=(("Va", jc, g),), eng="act")
            for jc in range(NC_):
                cols = slice(jc * 64, (jc + 1) * 64)
                jb = jc % 2
                pS = cm.bank(G)
                for hi in range(GH):
                    h = g * GH + hi
                    c.mm(P[pS][0:64, hi * 64:(hi + 1) * 64], kt[:, h, cols], qt[:, h, cols], start=True, stop=True,
                         r=(("kt", h // 2), ("qt", h // 2)), w=(("ps", pS),))
                c.tt(ST[jb][:, hsl, :], V3(P[pS][0:64, 0:GH * 64], 64),
                     mask[0:64, :].unsqueeze(1).broadcast_to([64, GH, 64]), ALU.mult, r=(("ps", pS),), w=(("ST", jb, g),))
                pN, pD = cm.bank(G), cm.bank(G)
                for hi in range(GH):
                    h = g * GH + hi
                    hs = slice(hi * 64, (hi + 1) * 64)
                    c.mm(P[pN][:, hs], CTb[:, h, :], qt[:, h, cols], start=True, stop=False,
                         r=(("CTb", g), ("qt", h // 2)), w=(("ps", pN),))
                    c.mm(P[pN][:, hs], Va[jc][:, h, :], ST[jb][:, h, :], start=False, stop=True,
                         r=(("Va", jc, g), ("ST", jb, g)), w=(("ps", pN),))
                    c.mm(P[pD][:, hs], nrep[:, h, :], qt[:, h, cols], start=True, stop=False,
                         r=(("nrep", g), ("qt", h // 2)), w=(("ps", pD),))
                    c.mm(P[pD][:, hs], onesb[:], ST[jb][:, h, :], start=False, stop=True,
                         r=("onesb", ("ST", jb, g)), w=(("ps", pD),))
                dm = dmax[:, g * GH * 64:(g + 1) * GH * 64]
                c.ts(dm, P[pD][:, 0:GH * 64], -1.0, ALU.mult, r=(("ps", pD),), w=(("dmax", g),), s2=1.0, op1=ALU.max)
                c.tt(dm, P[pD][:, 0:GH * 64], dm, ALU.max, r=(("ps", pD), ("dmax", g)), w=(("dmax", g),))
                c.recip(dm, dm, r=(("dmax", g),), w=(("dmax", g),))
                c.tt(hT[b][:, hsl, cols], V3(P[pN][:, 0:GH * 64], 64), V3(dm, 64), ALU.mult,
                     r=(("ps", pN), ("dmax", g)), w=(("hT", b, g),))
                pU = cm.bank(G)
                pUn = cm.bank(G)
                for hi in range(GH):
                    h = g * GH + hi
                    c.mm(P[pU][0:64, hi * 128:(hi + 1) * 128], KhT[jc][:, h, :], Va[jc][:, h, :],
                         start=True, stop=True, r=(("KhT", jc, g), ("Va", jc, g)), w=(("ps", pU),))
                    c.mm(P[pUn][0:64, hi:hi + 1], KhT[jc][:, h, :], onesb[:, 0:1], start=True, stop=True,
                         r=(("KhT", jc, g), "onesb"), w=(("ps", pUn),))
                ebl = EB[:, hsl, jc * 64 + 63:jc * 64 + 64]
                ebl_toks = tuple(("EB", q) for q in range(g * GH // 2, (g + 1) * GH // 2))
                c.tt(CT[:, hsl, :], CT[:, hsl, :], ebl.broadcast_to([64, GH, 128]), ALU.mult, r=(("CT", g),) + ebl_toks, w=(("CT", g),))
                c.tt(CT[:, hsl, :], CT[:, hsl, :], V3(P[pU][0:64, 0:GH * 128], 128), ALU.add, r=(("CT", g), ("ps", pU)), w=(("CT", g),))
                c.cp(CTb[:, hsl, :], CT[:, hsl, :], r=(("CT", g),), w=(("CTb", g),), eng="act")
                c.tt(nst[:, hsl], nst[:, hsl], ebl.rearrange("p h o -> p (h o)"), ALU.mult, r=(("nst", g),) + ebl_toks, w=(("nst", g),))
                c.tt(nst[:, hsl], nst[:, hsl], P[pUn][0:64, 0:GH], ALU.add, r=(("nst", g), ("ps", pUn)), w=(("nst", g),))
                c.cp(nrep[:, hsl, :], nst[:, hsl].unsqueeze(2).broadcast_to([64, GH, 128]), r=(("nst", g),), w=(("nrep", g),), eng="pool")
            H = hT[b]
            c.act(cm.sq[:, hsl, :], H[:, hsl, :], AF.Square, r=(("hT", b, g),), w=(("sqg", g),))
            for hi in range(GH):
                h = g * GH + hi
                hb2 = g * 2 + hi % 2
                pb = cm.bank(G)
                c.mm(P[pb][:, 0:TT], c.ones, cm.sq[:, h, :], start=True, stop=True, r=(("sqg", g),), w=(("ps", pb),))
                c.act(rsh[hb2][:], P[pb][:, 0:TT], AF.Ln, r=(("ps", pb),), w=(("rsh", hb2),), bias=c.eps_ap, scale=1.0 / 128.0)
                c.act(rsh[hb2][:], rsh[hb2][:], AF.Exp, r=(("rsh", hb2),), w=(("rsh", hb2),), scale=-0.5)
                po = cm.bank(G)
                for k in range(NK):
                    c.mm(P[po][:, 0:TT], win[:, k, 2048 + h * 128:2048 + (h + 1) * 128], hn[:, k, :], start=(k == 0), stop=(k == NK - 1),
                         r=("win", thn), w=(("ps", po),))
                c.act(sg[hb2][:], P[po][:, 0:TT], AF.Sigmoid, r=(("ps", po),), w=(("sg", hb2),))
                c.stt(t1[hb2][:], H[:, h, :], nw[:, h:h + 1], rsh[hb2][:], ALU.mult, ALU.mult, r=(("hT", b, g), ("rsh", hb2)), w=(("t1", hb2),))
                c.tt(mix[:, h, :], t1[hb2][:], sg[hb2][:], ALU.mult, r=(("t1", hb2), ("sg", hb2)), w=(("mix", h),), eng="pool")

        def tail(it):
            for o in range(NK):
                pb = cm.bank(CB_) if o % 2 == 0 else cm.bank(GBANK[0])
                for h in range(8):
                    c.mm(P[pb][:, 0:TT], wout[:, h, o * 128:(o + 1) * 128], mix[:, h, :], start=(h == 0), stop=(h == 7),
                         r=("wout", ("mix", h)), w=(("ps", pb),))
                c.cp(cm.yy[:, o, :], P[pb][:, 0:TT], r=(("ps", pb),), w=("yy",), eng="act")
            cm.post_residual(it, g_post)
            cm.store_x(it, x_out)

        S = c.s
        head(0)
        for it in range(NT):
            thr = [S.capture(group, it, g) for g in range(NG)]
            if os.environ.get("MK_NOIL", "0") == "1":
                for t_ in thr:
                    S.merge(t_, [])
            else:
                S.merge_greedy(thr[0], thr[1])
            cm.prefetch(it + 1, g_pre, x_in, NT)
            if it + 1 < NT:
                head(it + 1)
            tail(it)
        c.s.barrier()


def phase_mamba(c, layer, j, x_in, x_out, ya_d, w_in_d, w_outa_d, w_outb_d, cst):
    nc = c.nc
    TT = 256
    NT = T // TT
    NC_ = TT // 64
    with ExitStack() as es:
        cm = Common(c, es, TT)
        sb = cm.sb
        P = c.psum
        win = sb("mbwin", [128, NK, 1544], BF16)
        woa = sb("mbwoa", [128, 4, D], BF16)
        wob = sb("mbwob", [64, 8, D], BF16)
        load_weight_bf16(c, win, w_in_d, D, 1544, "win", NK)
        load_weight_bf16(c, woa, w_outa_d, 512, D, "woa", 4)
        for h in range(8):
            c.dma(wob[:, h, :], w_outb_d[h], r=(), w=(), wacc=("wob",), q="pool")
        g_pre = cst["norm_mix_pre"][layer]
        g_post = cst["norm_mix_post"][layer]
        cwx = [cst["mb_cwx%d" % i][j] for i in range(4)]
        cwb = [cst["mb_cwb%d" % i][j] for i in range(4)]
        cbx, cbb = cst["mb_cbx"][j], cst["mb_cbb"][j]
        dtb, alog = cst["mb_dt_bias"][j], cst["mb_A_log"][j]
        Dbc, nwm = cst["mb_D"][j], cst["mb_norm_w"][j]
        sel128, selh, ident = cst["sel128"][0], cst["selh"][0], cst["ident"][0]
        negm, reset = cst["neg_mask"][0], cst["reset"][0]
        gsm = lambda nm: sb(nm, [8, TT], F32)
        one1 = sb("one1", [8, 1], F32)
        eps5 = sb("eps5", [64, 1], F32)
        negA = sb("negA", [8, 1], F32)
        ones64 = sb("ones64", [64, 64], BF16)
        XR = sb("XR", [64, 8, TT + 3], F32)
        BR = sb("BR", [128, 4, TT + 3], F32)
        acc = sb("acc", [64, 8, TT], F32)
        accb = sb("accb", [128, 4, TT], F32)
        XS = sb("XS", [64, 8, TT], F32)
        BCf = sb("BCf", [128, 4, TT], F32)
        BCb = sb("BCb", [128, 4, TT], BF16)
        SZ = sb("SZ", [64, 8, TT], F32)
        e_dt, dt_, dA, acum, ea, warg, wend = [gsm(n) for n in ("edt", "dt", "dA", "acum", "ea", "warg", "wend")]
        Ct = sb("Ct", [128, 8, TT], BF16)
        eaL = sb("eaL", [128, 8, NC_], F32)
        ACb = sb("ACb", [64, 8, TT], F32)
        sc = [sb("sc", [64, 24], F32) for _ in range(2)]
        xdtT = [sb("xdtT", [64, 8, 64], BF16) for _ in range(2)]
        xhT = [sb("xhT", [64, 8, 64], BF16) for _ in range(2)]
        BT = [sb("BT", [64, 2, 128], BF16) for _ in range(2)]
        CBs = [sb("CBs", [64, 2, 64], F32) for _ in range(2)]
        arg = [sb("arg", [64, 8, 64], F32) for _ in range(2)]
        G = [sb("G", [64, 8, 64], BF16) for _ in range(2)]
        STs = sb("STs", [128, 8, 64], F32)
        STb = sb("STb", [128, 8, 64], BF16)
        Y = sb("Y", [64, 8, TT], F32)
        rsg = [sb("rsg", [64, TT], F32) for _ in range(2)]
        yb = sb("yb", [64, 8, TT], BF16)
        yab = sb("yab", [128, 4, TT], BF16)
        c.memset(one1[:], 1.0, w=("one1",))
        c.memset(eps5[:], 1e-5, w=("eps5",))
        c.memset(ones64[:], 1.0, w=("ones64",))
        c.memset(XR[:], 0.0, w=("XR",))
        c.memset(BR[:], 0.0, w=("BR",))
        c.memset(STs[:], 0.0, w=("STs",))
        c.memset(STb[:], 0.0, w=("STb",))
        c.act(negA[:], alog[0:8, :], AF.Exp, r=(), w=("negA",))
        for it in range(NT):
            t0 = it * TT
            c.dma(yab[:], ya_d[0][:, :, t0:t0 + TT].rearrange("k p t -> p k t"),
                  r=tuple(("dram", ya_d[1], q) for q in range(t0 // 64, (t0 + TT) // 64)), w=("yab",))
            hn, thn = cm.get_hn(it, g_pre, x_in)
            for h2 in range(4):
                pz, px = cm.bank(), cm.bank()
                for i in range(2):
                    h = 2 * h2 + i
                    cs = slice(i * TT, (i + 1) * TT)
                    for k in range(NK):
                        c.mm(P[pz][0:64, cs], win[:, k, h * 64:(h + 1) * 64], hn[:, k, :], start=(k == 0), stop=(k == NK - 1),
                             r=("win", thn), w=(("ps", pz),))
                    for k in range(NK):
                        c.mm(P[px][0:64, cs], win[:, k, 512 + h * 64:512 + (h + 1) * 64], hn[:, k, :], start=(k == 0),
                             stop=(k == NK - 1), r=("win", thn), w=(("ps", px),))
                hs2 = slice(2 * h2, 2 * h2 + 2)
                c.act(SZ[:, hs2, :], P[pz][0:64, :].rearrange("p (h t) -> p h t", t=TT), AF.Silu, r=(("ps", pz),), w=("SZ",))
                c.cp(XR[:, hs2, 3:3 + TT], P[px][0:64, :].rearrange("p (h t) -> p h t", t=TT), r=(("ps", px),), w=("XR",), eng="act")
            for q in range(4):
                pb = cm.bank()
                for k in range(NK):
                    c.mm(P[pb][:, 0:TT], win[:, k, 1024 + q * 128:1024 + (q + 1) * 128], hn[:, k, :], start=(k == 0), stop=(k == NK - 1),
                         r=("win", thn), w=(("ps", pb),))
                c.cp(BR[:, q, 3:3 + TT], P[pb][:, 0:TT], r=(("ps", pb),), w=("BR",), eng="act")
            for h in range(8):
                c.ts(acc[:, h, :], XR[:, h, 3:3 + TT], cwx[3][0:64, h:h + 1], ALU.mult, r=("XR",), w=(("acc", h),),
                     s2=cbx[0:64, h:h + 1], op1=ALU.add, eng="pool")
                for i in range(3):
                    c.stt(acc[:, h, :], XR[:, h, i:i + TT], cwx[i][0:64, h:h + 1], acc[:, h, :], ALU.mult, ALU.add,
                          r=("XR", ("acc", h)), w=(("acc", h),))
            for q in range(4):
                c.ts(accb[:, q, :], BR[:, q, 3:3 + TT], cwb[3][:, q:q + 1], ALU.mult, r=("BR",), w=(("accb", q),),
                     s2=cbb[:, q:q + 1], op1=ALU.add, eng="pool")
                for i in range(3):
                    c.stt(accb[:, q, :], BR[:, q, i:i + TT], cwb[i][:, q:q + 1], accb[:, q, :], ALU.mult, ALU.add,
                          r=("BR", ("accb", q)), w=(("accb", q),))
            acc_t = tuple(("acc", h) for h in range(8))
            accb_t = tuple(("accb", q) for q in range(4))
            c.act(XS[:], acc[:], AF.Silu, r=acc_t, w=("XS",))
            c.act(BCf[:], accb[:], AF.Silu, r=accb_t, w=("BCf",))
            c.cp(BCb[:], BCf[:], r=("BCf",), w=("BCb",), eng="pool")
            c.cp(XR[:, :, 0:3], XR[:, :, TT:TT + 3], r=("XR",), w=("XR",), eng="pool")
            c.cp(BR[:, :, 0:3], BR[:, :, TT:TT + 3], r=("BR",), w=("BR",), eng="pool")
            if DBG_STOP <= 1:
                cm.store_x(it, x_out)
                continue
            if os.environ.get("MK_PFPOS", "0") == "1":
                cm.prefetch(it + 1, g_pre, x_in, NT)
            pd = cm.bank()
            for k in range(NK):
                c.mm(P[pd][0:8, 0:TT], win[:, k, 1536:1544], hn[:, k, :], start=(k == 0), stop=(k == NK - 1),
                     r=("win", thn), w=(("ps", pd),))
            c.act(e_dt[:], P[pd][0:8, 0:TT], AF.Exp, r=(("ps", pd),), w=("edt",), bias=dtb[0:8, :])
            c.act(dt_[:], e_dt[:], AF.Ln, r=("edt", "one1"), w=("dt",), bias=one1[:])
            c.ts(dA[:], dt_[:], negA[:, 0:1], ALU.mult, r=("dt", "negA"), w=("dA",), s2=-1.0, op1=ALU.mult)
            c.s.add("dve", lambda e: e.tensor_tensor_scan(acum[:], reset[0:8, 0:TT], dA[:], 0.0, ALU.mult, ALU.add),
                    r=("dA",), w=("acum",))
            c.act(ea[:], acum[:], AF.Exp, r=("acum",), w=("ea",))
            aL = acum[:].rearrange("p (c l) -> p c l", l=64)[:, :, 63:64].broadcast_to([8, NC_, 64])
            c.tt(warg[:].rearrange("p (c l) -> p c l", l=64), aL, acum[:].rearrange("p (c l) -> p c l", l=64), ALU.subtract,
                 r=("acum",), w=("warg",))
            c.act(wend[:], warg[:], AF.Exp, r=("warg",), w=("wend",))
            c.tt(wend[:], wend[:], dt_[:], ALU.mult, r=("wend", "dt"), w=("wend",))
            for h2 in range(4):
                pe_, pa_ = cm.bank(), cm.bank()
                for i in range(2):
                    h = 2 * h2 + i
                    cs = slice(i * TT, (i + 1) * TT)
                    c.mm(P[pe_][:, cs], sel128[0:8, h * 128:(h + 1) * 128], ea[:], start=True, stop=True, r=("ea",), w=(("ps", pe_),))
                    c.mm(P[pa_][0:64, cs], selh[0:8, h * 64:(h + 1) * 64], acum[:], start=True, stop=True, r=("acum",), w=(("ps", pa_),))
                hs2 = slice(2 * h2, 2 * h2 + 2)
                g = h2 // 2
                pe3 = P[pe_][:, :].rearrange("p (h t) -> p h t", t=TT)
                c.tt(Ct[:, hs2, :], pe3, BCf[:, 2 + g, :].unsqueeze(1).broadcast_to([128, 2, TT]), ALU.mult,
                     r=(("ps", pe_), "BCf"), w=("Ct",))
                c.cp(eaL[:, hs2, :], P[pe_][:, :].rearrange("p (h c l) -> p h c l", h=2, l=64)[:, :, :, 63], r=(("ps", pe_),),
                     w=("eaL",), eng="act")
                c.cp(ACb[:, hs2, :], P[pa_][0:64, :].rearrange("p (h t) -> p h t", t=TT), r=(("ps", pa_),), w=("ACb",), eng="act")
            if DBG_STOP <= 2:
                cm.store_x(it, x_out)
                continue
            if os.environ.get("MK_PFPOS", "0") == "0":
                cm.prefetch(it + 1, g_pre, x_in, NT)
            for jc in range(NC_):
                cols = slice(jc * 64, (jc + 1) * 64)
                jb = jc % 2
                pt = cm.bank()
                for qi, src in enumerate((acum, dt_, wend)):
                    c.tr(P[pt][0:64, qi * 8:(qi + 1) * 8], src[0:8, cols], ident[0:8, 0:8], r=("acum", "dt", "wend"), w=(("ps", pt),))
                c.cp(sc[jb][:], P[pt][0:64, 0:24], r=(("ps", pt),), w=(("sc", jb),), eng="act")
                px_ = cm.bank()
                for h in range(8):
                    c.tr(P[px_][0:64, h * 64:(h + 1) * 64], XS[:, h, cols], ident[0:64, 0:64], r=("XS",), w=(("ps", px_),))
                px3 = P[px_][0:64, :].rearrange("p (h d) -> p h d", d=64)
                c.tt(xdtT[jb][:], px3, sc[jb][:, 8:16].unsqueeze(2).broadcast_to([64, 8, 64]), ALU.mult,
                     r=(("ps", px_), ("sc", jb)), w=(("xdtT", jb),))
                c.tt(xhT[jb][:], px3, sc[jb][:, 16:24].unsqueeze(2).broadcast_to([64, 8, 64]), ALU.mult,
                     r=(("ps", px_), ("sc", jb)), w=(("xhT", jb),))
                pbt = cm.bank()
                for g in range(2):
                    c.tr(P[pbt][0:64, g * 128:(g + 1) * 128], BCf[:, g, cols], ident[:, :], r=("BCf",), w=(("ps", pbt),))
                c.cp(BT[jb][:], P[pbt][0:64, 0:256].rearrange("p (g n) -> p g n", n=128), r=(("ps", pbt),), w=(("BT", jb),), eng="act")
                pcb = cm.bank()
                for g in range(2):
                    c.mm(P[pcb][0:64, g * 64:(g + 1) * 64], BCb[:, g, cols], BCb[:, 2 + g, cols], start=True, stop=True,
                         r=("BCb",), w=(("ps", pcb),))
                c.cp(CBs[jb][:], P[pcb][0:64, 0:128].rearrange("p (g l) -> p g l", l=64), r=(("ps", pcb),), w=(("CBs", jb),), eng="act")
                c.tt(arg[jb][:], ACb[:, :, cols], sc[jb][:, 0:8].unsqueeze(2).broadcast_to([64, 8, 64]), ALU.subtract,
                     r=("ACb", ("sc", jb)), w=(("arg", jb),))
                c.tt(arg[jb][:], arg[jb][:], negm[0:64, :].unsqueeze(1).broadcast_to([64, 8, 64]), ALU.add,
                     r=(("arg", jb),), w=(("arg", jb),), eng="pool")
                c.act(arg[jb][:], arg[jb][:], AF.Exp, r=(("arg", jb),), w=(("arg", jb),))
                for g in range(2):
                    c.tt(G[jb][:, 4 * g:4 * g + 4, :], arg[jb][:, 4 * g:4 * g + 4, :],
                         CBs[jb][:, g, :].unsqueeze(1).broadcast_to([64, 4, 64]), ALU.mult,
                         r=(("arg", jb), ("CBs", jb)), w=(("G", jb),))
                py = cm.bank()
                for h in range(8):
                    hs = slice(h * 64, (h + 1) * 64)
                    c.mm(P[py][0:64, hs], STb[:, h, :], Ct[:, h, cols], start=True, stop=False, r=("STb", "Ct"), w=(("ps", py),))
                    c.mm(P[py][0:64, hs], xdtT[jb][:, h, :], G[jb][:, h, :], start=False, stop=True,
                         r=(("xdtT", jb), ("G", jb)), w=(("ps", py),))
                c.cp(Y[:, :, cols], P[py][0:64, :].rearrange("p (h d) -> p h d", d=64), r=(("ps", py),), w=("Y",), eng="act")
                pst = cm.bank()
                for h in range(8):
                    c.mm(P[pst][:, h * 64:(h + 1) * 64], BT[jb][:, h // 4, :], xhT[jb][:, h, :], start=True, stop=True,
                         r=(("BT", jb), ("xhT", jb)), w=(("ps", pst),))
                c.tt(STs[:], STs[:], eaL[:, :, jc:jc + 1].broadcast_to([128, 8, 64]), ALU.mult, r=("STs", "eaL"), w=("STs",))
                c.tt(STs[:], STs[:], P[pst][:, :].rearrange("p (h d) -> p h d", d=64), ALU.add, r=("STs", ("ps", pst)), w=("STs",))
                c.cp(STb[:], STs[:], r=("STs",), w=("STb",), eng="act")
            if DBG_STOP <= 3:
                cm.store_x(it, x_out)
                continue
            if os.environ.get("MK_PFPOS", "0") == "2":
                cm.prefetch(it + 1, g_pre, x_in, NT)
            c.tt(acc[:], XS[:], Dbc[0:64, :].unsqueeze(2).broadcast_to([64, 8, TT]), ALU.mult, r=("XS",) + acc_t, w=acc_t, eng="pool")
            c.tt(Y[:], Y[:], acc[:], ALU.add, r=("Y",) + acc_t, w=("Y",))
            c.tt(Y[:], Y[:], SZ[:], ALU.mult, r=("Y", "SZ"), w=("Y",))
            if DBG_STOP <= 3.3:
                cm.store_x(it, x_out)
                continue
            sq = cm.sq[0:64, :, :]
            c.act(sq, Y[:], AF.Square, r=("Y",), w=("sq",))
            for g in range(2):
                pb = cm.bank()
                for i in range(4):
                    c.mm(P[pb][0:64, 0:TT], ones64[:], sq[:, 4 * g + i, :], start=(i == 0), stop=(i == 3), r=("sq", "ones64"),
                         w=(("ps", pb),))
                c.act(rsg[g][:], P[pb][0:64, 0:TT], AF.Ln, r=(("ps", pb), "eps5"), w=(("rsg", g),), bias=eps5[:], scale=1.0 / 256.0)
                c.act(rsg[g][:], rsg[g][:], AF.Exp, r=(("rsg", g),), w=(("rsg", g),), scale=-0.5)
                for i in range(4):
                    h = 4 * g + i
                    c.stt(yb[:, h, :], Y[:, h, :], nwm[0:64, h:h + 1], rsg[g][:], ALU.mult, ALU.mult, r=("Y", ("rsg", g)), w=(("yb", h),))
            if DBG_STOP <= 3.6:
                cm.store_x(it, x_out)
                continue
            for o in range(NK):
                pb = cm.bank()
                VAR = os.environ.get("MK_VAR", "")
                for cc in range(4):
                    if VAR in ("B", "C"):
                        break
                    c.mm(P[pb][:, 0:TT], woa[:, cc, o * 128:(o + 1) * 128], yab[:, cc, :], start=(cc == 0), stop=(VAR == "A" and cc == 3),
                         r=("woa", "yab"), w=(("ps", pb),))
                for h in range(8):
                    if VAR in ("A", "C"):
                        break
                    c.mm(P[pb][:, 0:TT], wob[:, h, o * 128:(o + 1) * 128], yb[:, h, :], start=(VAR == "B" and h == 0), stop=(h == 7),
                         r=("wob", ("yb", h)), w=(("ps", pb),))
                if VAR == "C":
                    v3 = os.environ.get("MK_VAR3", "act")
                    if v3 != "none":
                        c.cp(cm.yy[:, o, :], cm.xs[it % 2][:, o, :], r=(("x", it % 2),), w=("yy",), eng=v3)
                else:
                    c.cp(cm.yy[:, o, :], P[pb][:, 0:TT], r=(("ps", pb),), w=("yy",), eng="dve")
            if VAR != "C" or os.environ.get("MK_VAR2", "") != "D":
                cm.post_residual(it, g_post)
            cm.store_x(it, x_out)
        c.s.barrier()


def phase_rwkv(c, layer, j, x_in, ya_out, w_in_d, w2_d, a2_d, g2_d, cst):
    nc = c.nc
    TT = 128
    NT = T // TT
    NC_ = TT // 64
    NG = 2
    GH = 8 // NG
    GMAIN = [(0, 2), (2, 2)]
    GPRE = [(4, 1), (5, 1)]
    GSH = (6, 1)
    with ExitStack() as es:
        cm = Common(c, es, TT, lite=True)
        sb = cm.sb
        P = c.psum
        win = sb("rwwin", [128, NK, 1792], BF16)
        w2b = sb("w2b", [64, 512], BF16)
        a2b = sb("a2b", [64, 512], BF16)
        g2b = sb("g2b", [128, 512], BF16)
        load_weight_bf16(c, win, w_in_d, D, 1792, "win", NK)
        c.dma(w2b[:], w2_d, r=(), w=("w2b",), q="pool")
        c.dma(a2b[:], a2_d, r=(), w=("a2b",), q="pool")
        c.dma(g2b[:], g2_d, r=(), w=("g2b",), q="pool")
        g_pre = cst["norm_mix_pre"][layer]
        C8 = lambda nm: cst[nm][j][0:64, :]
        mu = [C8("rw_mu_r"), C8("rw_mu_k"), C8("rw_mu_v")]
        mul = cst["rw_mu_l"][j]
        w0, a0, kkc, kac, rkc, lnw, lnb = [C8(n) for n in ("rw_w0", "rw_a0", "rw_k_k", "rw_k_a", "rw_r_k", "rw_ln_w", "rw_ln_b")]
        ident = cst["ident"][0]
        m_strict, m_incl, m_low = cst["mask_strict"][0], cst["mask_incl"][0], cst["mask_low"][0]
        reset = cst["reset"][0]
        f32t = lambda nm: sb(nm, [64, 8, TT], F32)
        b16t = lambda nm: sb(nm, [64, 8, TT], BF16)
        ones64 = sb("ones64", [64, 64], BF16)
        lneps = sb("lneps", [64, 1], F32)
        RAW = [sb("RAW", [64, 8, TT + 1], F32) for _ in range(3)]
        RAWL = sb("RAWL", [128, 3, TT + 1], F32)
        X0, X1 = f32t("X0"), f32t("X1")
        XL = sb("XL", [128, 3, TT], F32)
        twd = sb("twd", [64, TT], BF16)
        adb = sb("adb", [64, TT], BF16)
        sgd = sb("sgd", [128, TT], BF16)
        LW, CL, E_, tE, KK, Kmod, Bsc = [f32t(n) for n in ("LW", "CL", "E", "tE", "KK", "Kmod", "Bsc")]
        sqh = b16t("sqh")
        Xv2 = [f32t("Xv") for _ in range(2)]
        Bh2 = [f32t("Bh") for _ in range(2)]
        Kh2 = [f32t("Kh") for _ in range(2)]
        Epos2 = [f32t("Epos") for _ in range(2)]
        BON2 = [f32t("BON") for _ in range(2)]
        GG2 = [f32t("GG") for _ in range(2)]
        At2 = [b16t("At") for _ in range(2)]
        Rt2 = [b16t("Rt") for _ in range(2)]
        Bt2 = [b16t("Bt") for _ in range(2)]
        Kt2 = [b16t("Kt") for _ in range(2)]
        YY = f32t("YY")
        sqh2 = b16t("sqh2")
        NR2 = f32t("NR2")
        YA = sqh2
        m64 = lambda nm, dt=BF16: sb(nm, [64, 8, 64], dt)
        Aab, AabT, Aak, Arb, Ark = [m64(n) for n in ("Aab", "AabT", "Aak", "Arb", "Ark")]
        Tm = [m64("Tm") for _ in range(2)]
        Ak = [m64("Ak") for _ in range(2)]
        AkT = [m64("AkT") for _ in range(2)]
        VT, BhT, KhT, XTs, UTs = [m64(n) for n in ("VT", "BhT", "KhT", "XTs", "UTs")]
        S0 = m64("S0", F32)
        S0b = m64("S0b")
        allg = lambda nm: tuple((nm, g) for g in range(NG))
        c.memset(ones64[:], 1.0, w=("ones64",))
        c.memset(lneps[:], 64e-5, w=("lneps",))
        c.memset(S0[:], 0.0, w=allg("S0"))
        c.memset(S0b[:], 0.0, w=allg("S0b"))
        for q in range(3):
            c.memset(RAW[q][:], 0.0, w=tuple(("RAW", q, g) for g in range(NG)))
        c.memset(RAWL[:], 0.0, w=("RAWL",))
        ya_v = ya_out[0].rearrange("c (two p) t -> p (c two) t", two=2)
        V3 = lambda ap, d=64: ap.rearrange("p (h d) -> p h d", d=d)

        def shared(it):
            hn, thn = cm.get_hn(it, g_pre, x_in)
            pb = cm.bank(GSH)
            for (li, c0, mrows) in ((0, 1536, 64), (1, 1600, 64), (2, 1664, 128)):
                for k in range(NK):
                    c.mm(P[pb][0:mrows, li * TT:(li + 1) * TT], win[:, k, c0:c0 + mrows], hn[:, k, :], start=(k == 0), stop=(k == NK - 1),
                         r=("win", thn), w=(("ps", pb),))
            c.cp(RAWL[0:64, 0:2, 1:1 + TT], V3(P[pb][0:64, 0:2 * TT], TT), r=(("ps", pb),), w=("RAWL",), eng="dve")
            c.cp(RAWL[:, 2, 1:1 + TT], P[pb][:, 2 * TT:3 * TT], r=(("ps", pb),), w=("RAWL",), eng="dve")
            c.tt(XL[:], RAWL[:, :, 0:TT], RAWL[:, :, 1:1 + TT], ALU.subtract, r=("RAWL",), w=("XL",), eng="pool")
            c.tt(XL[:], XL[:], mul[:, :].unsqueeze(2).broadcast_to([128, 3, TT]), ALU.mult, r=("XL",), w=("XL",))
            c.tt(XL[:], XL[:], RAWL[:, :, 1:1 + TT], ALU.add, r=("XL", "RAWL"), w=("XL",))
            c.cp(RAWL[:, :, 0:1], RAWL[:, :, TT:TT + 1], r=("RAWL",), w=("RAWL",), eng="pool")
            c.act(twd[:], XL[0:64, 0, :], AF.Tanh, r=("XL",), w=("twd",))
            c.act(adb[:], XL[0:64, 1, :], AF.Copy, r=("XL",), w=("adb",), scale=1.0)
            c.act(sgd[:], XL[:, 2, :], AF.Sigmoid, r=("XL",), w=("sgd",))

        def pre(it, g):
            b = it % 2
            GB = GPRE[g]
            hsl = slice(g * GH, (g + 1) * GH)
            HTg = GH * TT
            Xv, Bh, Kh, Epos, BON, GG = [t_[b][:, hsl, :] for t_ in (Xv2, Bh2, Kh2, Epos2, BON2, GG2)]
            At, Rt, Bt, Kt = [t_[b][:, hsl, :] for t_ in (At2, Rt2, Bt2, Kt2)]
            tXv, tBh, tKh, tEpos, tBON, tGG = [(n, b, g) for n in ("Xv", "Bh", "Kh", "Epos", "BON", "GG")]
            tAt, tRt, tBt, tKt = [(n, b, g) for n in ("At", "Rt", "Bt", "Kt")]
            hn, thn = cm.hn[b], ("hn", b)
            Xr, Xk = X0[:, hsl, :], X1[:, hsl, :]
            tXr, tXk = ("X0", g), ("X1", g)
            Xs, tXs = [Xr, Xk, Xv], [tXr, tXk, tXv]
            LWg, CLg, Eg, tEg, KKg, Kmodg, Bscg, sqhg = [t_[:, hsl, :] for t_ in (LW, CL, E_, tE, KK, Kmod, Bsc, sqh)]
            tLW, tCL, tEb, ttE, tKK, tKmod, tBsc, tsqh = [(n, g) for n in ("LW", "CL", "E", "tE", "KK", "Kmod", "Bsc", "sqh")]
            bcg = lambda ap: ap[:, hsl].unsqueeze(2).broadcast_to([64, GH, TT])
            fl = lambda ap: ap.rearrange("p h t -> p (h t)")
            for q in range(3):
                pb = cm.bank(GB)
                for i in range(GH):
                    h = g * GH + i
                    for k in range(NK):
                        c.mm(P[pb][0:64, i * TT:(i + 1) * TT], win[:, k, q * 512 + h * 64:q * 512 + (h + 1) * 64], hn[:, k, :],
                             start=(k == 0), stop=(k == NK - 1), r=("win", thn), w=(("ps", pb),))
                c.act(RAW[q][:, hsl, 1:1 + TT], V3(P[pb][0:64, 0:HTg], TT), AF.Copy, r=(("ps", pb),), w=(("RAW", q, g),), scale=1.0)
            for q in range(3):
                R_ = RAW[q]
                c.tt(tEg, R_[:, hsl, 0:TT], R_[:, hsl, 1:1 + TT], ALU.subtract, r=(("RAW", q, g),), w=(ttE,), eng="pool")
                c.tt(tEg, tEg, bcg(mu[q]), ALU.mult, r=(ttE,), w=(ttE,))
                c.tt(Xs[q], tEg, R_[:, hsl, 1:1 + TT], ALU.add, r=(ttE, ("RAW", q, g)), w=(tXs[q],), eng="pool")
                c.cp(R_[:, hsl, 0:1], R_[:, hsl, TT:TT + 1], r=(("RAW", q, g),), w=(("RAW", q, g),), eng="pool")
            AA = tEg
            pw = cm.bank(GB)
            for i in range(GH):
                h = g * GH + i
                c.mm(P[pw][0:64, i * TT:(i + 1) * TT], w2b[:, h * 64:(h + 1) * 64], twd[:], start=True, stop=True, r=("w2b", "twd"), w=(("ps", pw),))
            for i in range(GH):
                h = g * GH + i
                c.act(LW[:, h, :], P[pw][0:64, i * TT:(i + 1) * TT], AF.Sigmoid, r=(("ps", pw),), w=(tLW,), bias=w0[:, h:h + 1])
            pa = cm.bank(GB)
            for i in range(GH):
                h = g * GH + i
                c.mm(P[pa][0:64, i * TT:(i + 1) * TT], a2b[:, h * 64:(h + 1) * 64], adb[:], start=True, stop=True, r=("a2b", "adb"), w=(("ps", pa),))
            for i in range(GH):
                h = g * GH + i
                c.act(tE[:, h, :], P[pa][0:64, i * TT:(i + 1) * TT], AF.Sigmoid, r=(("ps", pa),), w=(ttE,), bias=a0[:, h:h + 1])
            pg = cm.bank(GB)
            for i in range(GH):
                h = g * GH + i
                c.mm(P[pg][0:64, i * TT:(i + 1) * TT], g2b[:, h * 64:(h + 1) * 64], sgd[:], start=True, stop=True, r=("g2b", "sgd"), w=(("ps", pg),))
            c.cp(GG, V3(P[pg][0:64, 0:HTg], TT), r=(("ps", pg),), w=(tGG,), eng="dve")
            c.act(LWg, LWg, AF.Copy, r=(tLW,), w=(tLW,), scale=-0.6065306597126334)
            c.s.add("dve", lambda e: e.tensor_tensor_scan(fl(CLg), reset[0:64, 0:HTg], fl(LWg), 0.0, ALU.mult, ALU.add),
                    r=(tLW,), w=(tCL,), cost=2 * HTg / 0.96 + 150)
            c.act(Epos, CLg, AF.Exp, r=(tCL,), w=(tEpos,))
            NR = Eg
            c.tt(KKg, Xk, bcg(kkc), ALU.mult, r=(tXk,), w=(tKK,))
            c.act(sqhg, KKg, AF.Square, r=(tKK,), w=(tsqh,))
            pb = cm.bank(GB)
            for i in range(GH):
                h = g * GH + i
                c.mm(P[pb][0:64, i * TT:(i + 1) * TT], ones64[:], sqh[:, h, :], start=True, stop=True, r=(tsqh, "ones64"), w=(("ps", pb),))
            c.act(NR, V3(P[pb][0:64, 0:HTg], TT), AF.Sqrt, r=(("ps", pb),), w=(tEb,))
            c.act(c.dummy2[g], c.dummy, AF.Exp, r=(), w=(("dummy", g),))
            c.ts(NR, NR, 1e-12, ALU.max, r=(tEb,), w=(tEb,))
            c.recip(NR, NR, r=(tEb,), w=(tEb,))
            c.tt(KKg, KKg, NR, ALU.mult, r=(tKK, tEb), w=(tKK,))
            c.stt(Kmodg, AA, 1.0, bcg(kac), ALU.subtract, ALU.mult, r=(ttE,), w=(tKmod,))
            c.stt(Kmodg, Kmodg, 1.0, Xk, ALU.add, ALU.mult, r=(tKmod, tXk), w=(tKmod,))
            c.tt(Bscg, KKg, AA, ALU.mult, r=(tKK, ttE), w=(tBsc,), eng="pool")
            c.tt(Rt, Xr, Epos, ALU.mult, r=(tXr, tEpos), w=(tRt,), eng="pool")
            c.act(Eg, CLg, AF.Exp, r=(tCL, tEb), w=(tEb,), scale=-1.0)
            c.tt(Bt, Bscg, Eg, ALU.mult, r=(tBsc, tEb), w=(tBt,))
            c.tt(Kt, Kmodg, Eg, ALU.mult, r=(tKmod, tEb), w=(tKt,), eng="pool")
            c.tt(Eg, CLg, LWg, ALU.subtract, r=(tCL, tLW, tEb), w=(tEb,))
            c.act(Eg, Eg, AF.Exp, r=(tEb,), w=(tEb,))
            c.stt(At, KKg, -1.0, Eg, ALU.mult, ALU.mult, r=(tKK, tEb), w=(tAt,))
            CL4 = CLg.rearrange("p h (c l) -> p h c l", l=64)
            c.tt(Eg.rearrange("p h (c l) -> p h c l", l=64), CL4[:, :, :, 63:64].broadcast_to([64, GH, NC_, 64]), CL4, ALU.subtract,
                 r=(tCL, tEb), w=(tEb,), eng="pool")
            c.act(Eg, Eg, AF.Exp, r=(tEb,), w=(tEb,))
            c.tt(Bh, Bscg, Eg, ALU.mult, r=(tBsc, tEb), w=(tBh,))
            c.tt(Kh, Kmodg, Eg, ALU.mult, r=(tKmod, tEb), w=(tKh,), eng="pool")
            c.tt(tEg, Xr, Kmodg, ALU.mult, r=(tXr, tKmod, ttE), w=(ttE,), eng="pool")
            c.tt(sqhg, tEg, bcg(rkc), ALU.mult, r=(ttE,), w=(tsqh,))
            pb = cm.bank(GB)
            for i in range(GH):
                h = g * GH + i
                c.mm(P[pb][0:64, i * TT:(i + 1) * TT], ones64[:], sqh[:, h, :], start=True, stop=True, r=(tsqh, "ones64"), w=(("ps", pb),))
            c.tt(BON, V3(P[pb][0:64, 0:HTg], TT), Xv, ALU.mult, r=(("ps", pb), tXv), w=(tBON,))

        def main(it, g):
            b = it % 2
            t0 = it * TT
            GA = GMAIN[g]
            hsl = slice(g * GH, (g + 1) * GH)
            HTg = GH * TT
            W = GH * 64
            Xv, Bh, Kh, Epos, BON, GG = [t_[b] for t_ in (Xv2, Bh2, Kh2, Epos2, BON2, GG2)]
            At, Rt, Bt, Kt = [t_[b] for t_ in (At2, Rt2, Bt2, Kt2)]
            tXv, tBh, tKh, tEpos, tBON, tGG = [(n, b, g) for n in ("Xv", "Bh", "Kh", "Epos", "BON", "GG")]
            tAt, tRt, tBt, tKt = [(n, b, g) for n in ("At", "Rt", "Bt", "Kt")]
            T_ = lambda nm: (nm, g)
            bcm = lambda ap: ap[:, hsl].unsqueeze(2).broadcast_to([64, GH, TT])
            mk = lambda m_: m_[0:64, 0:64].unsqueeze(1).broadcast_to([64, GH, 64])
            hs_of = lambda i: slice(i * 64, (i + 1) * 64)
            for jc in range(NC_):
                cols = slice(jc * 64, (jc + 1) * 64)
                for (src, stok, dst, tok) in ((Xv, tXv, VT, "VT"), (Bh, tBh, BhT, "BhT"), (Kh, tKh, KhT, "KhT")):
                    pb = cm.bank(GA)
                    for i in range(GH):
                        c.tr(P[pb][0:64, hs_of(i)], src[:, g * GH + i, cols], ident[0:64, 0:64], r=(stok,), w=(("ps", pb),))
                    c.act(dst[:, hsl, :], V3(P[pb][0:64, 0:W]), AF.Copy, r=(("ps", pb),), w=(T_(tok),), scale=1.0)
                for (lt, ltok, rt, rtok, dst, tok, msk) in ((Bt, tBt, At, tAt, Aab, "Aab", m_strict), (At, tAt, Bt, tBt, AabT, "AabT", m_low),
                                                            (Kt, tKt, At, tAt, Aak, "Aak", m_strict),
                                                            (Bt, tBt, Rt, tRt, Arb, "Arb", m_incl), (Kt, tKt, Rt, tRt, Ark, "Ark", m_incl)):
                    pb = cm.bank(GA)
                    for i in range(GH):
                        h = g * GH + i
                        c.mm(P[pb][0:64, hs_of(i)], lt[:, h, cols], rt[:, h, cols], start=True, stop=True, r=(ltok, rtok), w=(("ps", pb),))
                    c.tt(dst[:, hsl, :], V3(P[pb][0:64, 0:W]), mk(msk), ALU.mult, r=(("ps", pb),), w=(T_(tok),))
                c.tt(Tm[0][:, hsl, :], Aab[:, hsl, :], mk(ident), ALU.add, r=(T_("Aab"),), w=(("Tm", 0, g),), eng="pool")
                A_c, AT_c, tokA, tokAT = Aab, AabT, T_("Aab"), T_("AabT")
                tcur = 0
                for lvl, kpow in enumerate((1, 2, 4, 8, 16, 32)):
                    nb = lvl % 2
                    if kpow >= 2:
                        pz = cm.bank(GA)
                        for i in range(GH):
                            h = g * GH + i
                            c.mm(P[pz][0:64, hs_of(i)], AT_c[:, h, :], Tm[tcur][:, h, :], start=True, stop=True,
                                 r=(tokAT, ("Tm", tcur, g)), w=(("ps", pz),))
                        c.tt(Tm[1 - tcur][:, hsl, :], V3(P[pz][0:64, 0:W]), Tm[tcur][:, hsl, :], ALU.add,
                             r=(("ps", pz), ("Tm", tcur, g)), w=(("Tm", 1 - tcur, g),))
                        tcur = 1 - tcur
                    if kpow <= 16:
                        py = cm.bank(GA)
                        for i in range(GH):
                            h = g * GH + i
                            c.mm(P[py][0:64, hs_of(i)], A_c[:, h, :], AT_c[:, h, :], start=True, stop=True, r=(tokA, tokAT), w=(("ps", py),))
                        if kpow <= 8:
                            px_ = cm.bank(GA)
                            for i in range(GH):
                                h = g * GH + i
                                c.mm(P[px_][0:64, hs_of(i)], AT_c[:, h, :], A_c[:, h, :], start=True, stop=True, r=(tokA, tokAT), w=(("ps", px_),))
                            c.act(Ak[nb][:, hsl, :], V3(P[px_][0:64, 0:W]), AF.Copy, r=(("ps", px_),), w=(("Ak", nb, g),), scale=1.0)
                        c.cp(AkT[nb][:, hsl, :], V3(P[py][0:64, 0:W]), r=(("ps", py),), w=(("AkT", nb, g),), eng="dve")
                        A_c, AT_c, tokA, tokAT = Ak[nb], AkT[nb], ("Ak", nb, g), ("AkT", nb, g)
                Tf, tokT = Tm[tcur], ("Tm", tcur, g)
                pb = cm.bank(GA)
                for i in range(GH):
                    h = g * GH + i
                    c.mm(P[pb][0:64, hs_of(i)], At[:, h, cols], S0b[:, h, :], start=True, stop=False, r=(tAt, T_("S0b")), w=(("ps", pb),))
                    c.mm(P[pb][0:64, hs_of(i)], Aak[:, h, :], VT[:, h, :], start=False, stop=True, r=(T_("Aak"), T_("VT")), w=(("ps", pb),))
                c.act(XTs[:, hsl, :], V3(P[pb][0:64, 0:W]), AF.Copy, r=(("ps", pb),), w=(T_("XTs"),), scale=1.0)
                pb = cm.bank(GA)
                for i in range(GH):
                    h = g * GH + i
                    c.mm(P[pb][0:64, hs_of(i)], Tf[:, h, :], XTs[:, h, :], start=True, stop=True, r=(tokT, T_("XTs")), w=(("ps", pb),))
                c.cp(UTs[:, hsl, :], V3(P[pb][0:64, 0:W]), r=(("ps", pb),), w=(T_("UTs"),), eng="dve")
                pb = cm.bank(GA)
                for i in range(GH):
                    h = g * GH + i
                    c.mm(P[pb][0:64, hs_of(i)], S0b[:, h, :], Rt[:, h, cols], start=True, stop=False, r=(T_("S0b"), tRt), w=(("ps", pb),))
                    c.mm(P[pb][0:64, hs_of(i)], UTs[:, h, :], Arb[:, h, :], start=False, stop=False, r=(T_("UTs"), T_("Arb")), w=(("ps", pb),))
                    c.mm(P[pb][0:64, hs_of(i)], VT[:, h, :], Ark[:, h, :], start=False, stop=True, r=(T_("VT"), T_("Ark")), w=(("ps", pb),))
                c.act(YY[:, hsl, cols], V3(P[pb][0:64, 0:W]), AF.Copy, r=(("ps", pb),), w=(T_("YY"),), scale=1.0)
                pb = cm.bank(GA)
                for i in range(GH):
                    h = g * GH + i
                    c.mm(P[pb][0:64, hs_of(i)], BhT[:, h, :], UTs[:, h, :], start=True, stop=False, r=(T_("BhT"), T_("UTs")), w=(("ps", pb),))
                    c.mm(P[pb][0:64, hs_of(i)], KhT[:, h, :], VT[:, h, :], start=False, stop=True, r=(T_("KhT"), T_("VT")), w=(("ps", pb),))
                c.tt(S0[:, hsl, :], S0[:, hsl, :], Epos[:, hsl, jc * 64 + 63:jc * 64 + 64].broadcast_to([64, GH, 64]), ALU.mult,
                     r=(T_("S0"), tEpos), w=(T_("S0"),))
                c.tt(S0[:, hsl, :], S0[:, hsl, :], V3(P[pb][0:64, 0:W]), ALU.add, r=(T_("S0"), ("ps", pb)), w=(T_("S0"),))
                c.act(S0b[:, hsl, :], S0[:, hsl, :], AF.Copy, r=(T_("S0"),), w=(T_("S0b"),), scale=1.0)
            YYg, sq2g, NR2g = YY[:, hsl, :], sqh2[:, hsl, :], NR2[:, hsl, :]
            c.act(sq2g, YYg, AF.Copy, r=(T_("YY"),), w=(T_("sqh2"),), scale=1.0)
            pb = cm.bank(GA)
            for i in range(GH):
                h = g * GH + i
                c.mm(P[pb][0:64, i * TT:(i + 1) * TT], ones64[:], sqh2[:, h, :], start=True, stop=True, r=(T_("sqh2"), "ones64"), w=(("ps", pb),))
            c.stt(YYg, V3(P[pb][0:64, 0:HTg], TT), -1.0 / 64.0, YYg, ALU.mult, ALU.add, r=(("ps", pb), T_("YY")), w=(T_("YY"),))
            c.act(sq2g, YYg, AF.Square, r=(T_("YY"),), w=(T_("sqh2"),))
            pb = cm.bank(GA)
            for i in range(GH):
                h = g * GH + i
                c.mm(P[pb][0:64, i * TT:(i + 1) * TT], ones64[:], sqh2[:, h, :], start=True, stop=True, r=(T_("sqh2"), "ones64"), w=(("ps", pb),))
            c.act(NR2g, V3(P[pb][0:64, 0:HTg], TT), AF.Ln, r=(("ps", pb), "lneps"), w=(T_("NR2"),), bias=lneps[:], scale=1.0 / 64.0)
            c.act(NR2g, NR2g, AF.Exp, r=(T_("NR2"),), w=(T_("NR2"),), scale=-0.5)
            c.tt(YYg, YYg, NR2g, ALU.mult, r=(T_("YY"), T_("NR2")), w=(T_("YY"),))
            c.tt(YYg, YYg, bcm(lnw), ALU.mult, r=(T_("YY"),), w=(T_("YY"),), eng="pool")
            c.tt(YYg, YYg, bcm(lnb), ALU.add, r=(T_("YY"),), w=(T_("YY"),), eng="pool")
            c.tt(YYg, YYg, BON[:, hsl, :], ALU.add, r=(T_("YY"), tBON), w=(T_("YY"),))
            c.tt(sq2g, YYg, GG[:, hsl, :], ALU.mult, r=(T_("YY"), tGG), w=(T_("sqh2"),))
            c.dma(ya_v[:, hsl, t0:t0 + TT], sq2g, r=(T_("sqh2"),), w=(),
                  wacc=tuple(("dram", ya_out[1], q) for q in range(t0 // 64, (t0 + TT) // 64)))

        S = c.s
        shared(0)
        cm.prefetch(1, g_pre, x_in, NT)
        for g in range(NG):
            pre(0, g)
        noil = os.environ.get("MK_NOIL", "0") == "1"
        pr = float(os.environ.get("MK_PRIO", "-300"))
        for it in range(NT):
            if it + 1 < NT:
                shared(it + 1)
                cm.prefetch(it + 2, g_pre, x_in, NT)
            thr = [S.capture(main, it, g) for g in range(NG)]
            if it + 1 < NT:
                thr += [S.capture(pre, it + 1, g) for g in range(NG)]
            if noil:
                for t_ in thr:
                    S.merge(t_, [])
            else:
                S.merge_n(thr, prio=[0.0, 0.0, pr, pr][:len(thr)])
        c.s.barrier()


def pack_consts(inputs):
    cols = []
    index = {}

    def add(name, arr2d):
        off = sum(a.shape[1] for a in cols)
        cols.append(np.ascontiguousarray(arr2d, dtype=np.float32))
        index[name] = (off, arr2d.shape[1])

    for nm in ("norm_mix_pre", "norm_mix_post", "norm_xa_pre", "norm_xa_post", "norm_mem", "norm_ff_pre", "norm_ff_post"):
        a = inputs[nm]
        for l in range(a.shape[0]):
            add((nm, l), a[l].reshape(NK, 128).T)
    a = inputs["ml_norm_w"]
    for l in range(a.shape[0]):
        add(("ml_norm_w", l), a[l].reshape(8, 128).T)
    a = inputs["ml_b_gates"]
    for l in range(a.shape[0]):
        z = np.zeros((128, 1), np.float32)
        z[0:8, 0] = a[l][0:8]
        add(("ml_b_i", l), z)
        z = np.zeros((128, 1), np.float32)
        z[0:8, 0] = a[l][8:16]
        add(("ml_b_f", l), z)
    def hl(v512):
        z = np.zeros((128, 8), np.float32)
        z[0:64] = np.asarray(v512).reshape(8, 64).T
        return z
    for l in range(inputs["rw_mu"].shape[0]):
        mu_ = inputs["rw_mu"][l]
        add(("rw_mu_r", l), hl(mu_[0:512]))
        add(("rw_mu_k", l), hl(mu_[512:1024]))
        add(("rw_mu_v", l), hl(mu_[1024:1536]))
        z = np.zeros((128, 3), np.float32)
        z[0:64, 0] = mu_[1536:1600]
        z[0:64, 1] = mu_[1600:1664]
        z[:, 2] = mu_[1664:1792]
        add(("rw_mu_l", l), z)
        for nm in ("rw_w0", "rw_a0", "rw_k_k", "rw_k_a", "rw_ln_w", "rw_ln_b"):
            add((nm, l), hl(inputs[nm][l]))
        add(("rw_r_k", l), hl(inputs["rw_r_k"][l].reshape(512)))
    for l in range(inputs["mb_conv_w"].shape[0]):
        cw = inputs["mb_conv_w"][l]
        cb = inputs["mb_conv_b"][l]
        for i in range(4):
            z = np.zeros((128, 8), np.float32)
            z[0:64] = cw[i, 0:512].reshape(8, 64).T
            add(("mb_cwx%d" % i, l), z)
            add(("mb_cwb%d" % i, l), cw[i, 512:1024].reshape(4, 128).T)
        z = np.zeros((128, 8), np.float32)
        z[0:64] = cb[0:512].reshape(8, 64).T
        add(("mb_cbx", l), z)
        add(("mb_cbb", l), cb[512:1024].reshape(4, 128).T)
        for nm in ("mb_dt_bias", "mb_A_log"):
            z = np.zeros((128, 1), np.float32)
            z[0:8, 0] = inputs[nm][l]
            add((nm, l), z)
        add(("mb_D", l), np.broadcast_to(inputs["mb_D"][l][None, :], (128, 8)))
        z = np.zeros((128, 8), np.float32)
        z[0:64] = inputs["mb_norm_w"][l].reshape(8, 64).T
        add(("mb_norm_w", l), z)
    p = np.arange(128)[:, None]
    t = np.arange(64)[None, :]
    add(("neg_mask", 0), np.where((p % 64) <= t, 0.0, -30000.0).astype(np.float32))
    sel128 = np.zeros((128, 8 * 128), np.float32)
    for h in range(8):
        sel128[h, h * 128:(h + 1) * 128] = 1.0
    add(("sel128", 0), sel128)
    add(("mask_incl", 0), ((p % 64) <= t).astype(np.float32))
    add(("mask_strict", 0), ((p % 64) < t).astype(np.float32))
    add(("mask_low", 0), ((p % 64) > t).astype(np.float32))
    add(("ident", 0), np.eye(128, dtype=np.float32))
    selm = np.zeros((128, 4 * 128), np.float32)
    for h in range(8):
        hp = h // 2
        selm[h, hp * 128 + (h % 2) * 64: hp * 128 + (h % 2) * 64 + 64] = 1.0
    selh = np.zeros((128, 8 * 64), np.float32)
    for h in range(8):
        selh[h, h * 64:(h + 1) * 64] = 1.0
    add(("selh", 0), selh)
    rst = np.ones((128, 1024), np.float32)
    rst[:, ::64] = 0.0
    add(("reset", 0), rst)
    return np.concatenate(cols, axis=1), index


def build(phases, consts_np, cindex):
    nc = bass.Bass("TRN2", target_bir_lowering=False)
    c = Ctx(nc)
    dr = {}

    def din(name, shape):
        dr[name] = nc.dram_tensor(name, list(shape), F32, kind="ExternalInput")
        return dr[name]

    ncst = consts_np.shape[1]
    x0 = din("xT", [NK, 128, T])
    cst_d = din("consts", [128, ncst])
    out = nc.dram_tensor("outT", [NK, 128, T], F32, kind="ExternalOutput")
    with ExitStack() as es:
        cst_sb = es.enter_context(nc.sbuf_tensor("cst", [128, ncst], F32))
        ones = es.enter_context(nc.sbuf_tensor("ones", [128, 128], BF16))
        epst = es.enter_context(nc.sbuf_tensor("epst", [128, 1], F32))
        c.psum = [es.enter_context(nc.psum_tensor("ps%d" % i, [128, 512], F32)) for i in range(8)]
        c.ones = ones[:]
        c.eps_ap = epst[:]
        dmy = es.enter_context(nc.sbuf_tensor("dmy", [128, 1], F32))
        c.dummy = dmy[:]
        c.memset(dmy[:], 0.0, w=("dummy",))
        dmy2 = es.enter_context(nc.sbuf_tensor("dmy2", [128, 4], F32))
        c.dummy2 = [dmy2[:, i:i + 1] for i in range(4)]
        c.memset(dmy2[:], 0.0, w=tuple(("dummy", i) for i in range(4)))
        c.dma(cst_sb[:], cst_d.ap(), r=(), w=("cst",))
        c.memset(ones[:], 1.0, w=("ones",))
        c.memset(epst[:], EPS, w=("eps",))
        c.s.barrier()
        cst = {}
        for (nm, l), (off, n) in cindex.items():
            cst.setdefault(nm, {})[l] = cst_sb[:, off:off + n]
        cur = (x0.ap(), "xT")
        for pi_, ph in enumerate(phases):
            last = pi_ == len(phases) - 1
            if last:
                dst = (out.ap(), "outT")
            elif ph[0] == "rwkv":
                dst = None
            else:
                dst = (nc.dram_tensor("scr%d" % pi_, [NK, 128, T], F32, kind="Internal").ap(), "scr%d" % pi_)
            kind = ph[0]
            if kind == "mlp":
                l = ph[1]
                wu = din("ff_up%d" % l, [D, DFF])
                wdn = din("ff_down%d" % l, [DFF, D])
                phase_mlp(c, l, cur, dst, wu.ap(), wdn.ap(), cst)
            elif kind == "rwkv":
                l = ph[1]
                wi = din("ev_w_in_rw", [D, 1792])
                w2_ = din("rw_w2", [64, 512])
                a2_ = din("rw_a2", [64, 512])
                g2_ = din("rw_g2", [128, 512])
                if len(ph) > 2 and ph[2] == "out":
                    ya_t = (nc.dram_tensor("ya_out", [4, 128, T], BF16, kind="ExternalOutput").ap(), "ya_out")
                else:
                    ya_t = (nc.dram_tensor("ya_scr%d" % l, [4, 128, T], BF16, kind="Internal").ap(), "ya_scr%d" % l)
                    dr["ya_scr"] = ya_t
                phase_rwkv(c, l, l // 2, cur, ya_t, wi.ap(), w2_.ap(), a2_.ap(), g2_.ap(), cst)
                dst = cur
            elif kind == "mamba":
                l = ph[1]
                ya_t = dr.get("ya_scr")
                if ya_t is None:
                    ya_t = (nc.dram_tensor("ya_in", [4, 128, T], BF16, kind="ExternalInput").ap(), "ya_in")
                wi = din("ev_w_in_mb", [D, 1544])
                woa_ = din("ev_w_out_a", [512, D])
                wob_ = din("ev_w_out_b", [8, 64, D])
                phase_mamba(c, l, l // 2, cur, dst, ya_t, wi.ap(), woa_.ap(), wob_.ap(), cst)
            elif kind == "mlstm":
                l = ph[1]
                wi = din("ml_w_in", [D, 3088])
                wo_ = din("ml_w_out", [D, D])
                phase_mlstm(c, l, l // 2, cur, dst, wi.ap(), wo_.ap(), cst)
            elif kind == "xattn":
                l = ph[1]
                if "memT" not in dr:
                    din("memT", [NK, 128, NMEM])
                ws = [din("xa_w%s%d" % (nm, l), [D, D]).ap() for nm in "qkvo"]
                phase_xattn(c, l, cur, dst, dr["memT"].ap(), ws[0], ws[1], ws[2], ws[3], cst)
            cur = dst
        fin = [("dram", "outT", it) for it in range(64)] + [("dram", "ya_out", it) for it in range(64)]
        c.s.add("sp", lambda e: e.nop(), r=fin)
        c.s.emit(nc, es)
    return nc, c


def phase_inputs(phases, inputs, b):
    m = {}
    for ph in phases:
        if ph[0] == "mlp":
            l = ph[1]
            m["ff_up%d" % l] = np.ascontiguousarray(inputs["ff_up"][l])
            m["ff_down%d" % l] = np.ascontiguousarray(inputs["ff_down"][l])
        elif ph[0] == "rwkv":
            jj = ph[1] // 2
            m["ev_w_in_rw"] = np.ascontiguousarray(inputs["ev_w_in"][jj][:, 0:1792])
            m["rw_w2"] = np.ascontiguousarray(inputs["rw_w2"][jj])
            m["rw_a2"] = np.ascontiguousarray(inputs["rw_a2"][jj])
            m["rw_g2"] = np.ascontiguousarray(inputs["rw_g2"][jj])
        elif ph[0] == "mamba":
            jj = ph[1] // 2
            m["ev_w_in_mb"] = np.ascontiguousarray(inputs["ev_w_in"][jj][:, 1792:3336])
            m["ev_w_out_a"] = np.ascontiguousarray(inputs["ev_w_out"][jj][0:512])
            m["ev_w_out_b"] = np.ascontiguousarray(inputs["ev_w_out"][jj][512:1024].reshape(8, 64, D))
        elif ph[0] == "mlstm":
            jj = ph[1] // 2
            m["ml_w_in"] = np.ascontiguousarray(inputs["ml_w_in"][jj])
            m["ml_w_out"] = np.ascontiguousarray(inputs["ml_w_out"][jj])
        elif ph[0] == "xattn":
            l = ph[1]
            m["memT"] = np.ascontiguousarray(inputs["mem"][b].T.reshape(NK, 128, NMEM))
            xa = {"q": inputs["xa_wq"], "k": inputs["xa_wk"], "v": inputs["xa_wv"], "o": inputs["xa_wo"]}
            for nm in "qkvo":
                m["xa_w%s%d" % (nm, l)] = np.ascontiguousarray(xa[nm][l])
    return m


FULL_PHASES = [("rwkv", 0), ("mamba", 0), ("xattn", 0), ("mlp", 0), ("mlstm", 1), ("xattn", 1), ("mlp", 1)]
_CACHE = {}


def kernel(**inputs):
    inputs = {k: np.asarray(v) for k, v in inputs.items()}
    consts, cindex = pack_consts(inputs)
    key = consts.shape
    if key not in _CACHE:
        _CACHE[key] = build(FULL_PHASES, consts, cindex)[0]
    nc = _CACHE[key]
    B = inputs["x"].shape[0]
    shared = phase_inputs(FULL_PHASES, inputs, 0)
    in_maps = []
    for b in range(B):
        m = dict(shared)
        m["xT"] = np.ascontiguousarray(inputs["x"][b].T.reshape(NK, 128, T))
        m["memT"] = np.ascontiguousarray(inputs["mem"][b].T.reshape(NK, 128, NMEM))
        m["consts"] = consts
        in_maps.append(m)
    res = run_bass_kernel_spmd(nc, in_maps, core_ids=list(range(B)))
    out = np.empty((B, T, D), np.float32)
    for b in range(B):
        out[b] = np.asarray(res.results[b]["outT"]).reshape(D, T).T
    return out
```

```python
import os
import numpy as np
from contextlib import ExitStack
import concourse.bass as bass
import concourse.mybir as mybir
from concourse.alu_op_type import AluOpType as ALU
from concourse.bass_utils import run_bass_kernel_spmd

F32 = mybir.dt.float32
BF16 = mybir.dt.bfloat16
AF = mybir.ActivationFunctionType

D = 1024
T = 4096
NK = 8
NMEM = 256
DFF = 4096
EPS = 1e-6
N_DSEM = 12
DBG_STOP = float(os.environ.get('MK_STOP', '99'))
PE_SWITCH_NS = float(os.environ.get('MK_PESW', '400'))


class Op:
    __slots__ = ("eng", "fn", "deps", "dma", "sig", "waits", "signal", "clock")

    def __init__(self, eng, fn, deps, dma):
        self.eng = eng
        self.fn = fn
        self.deps = deps
        self.dma = dma
        self.sig = None
        self.waits = None
        self.signal = False
        self.clock = None


class Sched:
    ENGS = ("pe", "act", "dve", "pool", "sp")

    def __init__(self):
        self.ops = []
        self.res = {}
        self.pending = {e: set() for e in self.ENGS}
        self.last_op = {}
        self.dmas = []
        self._cap = None
        self.fin = []
        self.eng_free = {}
        self.pe_mode = None

    def add(self, eng, fn, r=(), w=(), dma=False, wacc=(), cost=500.0, mode=None):
        if self._cap is not None:
            self._cap.append((eng, fn, tuple(r), tuple(w), dma, tuple(wacc), cost, mode))
            return None
        idx = len(self.ops)
        deps = self.pending[eng]
        if deps:
            self.pending[eng] = set()
        else:
            deps = set()
        res = self.res
        for t in r:
            st = res.get(t)
            if st is None:
                st = res[t] = [None, []]
            if st[0] is not None:
                if isinstance(st[0], list):
                    deps.update(st[0])
                else:
                    deps.add(st[0])
            st[1].append(idx)
        for t in wacc:
            st = res.get(t)
            if st is None:
                st = res[t] = [None, []]
            deps.update(st[1])
            if isinstance(st[0], list):
                st[0].append(idx)
            else:
                if st[0] is not None:
                    deps.add(st[0])
                st[0] = [idx]
            st[1] = []
        for t in w:
            st = res.get(t)
            if st is None:
                st = res[t] = [None, []]
            if st[0] is not None:
                if isinstance(st[0], list):
                    deps.update(st[0])
                else:
                    deps.add(st[0])
            if st[1]:
                ops = self.ops
                lastc = {}
                for ri in st[1]:
                    rop = ops[ri] if ri < idx else None
                    if rop is None:
                        continue
                    if rop.dma:
                        deps.add(ri)
                    else:
                        lastc[rop.eng] = ri
                deps.update(lastc.values())
            st[0] = idx
            st[1] = []
        deps.discard(idx)
        self.ops.append(Op(eng, fn, deps, dma))
        fin = self.fin
        ready = 0.0
        for d in deps:
            fd = fin[d]
            if fd > ready:
                ready = fd
        if dma:
            fin.append(max(ready + 100.0, self.eng_free.get(eng, 0.0)) + cost)
            self.eng_free[eng] = max(ready, self.eng_free.get(eng, 0.0)) + 60.0
            self.dmas.append(idx)
        else:
            st_ = max(ready + 120.0, self.eng_free.get(eng, 0.0))
            if mode is not None:
                if mode != self.pe_mode:
                    st_ += PE_SWITCH_NS
                self.pe_mode = mode
            fin.append(st_ + cost)
            self.eng_free[eng] = st_ + cost
            self.last_op[eng] = idx
        return idx

    def peek_start(self, a):
        eng, _, r, w, dma, wacc, cost, mode = a
        res, fin = self.res, self.fin
        ready = 0.0
        for t in r:
            st = res.get(t)
            if st is not None and st[0] is not None:
                for d in (st[0] if isinstance(st[0], list) else (st[0],)):
                    if fin[d] > ready:
                        ready = fin[d]
        for t in tuple(w) + tuple(wacc):
            st = res.get(t)
            if st is not None:
                if st[0] is not None:
                    for d in (st[0] if isinstance(st[0], list) else (st[0],)):
                        if fin[d] > ready:
                            ready = fin[d]
                for d in st[1]:
                    if fin[d] > ready:
                        ready = fin[d]
        pen = PE_SWITCH_NS if (mode is not None and mode != self.pe_mode) else 0.0
        return max(ready + 120.0, self.eng_free.get(eng, 0.0)) + pen

    def merge_n(self, threads, prio=None):
        pos = [0] * len(threads)
        prio = prio or [0.0] * len(threads)
        while True:
            best, bi = None, -1
            for i, th in enumerate(threads):
                if pos[i] < len(th):
                    st = self.peek_start(th[pos[i]]) + prio[i]
                    if best is None or st < best:
                        best, bi = st, i
            if bi < 0:
                break
            a = threads[bi][pos[bi]]
            self.add(a[0], a[1], r=a[2], w=a[3], dma=a[4], wacc=a[5], cost=a[6], mode=a[7])
            pos[bi] += 1

    def merge_greedy(self, A, B, bias=0.0):
        ia = ib = 0
        while ia < len(A) or ib < len(B):
            if ib >= len(B):
                pick = 0
            elif ia >= len(A):
                pick = 1
            else:
                sa = self.peek_start(A[ia])
                sb_ = self.peek_start(B[ib])
                pick = 0 if sa <= sb_ + bias else 1
            a = A[ia] if pick == 0 else B[ib]
            self.add(a[0], a[1], r=a[2], w=a[3], dma=a[4], wacc=a[5], cost=a[6], mode=a[7])
            if pick == 0:
                ia += 1
            else:
                ib += 1

    def capture(self, fn, *args):
        prev = self._cap
        self._cap = []
        fn(*args)
        caps, self._cap = self._cap, prev
        return caps

    def merge(self, A, B):
        nb = 0
        for i, a in enumerate(A):
            self.add(*a[:2], r=a[2], w=a[3], dma=a[4], wacc=a[5], cost=a[6], mode=a[7])
            tgt = ((i + 1) * len(B)) // max(1, len(A))
            while nb < tgt:
                b = B[nb]
                self.add(*b[:2], r=b[2], w=b[3], dma=b[4], wacc=b[5], cost=b[6], mode=b[7])
                nb += 1
        while nb < len(B):
            b = B[nb]
            self.add(*b[:2], r=b[2], w=b[3], dma=b[4], wacc=b[5], cost=b[6], mode=b[7])
            nb += 1

    def barrier(self):
        s = set(self.last_op.values()) | set(self.dmas)
        self.dmas = []
        for e in self.ENGS:
            self.pending[e] |= s

    def emit(self, nc, es):
        ops = self.ops
        n = len(ops)
        for op in ops:
            for d in op.deps:
                dop = ops[d]
                if dop.eng == "pe" and op.eng == "pe" and not dop.dma:
                    continue
                dop.signal = True
        sems = {}
        for e in ("pe", "act", "dve", "pool"):
            sems[e] = es.enter_context(nc.semaphore("s_" + e))
        for q in ("sp", "pool", "act"):
            for i in range(N_DSEM):
                sems[("d", q, i)] = es.enter_context(nc.semaphore("d_%s_%d" % (q, i)))
        cnt = {e: 0 for e in self.ENGS}
        dcnt = {e: 0 for e in self.ENGS}
        know = {e: {} for e in self.ENGS}
        nwaits = 0
        for op in ops:
            K = know[op.eng]
            waits = {}
            if op.dma:
                k = dcnt[op.eng]
                dcnt[op.eng] += 1
                key = ("d", op.eng, k % N_DSEM)
                op.sig = (key, 16 * (k // N_DSEM + 1))
                op.signal = True
                if k >= N_DSEM and K.get(key, 0) < 16 * (k // N_DSEM):
                    waits[key] = 16 * (k // N_DSEM)
                    K[key] = 16 * (k // N_DSEM)
            elif op.signal:
                cnt[op.eng] += 1
                op.sig = (op.eng, cnt[op.eng])
            for d in sorted(op.deps):
                dop = ops[d]
                if dop.eng == "pe" and op.eng == "pe" and not dop.dma:
                    continue
                key, val = dop.sig
                if K.get(key, 0) >= val:
                    continue
                if waits.get(key, 0) < val:
                    waits[key] = val
                for kk, vv in dop.clock.items():
                    if K.get(kk, 0) < vv:
                        K[kk] = vv
            op.waits = list(waits.items())
            nwaits += len(op.waits)
            if op.signal:
                c = dict(K)
                c[op.sig[0]] = op.sig[1]
                op.clock = c
        self.stats = dict(n_ops=n, n_waits=nwaits, cnt=dict(cnt), dcnt=dict(dcnt))
        self.check()
        engmap = {"pe": "tensor", "act": "scalar", "dve": "vector", "pool": "gpsimd", "sp": "sync"}
        with nc.Block() as block:
            for e in self.ENGS:
                mine = [op for op in ops if op.eng == e]
                if not mine:
                    continue

                def body(eng, mine=mine):
                    for op in mine:
                        for key, val in op.waits:
                            eng.wait_ge(sems[key], val)
                        ins = op.fn(eng)
                        if op.signal:
                            ins.then_inc(sems[op.sig[0]], 16 if op.dma else 1)

                getattr(block, engmap[e])(body)


def _sched_check(self):
    per = {e: [op for op in self.ops if op.eng == e] for e in self.ENGS}
    ptr = {e: 0 for e in self.ENGS}
    val = {}
    progress = True
    while progress:
        progress = False
        for e in self.ENGS:
            while ptr[e] < len(per[e]):
                op = per[e][ptr[e]]
                if all(val.get(k, 0) >= v for k, v in op.waits):
                    if op.signal:
                        val[op.sig[0]] = val.get(op.sig[0], 0) + (16 if op.dma else 1)
                        assert val[op.sig[0]] == op.sig[1], (op.sig, val[op.sig[0]])
                    ptr[e] += 1
                    progress = True
                else:
                    break
    stuck = {e: (ptr[e], len(per[e])) for e in self.ENGS if ptr[e] < len(per[e])}
    assert not stuck, "sync deadlock: %s" % stuck


Sched.check = _sched_check


class Ctx:
    def __init__(self, nc):
        self.nc = nc
        self.s = Sched()
        self.uid = 0

    def name(self, base):
        self.uid += 1
        return "%s_%d" % (base, self.uid)

    @staticmethod
    def _n(ap):
        n = 1
        for d in ap.shape[1:]:
            n *= int(d)
        return n

    def _cost(self, eng, ap):
        n = self._n(ap)
        if eng == "dve":
            return n / 0.96 + 150.0
        if eng == "act":
            return n / 1.2 + 250.0
        if eng == "pool":
            return n * 2.4 + 200.0
        return 500.0

    def mm(self, out, lhsT, rhs, start, stop, r, w, **kw):
        n = max(64, self._n(rhs))
        cost = (n / 2.4 + 12.0) * (4.0 if rhs.dtype == F32 else 1.0)
        rnd = lambda v: 32 if v <= 32 else (64 if v <= 64 else 128)
        mode = (rnd(int(lhsT.shape[0])), rnd(self._n(lhsT)), rhs.dtype == F32)
        self.s.add("pe", lambda e: e.matmul(out, lhsT, rhs, start=start, stop=stop, **kw), r=r, w=w, cost=cost, mode=mode)

    def tr(self, out, in_, ident, r, w):
        rnd = lambda v: 32 if v <= 32 else (64 if v <= 64 else 128)
        mode = ("T", rnd(int(in_.shape[0])), rnd(self._n(in_)), in_.dtype == F32)
        self.s.add("pe", lambda e: e.transpose(out, in_, ident), r=r, w=w, cost=70.0, mode=mode)

    def act(self, out, in_, func, r, w, bias=None, scale=None, accum_out=None, eng="act"):
        kw = {}
        if bias is not None:
            kw["bias"] = bias
        if scale is not None:
            kw["scale"] = scale
        if accum_out is not None:
            kw["accum_out"] = accum_out
        self.s.add(eng, lambda e: e.activation(out, in_, func, **kw), r=r, w=w, cost=self._cost(eng, out))

    def tt(self, out, in0, in1, op, r, w, eng="dve"):
        self.s.add(eng, lambda e: e.tensor_tensor(out, in0, in1, op), r=r, w=w, cost=self._cost(eng, out))

    def ts(self, out, in0, s1, op0, r, w, s2=None, op1=None, eng="dve"):
        if op1 is None:
            self.s.add(eng, lambda e: e.tensor_scalar(out, in0, s1, None, op0), r=r, w=w, cost=self._cost(eng, out))
        else:
            self.s.add(eng, lambda e: e.tensor_scalar(out, in0, s1, s2, op0, op1), r=r, w=w, cost=self._cost(eng, out))

    def stt(self, out, in0, scalar, in1, op0, op1, r, w):
        self.s.add("dve", lambda e: e.scalar_tensor_tensor(out, in0, scalar, in1, op0, op1), r=r, w=w, cost=self._cost("dve", out))

    def cp(self, out, in_, r, w, eng="dve"):
        if eng == "act":
            self.s.add("act", lambda e: e.copy(out, in_), r=r, w=w, cost=self._cost("act", out))
        else:
            self.s.add(eng, lambda e: e.tensor_copy(out, in_), r=r, w=w, cost=self._cost(eng, out))

    def recip(self, out, in_, r, w):
        self.s.add("dve", lambda e: e.reciprocal(out, in_), r=r, w=w, cost=self._cost("dve", out))

    def memset(self, ap, val, w, eng="pool"):
        self.s.add(eng, lambda e: e.memset(ap, val), w=w)

    def dma(self, out, in_, r, w, q="sp", wacc=(), **kw):
        self.s.add(q, lambda e: e.dma_start(out=out, in_=in_, **kw), r=r, w=w, dma=True, wacc=wacc,
                   cost=2500.0 + self._n(out) * 128 * 4 / 150.0)


def load_weight_bf16(c, dst, src, rows_tok, cols, wtok, kchunks, tokfn=None):
    srcv = src.rearrange("(k p) n -> p k n", p=128)
    step = 2048
    for k in range(kchunks):
        for c0 in range(0, cols, step):
            c1 = min(cols, c0 + step)
            c.dma(dst[:, k, c0:c1], srcv[:, k, c0:c1], r=(), w=(), wacc=(wtok if tokfn is None else tokfn(k, c0),), q="pool")


class Common:
    def __init__(self, c, es, TT, lite=False):
        nc = c.nc
        self.c = c
        self.TT = TT
        sb = lambda name, shape, dt: es.enter_context(nc.sbuf_tensor(c.name(name), shape, dt))
        self.sb = sb
        self.xs = [sb("x", [128, NK, TT], F32) for _ in range(2)]
        self.hn = [sb("hn", [128, NK, TT], BF16) for _ in range(2)]
        self.sq = sb("sq", [128, NK, TT], BF16)
        self.rstd = [sb("rstd", [128, TT], F32) for _ in range(2)]
        if not lite:
            self.tmp = [sb("tmp", [128, TT], F32) for _ in range(2)]
            self.yy = sb("yy", [128, NK, TT], F32)
        self.nrm = 0
        self.pb = 0
        self.gpb = {}

    def bank(self, group=None):
        if group is None:
            self.pb = (self.pb + 1) % 7
            return self.pb
        lo, n = group
        k = self.gpb.get(group, 0)
        self.gpb[group] = k + 1
        return lo + k % n

    def load_x(self, it, x_in):
        c, TT = self.c, self.TT
        b = it % 2
        t0 = it * TT
        c.dma(self.xs[b][:], x_in[0][:, :, t0:t0 + TT].rearrange("k p t -> p k t"),
              r=tuple(("dram", x_in[1], j) for j in range(t0 // 64, (t0 + TT) // 64)), w=(("x", b),))
        return self.xs[b], ("x", b)

    def store_x(self, it, x_out):
        c, TT = self.c, self.TT
        b = it % 2
        t0 = it * TT
        c.dma(x_out[0][:, :, t0:t0 + TT].rearrange("k p t -> p k t"), self.xs[b][:], r=(("x", b),),
              w=tuple(("dram", x_out[1], j) for j in range(t0 // 64, (t0 + TT) // 64)))

    def stats(self, src, tok_src, n=None):
        c = self.c
        n = n or self.TT
        P = c.psum
        self.nrm += 1
        rb = self.nrm % 2
        c.act(self.sq[:, :, 0:n], src, AF.Square, r=(tok_src,), w=("sq",) + tuple(getattr(self, "sq_alias", ())))
        for k in range(NK):
            c.mm(P[7][:, 0:n], c.ones, self.sq[:, k, 0:n], start=(k == 0), stop=(k == NK - 1), r=("sq",), w=(("ps", 7),))
        rs = self.rstd[rb][:, 0:n]
        tok = ("rstd", rb)
        c.act(rs, P[7][:, 0:n], AF.Ln, r=(("ps", 7),), w=(tok,), bias=c.eps_ap, scale=1.0 / D)
        c.act(rs, rs, AF.Exp, r=(tok,), w=(tok,), scale=-0.5)
        return rs, tok

    def prenorm(self, it, g):
        c = self.c
        b = it % 2
        X, tx = self.xs[b], ("x", b)
        rs, tok = self.stats(X[:], tx)
        for k in range(NK):
            c.stt(self.hn[b][:, k, :], X[:, k, :], g[:, k:k + 1], rs, ALU.mult, ALU.mult, r=(tx, tok), w=(("hn", b),))
        return self.hn[b], ("hn", b)

    def get_hn(self, it, g, x_in):
        if getattr(self, "pref", None) == it:
            return self.hn[it % 2], ("hn", it % 2)
        self.load_x(it, x_in)
        return self.prenorm(it, g)

    def prefetch(self, it, g, x_in, NT):
        if it < NT:
            self.load_x(it, x_in)
            self.prenorm(it, g)
            self.pref = it

    def post_residual(self, it, g):
        c = self.c
        b = it % 2
        X, tx = self.xs[b], ("x", b)
        rs, tok = self.stats(self.yy[:], "yy")
        for k in range(NK):
            tb = k % 2
            c.stt(self.tmp[tb][:], self.yy[:, k, :], g[:, k:k + 1], rs, ALU.mult, ALU.mult, r=("yy", tok), w=(("tmp", tb),))
            c.tt(X[:, k, :], self.tmp[tb][:], X[:, k, :], ALU.add, r=(("tmp", tb), tx), w=(tx,),
                 eng=("dve" if os.environ.get("MK_VAR2", "") == "E" else "pool"))


def phase_mlp(c, layer, x_in, x_out, w_up, w_down, cst):
    nc = c.nc
    TT = 256
    NT = T // TT
    NF = DFF // 128
    with ExitStack() as es:
        cm = Common(c, es, TT)
        sb = cm.sb
        wu = sb("wu", [128, NK, DFF], BF16)
        wd = sb("wd", [128, NF, D], BF16)
        hh = sb("hh", [128, NF, TT], BF16)
        rr = [sb("rr", [128, TT], BF16) for _ in range(2)]
        P = c.psum
        g1 = cst["norm_ff_pre"][layer]
        g2 = cst["norm_ff_post"][layer]
        load_weight_bf16(c, wu, w_up, D, DFF, "wu", NK, tokfn=lambda k, c0: ("wu", c0 // 2048))
        load_weight_bf16(c, wd, w_down, DFF, D, "wd", NF, tokfn=lambda k, c0: ("wd", k))
        for it in range(NT):
            hn, thn = cm.get_hn(it, g1, x_in)
            for f in range(NF):
                pb = cm.bank()
                for k in range(NK):
                    c.mm(P[pb][:, 0:TT], wu[:, k, f * 128:(f + 1) * 128], hn[:, k, :], start=(k == 0), stop=(k == NK - 1),
                         r=(("wu", f // 16), thn), w=(("ps", pb),))
                rb = f % 2
                c.act(rr[rb][:], P[pb][:, 0:TT], AF.Relu, r=(("ps", pb),), w=(("rr", rb),))
                c.tt(hh[:, f, :], P[pb][:, 0:TT], rr[rb][:], ALU.mult, r=(("ps", pb), ("rr", rb)), w=(("hh", f),))
            cm.prefetch(it + 1, g1, x_in, NT)
            for o in range(NK):
                pb = cm.bank()
                for f in range(NF):
                    c.mm(P[pb][:, 0:TT], wd[:, f, o * 128:(o + 1) * 128], hh[:, f, :], start=(f == 0), stop=(f == NF - 1),
                         r=(("wd", f), ("hh", f)), w=(("ps", pb),))
                c.cp(cm.yy[:, o, :], P[pb][:, 0:TT], r=(("ps", pb),), w=("yy",), eng="act")
            cm.post_residual(it, g2)
            cm.store_x(it, x_out)
        c.s.barrier()


def phase_xattn(c, layer, x_in, x_out, memT, wq_d, wk_d, wv_d, wo_d, cst):
    nc = c.nc
    TT = 512
    NT = T // TT
    with ExitStack() as es:
        cm = Common(c, es, TT)
        sb = cm.sb
        wq = sb("wq", [128, NK, D], BF16)
        wo = sb("wo", [128, NK, D], BF16)
        kT = sb("kT", [128, NK, NMEM], BF16)
        V = sb("V", [128, 2, D], BF16)
        qT = sb("qT", [128, NK, TT], BF16)
        oT = sb("oT", [128, NK, TT], BF16)
        E = [sb("E", [128, 2, TT], BF16) for _ in range(2)]
        rden = [sb("rden", [128, TT], F32) for _ in range(2)]
        P = c.psum
        g_pre = cst["norm_xa_pre"][layer]
        g_post = cst["norm_xa_post"][layer]
        g_mem = cst["norm_mem"][layer]
        with ExitStack() as es2:
            sb2 = lambda name, shape, dt: es2.enter_context(nc.sbuf_tensor(c.name(name), shape, dt))
            wk = sb2("wk", [128, NK, D], BF16)
            wv = sb2("wv", [128, NK, D], BF16)
            load_weight_bf16(c, wk, wk_d, D, D, "wk", NK)
            load_weight_bf16(c, wv, wv_d, D, D, "wv", NK)
            load_weight_bf16(c, wq, wq_d, D, D, "wq", NK)
            load_weight_bf16(c, wo, wo_d, D, D, "wo", NK)
            mx = cm.xs[1]
            c.dma(mx[:, :, 0:NMEM], memT.rearrange("k p t -> p k t"), r=(), w=(("x", 1),))
            rs, tok = cm.stats(mx[:, :, 0:NMEM], ("x", 1), n=NMEM)
            mn = cm.hn[1]
            for k in range(NK):
                c.stt(mn[:, k, 0:NMEM], mx[:, k, 0:NMEM], g_mem[:, k:k + 1], rs, ALU.mult, ALU.mult,
                      r=(("x", 1), tok), w=(("hn", 1),))
            for cc in range(NK):
                pb = cm.bank()
                for k in range(NK):
                    c.mm(P[pb][:, 0:NMEM], wk[:, k, cc * 128:(cc + 1) * 128], mn[:, k, 0:NMEM], start=(k == 0), stop=(k == NK - 1),
                         r=("wk", ("hn", 1)), w=(("ps", pb),))
                c.cp(kT[:, cc, :], P[pb][:, 0:NMEM], r=(("ps", pb),), w=("kT",), eng="act")
            for mc in range(2):
                for hf in range(2):
                    pb = cm.bank()
                    for k in range(NK):
                        c.mm(P[pb][:, :], mn[:, k, mc * 128:(mc + 1) * 128], wv[:, k, hf * 512:(hf + 1) * 512],
                             start=(k == 0), stop=(k == NK - 1), r=("wv", ("hn", 1)), w=(("ps", pb),))
                    c.cp(V[:, mc, hf * 512:(hf + 1) * 512], P[pb][:, :], r=(("ps", pb),), w=("V",), eng="dve")
            c.s.barrier()
        for it in range(NT):
            hn, thn = cm.get_hn(it, g_pre, x_in)
            for cc in range(NK):
                pb = cm.bank()
                for k in range(NK):
                    c.mm(P[pb][:, :], wq[:, k, cc * 128:(cc + 1) * 128], hn[:, k, :], start=(k == 0), stop=(k == NK - 1),
                         r=("wq", thn), w=(("ps", pb),))
                c.act(qT[:, cc, :], P[pb][:, :], AF.Copy, r=(("ps", pb),), w=(("qT", cc),), scale=1.0 / 16.0)
            cm.prefetch(it + 1, g_pre, x_in, NT)
            def scores(h):
                eb = h % 2
                for mc in range(2):
                    pb = cm.bank()
                    for ci in range(2):
                        cc = 2 * h + ci
                        c.mm(P[pb][:, :], kT[:, cc, mc * 128:(mc + 1) * 128], qT[:, cc, :], start=(ci == 0), stop=(ci == 1),
                             r=("kT", ("qT", cc)), w=(("ps", pb),))
                    c.act(E[eb][:, mc, :], P[pb][:, :], AF.Exp, r=(("ps", pb),), w=(("E", eb, mc),))

            def attend(h):
                eb = h % 2
                pd = cm.bank()
                for mc in range(2):
                    c.mm(P[pd][:, :], c.ones, E[eb][:, mc, :], start=(mc == 0), stop=(mc == 1), r=(("E", eb, mc),), w=(("ps", pd),))
                c.recip(rden[eb][:], P[pd][:, :], r=(("ps", pd),), w=(("rden", eb),))
                for ci in range(2):
                    cc = 2 * h + ci
                    pb = cm.bank()
                    for mc in range(2):
                        c.mm(P[pb][:, :], V[:, mc, cc * 128:(cc + 1) * 128], E[eb][:, mc, :], start=(mc == 0), stop=(mc == 1),
                             r=("V", ("E", eb, mc)), w=(("ps", pb),))
                    c.tt(oT[:, cc, :], P[pb][:, :], rden[eb][:], ALU.mult, r=(("ps", pb), ("rden", eb)), w=(("oT", cc),))

            scores(0)
            for h in range(4):
                if h + 1 < 4:
                    scores(h + 1)
                attend(h)
            for o in range(NK):
                pb = cm.bank()
                for cc in range(NK):
                    c.mm(P[pb][:, :], wo[:, cc, o * 128:(o + 1) * 128], oT[:, cc, :], start=(cc == 0), stop=(cc == NK - 1),
                         r=("wo", ("oT", cc)), w=(("ps", pb),))
                c.cp(cm.yy[:, o, :], P[pb][:, :], r=(("ps", pb),), w=("yy",), eng="act")
            cm.post_residual(it, g_post)
            cm.store_x(it, x_out)
        c.s.barrier()


def phase_mlstm(c, layer, j, x_in, x_out, w_in_d, w_out_d, cst):
    nc = c.nc
    TT = 256
    NT = T // TT
    NC_ = TT // 64
    NG = 2
    GH = 8 // NG
    with ExitStack() as es:
        cm = Common(c, es, TT)
        sb = cm.sb
        P = c.psum
        win = sb("mlwin", [128, NK, 3088], BF16)
        wout = sb("mlwout", [128, NK, D], BF16)
        load_weight_bf16(c, win, w_in_d, D, 3088, "win", NK)
        load_weight_bf16(c, wout, w_out_d, D, D, "wout", NK)
        g_pre = cst["norm_mix_pre"][layer]
        g_post = cst["norm_mix_post"][layer]
        nw = cst["ml_norm_w"][j]
        bi = cst["ml_b_i"][j]
        bf_ = cst["ml_b_f"][j]
        selh = cst["selh"][0]
        ident = cst["ident"][0]
        mask = cst["mask_incl"][0]
        reset = cst["reset"][0]
        gsm = lambda nm: sb(nm, [8, TT], F32)
        b15 = sb("b15", [8, 2], F32)
        one1 = sb("one1", [8, 1], F32)
        onesb = sb("onesb", [64, 128], BF16)
        th_i, th_f, e1, l1, bneg, arg, arg2 = [gsm(n) for n in ("thi", "thf", "e1", "l1", "bneg", "arg", "arg2")]
        eb, ek, eend = gsm("eb"), gsm("ek"), gsm("eend")
        EB = sb("EB", [64, 8, TT], F32)
        EK = sb("EK", [64, 8, TT], F32)
        qt = sb("qt", [64, 8, TT], BF16)
        kt = sb("kt", [64, 8, TT], BF16)
        eT = [sb("eT", [64, 8], F32) for _ in range(NC_)]
        KhT = [sb("KhT", [64, 8, 64], BF16) for _ in range(NC_)]
        Va = [sb("Va", [64, 8, 128], BF16) for _ in range(NC_)]
        ST = [sb("ST", [64, 8, 64], BF16) for _ in range(2)]
        CT = sb("CT", [64, 8, 128], F32)
        CTb = sb("CTb", [64, 8, 128], BF16)
        nst = sb("nst", [64, 8], F32)
        nrep = sb("nrep", [64, 8, 128], BF16)
        dmax = sb("dmax", [128, 512], F32)
        hT = [sb("hT", [128, 8, TT], F32) for _ in range(2)]
        rsh = [sb("rsh", [128, TT], F32) for _ in range(2 * NG)]
        sg = [sb("sg", [128, TT], F32) for _ in range(2 * NG)]
        t1 = [sb("t1", [128, TT], F32) for _ in range(2 * NG)]
        mix = sb("mix", [128, 8, TT], BF16)
        c.memset(CT[:], 0.0, w=tuple(("CT", g) for g in range(NG)))
        c.memset(CTb[:], 0.0, w=tuple(("CTb", g) for g in range(NG)))
        c.memset(nst[:], 0.0, w=tuple(("nst", g) for g in range(NG)))
        c.memset(nrep[:], 0.0, w=tuple(("nrep", g) for g in range(NG)))
        c.memset(onesb[:], 1.0, w=("onesb",))
        c.memset(one1[:], 1.0, w=("one1",))
        c.ts(b15[:, 0:1], bi[0:8, :], 1.0 / 15.0, ALU.mult, r=(), w=("b15",))
        c.ts(b15[:, 1:2], bf_[0:8, :], 1.0 / 15.0, ALU.mult, r=(), w=("b15",))
        V3 = lambda ap, d: ap.rearrange("p (h d) -> p h d", d=d)
        GBANK = [(0, 3), (3, 3)]
        CB_ = (6, 1)
        cm.sq_alias = tuple(("sqg", g) for g in range(NG))

        def head(it):
            hn, thn = cm.get_hn(it, g_pre, x_in)
            pgi, pgf = cm.bank(CB_), cm.bank(GBANK[0])
            for (pg, c0) in ((pgi, 3072), (pgf, 3080)):
                for k in range(NK):
                    c.mm(P[pg][0:8, 0:TT], win[:, k, c0:c0 + 8], hn[:, k, :], start=(k == 0), stop=(k == NK - 1),
                         r=("win", thn), w=(("ps", pg),))
            c.act(th_i[:], P[pgi][0:8, 0:TT], AF.Tanh, r=(("ps", pgi), "b15"), w=("thi",), bias=b15[:, 0:1], scale=1.0 / 15.0)
            c.act(th_f[:], P[pgf][0:8, 0:TT], AF.Tanh, r=(("ps", pgf), "b15"), w=("thf",), bias=b15[:, 1:2], scale=1.0 / 15.0)
            c.act(e1[:], th_f[:], AF.Exp, r=("thf",), w=("e1",), scale=-15.0)
            c.act(l1[:], e1[:], AF.Ln, r=("e1", "one1"), w=("l1",), bias=one1[:], scale=1.0)
            c.s.add("dve", lambda e: e.tensor_tensor_scan(bneg[:], reset[0:8, 0:TT], l1[:], 0.0, ALU.mult, ALU.add),
                    r=("l1",), w=("bneg",))
            c.act(eb[:], bneg[:], AF.Exp, r=("bneg",), w=("eb",), scale=-1.0)
            c.stt(arg[:], th_i[:], 15.0, bneg[:], ALU.mult, ALU.add, r=("thi", "bneg"), w=("arg",))
            c.act(ek[:], arg[:], AF.Exp, r=("arg",), w=("ek",))
            a3 = arg[:].rearrange("p (c l) -> p c l", l=64)
            bL = bneg[:].rearrange("p (c l) -> p c l", l=64)[:, :, 63:64].broadcast_to([8, NC_, 64])
            c.tt(arg2[:].rearrange("p (c l) -> p c l", l=64), a3, bL, ALU.subtract, r=("arg", "bneg"), w=("arg2",))
            c.act(eend[:], arg2[:], AF.Exp, r=("arg2",), w=("eend",))
            for jc in range(NC_):
                tsl = slice(jc * 64, (jc + 1) * 64)
                pt = cm.bank(CB_)
                c.tr(P[pt][0:64, 0:8], eend[0:8, tsl], ident[0:8, 0:8], r=("eend",), w=(("ps", pt),))
                c.ts(eT[jc][:], P[pt][0:64, 0:8], 0.125, ALU.mult, r=(("ps", pt),), w=(("eT", jc),))

        def group(it, g):
            b = it % 2
            G = GBANK[g]
            hn, thn = cm.hn[b], ("hn", b)
            hsl = slice(g * GH, (g + 1) * GH)
            for h2 in range(g * GH // 2, (g + 1) * GH // 2):
                pe_, pk_ = cm.bank(G), cm.bank(G)
                for i in range(2):
                    h = 2 * h2 + i
                    cs = slice(i * TT, (i + 1) * TT)
                    c.mm(P[pe_][0:64, cs], selh[0:8, h * 64:(h + 1) * 64], eb[:], start=True, stop=True, r=("eb",), w=(("ps", pe_),))
                    c.mm(P[pk_][0:64, cs], selh[0:8, h * 64:(h + 1) * 64], ek[:], start=True, stop=True, r=("ek",), w=(("ps", pk_),))
                hs2 = slice(2 * h2, 2 * h2 + 2)
                c.cp(EB[:, hs2, :], V3(P[pe_][0:64, :], TT), r=(("ps", pe_),), w=(("EB", h2),), eng="act")
                c.cp(EK[:, hs2, :], V3(P[pk_][0:64, :], TT), r=(("ps", pk_),), w=(("EK", h2),), eng="act")
                pq, pk2 = cm.bank(G), cm.bank(G)
                for i in range(2):
                    h = 2 * h2 + i
                    cs = slice(i * TT, (i + 1) * TT)
                    for k in range(NK):
                        c.mm(P[pq][0:64, cs], win[:, k, h * 64:(h + 1) * 64], hn[:, k, :], start=(k == 0), stop=(k == NK - 1),
                             r=("win", thn), w=(("ps", pq),))
                    for k in range(NK):
                        c.mm(P[pk2][0:64, cs], win[:, k, 512 + h * 64:512 + (h + 1) * 64], hn[:, k, :], start=(k == 0),
                             stop=(k == NK - 1), r=("win", thn), w=(("ps", pk2),))
                c.tt(qt[:, hs2, :], V3(P[pq][0:64, :], TT), EB[:, hs2, :], ALU.mult,
                     r=(("ps", pq), ("EB", h2)), w=(("qt", h2),))
                c.stt(kt[:, hs2, :], V3(P[pk2][0:64, :], TT), 0.125, EK[:, hs2, :], ALU.mult, ALU.mult,
                      r=(("ps", pk2), ("EK", h2)), w=(("kt", h2),))
            for jc in range(NC_):
                tsl = slice(jc * 64, (jc + 1) * 64)
                pkk = cm.bank(G)
                for k in range(NK):
                    c.mm(P[pkk][0:64, 0:GH * 64], hn[:, k, tsl], win[:, k, 512 + g * GH * 64:512 + (g + 1) * GH * 64], start=(k == 0),
                         stop=(k == NK - 1), r=("win", thn), w=(("ps", pkk),))
                c.tt(KhT[jc][:, hsl, :], V3(P[pkk][0:64, 0:GH * 64], 64),
                     eT[jc][:, hsl].unsqueeze(2).broadcast_to([64, GH, 64]),
                     ALU.mult, r=(("ps", pkk), ("eT", jc)), w=(("KhT", jc, g),))
                pv = cm.bank(G)
                for k in range(NK):
                    c.mm(P[pv][0:64, 0:GH * 128], hn[:, k, tsl], win[:, k, 1024 + g * GH * 128:1024 + (g + 1) * GH * 128], start=(k == 0),
                         stop=(k == NK - 1), r=("win", thn), w=(("ps", pv),))
                c.cp(Va[jc][:, hsl, :], V3(P[pv][0:64, 0:GH * 128], 128), r=(("ps", pv),), w=(("Va", jc, g),), eng="act")
            for jc in range(NC_):
                cols = slice(jc * 64, (jc + 1) * 64)
                jb = jc % 2
                pS = cm.bank(G)
                for hi in range(GH):
                    h = g * GH + hi
                    c.mm(P[pS][0:64, hi * 64:(hi + 1) * 64], kt[:, h, cols], qt[:, h, cols], start=True, stop=True,
                         r=(("kt", h // 2), ("qt", h // 2)), w=(("ps", pS),))
                c.tt(ST[jb][:, hsl, :], V3(P[pS][0:64, 0:GH * 64], 64),
                     mask[0:64, :].unsqueeze(1).broadcast_to([64, GH, 64]), ALU.mult, r=(("ps", pS),), w=(("ST", jb, g),))
                pN, pD = cm.bank(G), cm.bank(G)
                for hi in range(GH):
                    h = g * GH + hi
                    hs = slice(hi * 64, (hi + 1) * 64)
                    c.mm(P[pN][:, hs], CTb[:, h, :], qt[:, h, cols], start=True, stop=False,
                         r=(("CTb", g), ("qt", h // 2)), w=(("ps", pN),))
                    c.mm(P[pN][:, hs], Va[jc][:, h, :], ST[jb][:, h, :], start=False, stop=True,
                         r=(("Va", jc, g), ("ST", jb, g)), w=(("ps", pN),))
                    c.mm(P[pD][:, hs], nrep[:, h, :], qt[:, h, cols], start=True, stop=False,
                         r=(("nrep", g), ("qt", h // 2)), w=(("ps", pD),))
                    c.mm(P[pD][:, hs], onesb[:], ST[jb][:, h, :], start=False, stop=True,
                         r=("onesb", ("ST", jb, g)), w=(("ps", pD),))
                dm = dmax[:, g * GH * 64:(g + 1) * GH * 64]
                c.ts(dm, P[pD][:, 0:GH * 64], -1.0, ALU.mult, r=(("ps", pD),), w=(("dmax", g),), s2=1.0, op1=ALU.max)
                c.tt(dm, P[pD][:, 0:GH * 64], dm, ALU.max, r=(("ps", pD), ("dmax", g)), w=(("dmax", g),))
                c.recip(dm, dm, r=(("dmax", g),), w=(("dmax", g),))
                c.tt(hT[b][:, hsl, cols], V3(P[pN][:, 0:GH * 64], 64), V3(dm, 64), ALU.mult,
                     r=(("ps", pN), ("dmax", g)), w=(("hT", b, g),))
                pU = cm.bank(G)
                pUn = cm.bank(G)
                for hi in range(GH):
                    h = g * GH + hi
                    c.mm(P[pU][0:64, hi * 128:(hi + 1) * 128], KhT[jc][:, h, :], Va[jc][:, h, :],
                         start=True, stop=True, r=(("KhT", jc, g), ("Va", jc, g)), w=(("ps", pU),))
                    c.mm(P[pUn][0:64, hi:hi + 1], KhT[jc][:, h, :], onesb[:, 0:1], start=True, stop=True,
                         r=(("KhT", jc, g), "onesb"), w=(("ps", pUn),))
                ebl = EB[:, hsl, jc * 64 + 63:jc * 64 + 64]
                ebl_toks = tuple(("EB", q) for q in range(g * GH // 2, (g + 1) * GH // 2))
                c.tt(CT[:, hsl, :], CT[:, hsl, :], ebl.broadcast_to([64, GH, 128]), ALU.mult, r=(("CT", g),) + ebl_toks, w=(("CT", g),))
                c.tt(CT[:, hsl, :], CT[:, hsl, :], V3(P[pU][0:64, 0:GH * 128], 128), ALU.add, r=(("CT", g), ("ps", pU)), w=(("CT", g),))
                c.cp(CTb[:, hsl, :], CT[:, hsl, :], r=(("CT", g),), w=(("CTb", g),), eng="act")
                c.tt(nst[:, hsl], nst[:, hsl], ebl.rearrange("p h o -> p (h o)"), ALU.mult, r=(("nst", g),) + ebl_toks, w=(("nst", g),))
                c.tt(nst[:, hsl], nst[:, hsl], P[pUn][0:64, 0:GH], ALU.add, r=(("nst", g), ("ps", pUn)), w=(("nst", g),))
                c.cp(nrep[:, hsl, :], nst[:, hsl].unsqueeze(2).broadcast_to([64, GH, 128]), r=(("nst", g),), w=(("nrep", g),), eng="pool")
            H = hT[b]
            c.act(cm.sq[:, hsl, :], H[:, hsl, :], AF.Square, r=(("hT", b, g),), w=(("sqg", g),))
            for hi in range(GH):
                h = g * GH + hi
                hb2 = g * 2 + hi % 2
                pb = cm.bank(G)
                c.mm(P[pb][:, 0:TT], c.ones, cm.sq[:, h, :], start=True, stop=True, r=(("sqg", g),), w=(("ps", pb),))
                c.act(rsh[hb2][:], P[pb][:, 0:TT], AF.Ln, r=(("ps", pb),), w=(("rsh", hb2),), bias=c.eps_ap, scale=1.0 / 128.0)
                c.act(rsh[hb2][:], rsh[hb2][:], AF.Exp, r=(("rsh", hb2),), w=(("rsh", hb2),), scale=-0.5)
                po = cm.bank(G)
                for k in range(NK):
                    c.mm(P[po][:, 0:TT], win[:, k, 2048 + h * 128:2048 + (h + 1) * 128], hn[:, k, :], start=(k == 0), stop=(k == NK - 1),
                         r=("win", thn), w=(("ps", po),))
                c.act(sg[hb2][:], P[po][:, 0:TT], AF.Sigmoid, r=(("ps", po),), w=(("sg", hb2),))
                c.stt(t1[hb2][:], H[:, h, :], nw[:, h:h + 1], rsh[hb2][:], ALU.mult, ALU.mult, r=(("hT", b, g), ("rsh", hb2)), w=(("t1", hb2),))
                c.tt(mix[:, h, :], t1[hb2][:], sg[hb2][:], ALU.mult, r=(("t1", hb2), ("sg", hb2)), w=(("mix", h),), eng="pool")

        def tail(it):
            for o in range(NK):
                pb = cm.bank(CB_) if o % 2 == 0 else cm.bank(GBANK[0])
                for h in range(8):
                    c.mm(P[pb][:, 0:TT], wout[:, h, o * 128:(o + 1) * 128], mix[:, h, :], start=(h == 0), stop=(h == 7),
                         r=("wout", ("mix", h)), w=(("ps", pb),))
                c.cp(cm.yy[:, o, :], P[pb][:, 0:TT], r=(("ps", pb),), w=("yy",), eng="act")
            cm.post_residual(it, g_post)
            cm.store_x(it, x_out)

        S = c.s
        head(0)
        for it in range(NT):
            thr = [S.capture(group, it, g) for g in range(NG)]
            if os.environ.get("MK_NOIL", "0") == "1":
                for t_ in thr:
                    S.merge(t_, [])
            else:
                S.merge_greedy(thr[0], thr[1])
            cm.prefetch(it + 1, g_pre, x_in, NT)
            if it + 1 < NT:
                head(it + 1)
            tail(it)
        c.s.barrier()


def phase_mamba(c, layer, j, x_in, x_out, ya_d, w_in_d, w_outa_d, w_outb_d, cst):
    nc = c.nc
    TT = 256
    NT = T // TT
    NC_ = TT // 64
    with ExitStack() as es:
        cm = Common(c, es, TT)
        sb = cm.sb
        P = c.psum
        win = sb("mbwin", [128, NK, 1544], BF16)
        woa = sb("mbwoa", [128, 4, D], BF16)
        wob = sb("mbwob", [64, 8, D], BF16)
        load_weight_bf16(c, win, w_in_d, D, 1544, "win", NK)
        load_weight_bf16(c, woa, w_outa_d, 512, D, "woa", 4)
        for h in range(8):
            c.dma(wob[:, h, :], w_outb_d[h], r=(), w=(), wacc=("wob",), q="pool")
        g_pre = cst["norm_mix_pre"][layer]
        g_post = cst["norm_mix_post"][layer]
        cwx = [cst["mb_cwx%d" % i][j] for i in range(4)]
        cwb = [cst["mb_cwb%d" % i][j] for i in range(4)]
        cbx, cbb = cst["mb_cbx"][j], cst["mb_cbb"][j]
        dtb, alog = cst["mb_dt_bias"][j], cst["mb_A_log"][j]
        Dbc, nwm = cst["mb_D"][j], cst["mb_norm_w"][j]
        sel128, selh, ident = cst["sel128"][0], cst["selh"][0], cst["ident"][0]
        negm, reset = cst["neg_mask"][0], cst["reset"][0]
        gsm = lambda nm: sb(nm, [8, TT], F32)
        one1 = sb("one1", [8, 1], F32)
        eps5 = sb("eps5", [64, 1], F32)
        negA = sb("negA", [8, 1], F32)
        ones64 = sb("ones64", [64, 64], BF16)
        XR = sb("XR", [64, 8, TT + 3], F32)
        BR = sb("BR", [128, 4, TT + 3], F32)
        acc = sb("acc", [64, 8, TT], F32)
        accb = sb("accb", [128, 4, TT], F32)
        XS = sb("XS", [64, 8, TT], F32)
        BCf = sb("BCf", [128, 4, TT], F32)
        BCb = sb("BCb", [128, 4, TT], BF16)
        SZ = sb("SZ", [64, 8, TT], F32)
        e_dt, dt_, dA, acum, ea, warg, wend = [gsm(n) for n in ("edt", "dt", "dA", "acum", "ea", "warg", "wend")]
        Ct = sb("Ct", [128, 8, TT], BF16)
        eaL = sb("eaL", [128, 8, NC_], F32)
        ACb = sb("ACb", [64, 8, TT], F32)
        sc = [sb("sc", [64, 24], F32) for _ in range(2)]
        xdtT = [sb("xdtT", [64, 8, 64], BF16) for _ in range(2)]
        xhT = [sb("xhT", [64, 8, 64], BF16) for _ in range(2)]
        BT = [sb("BT", [64, 2, 128], BF16) for _ in range(2)]
        CBs = [sb("CBs", [64, 2, 64], F32) for _ in range(2)]
        arg = [sb("arg", [64, 8, 64], F32) for _ in range(2)]
        G = [sb("G", [64, 8, 64], BF16) for _ in range(2)]
        STs = sb("STs", [128, 8, 64], F32)
        STb = sb("STb", [128, 8, 64], BF16)
        Y = sb("Y", [64, 8, TT], F32)
        rsg = [sb("rsg", [64, TT], F32) for _ in range(2)]
        yb = sb("yb", [64, 8, TT], BF16)
        yab = sb("yab", [128, 4, TT], BF16)
        c.memset(one1[:], 1.0, w=("one1",))
        c.memset(eps5[:], 1e-5, w=("eps5",))
        c.memset(ones64[:], 1.0, w=("ones64",))
        c.memset(XR[:], 0.0, w=("XR",))
        c.memset(BR[:], 0.0, w=("BR",))
        c.memset(STs[:], 0.0, w=("STs",))
        c.memset(STb[:], 0.0, w=("STb",))
        c.act(negA[:], alog[0:8, :], AF.Exp, r=(), w=("negA",))
        for it in range(NT):
            t0 = it * TT
            c.dma(yab[:], ya_d[0][:, :, t0:t0 + TT].rearrange("k p t -> p k t"),
                  r=tuple(("dram", ya_d[1], q) for q in range(t0 // 64, (t0 + TT) // 64)), w=("yab",))
            hn, thn = cm.get_hn(it, g_pre, x_in)
            for h2 in range(4):
                pz, px = cm.bank(), cm.bank()
                for i in range(2):
                    h = 2 * h2 + i
                    cs = slice(i * TT, (i + 1) * TT)
                    for k in range(NK):
                        c.mm(P[pz][0:64, cs], win[:, k, h * 64:(h + 1) * 64], hn[:, k, :], start=(k == 0), stop=(k == NK - 1),
                             r=("win", thn), w=(("ps", pz),))
                    for k in range(NK):
                        c.mm(P[px][0:64, cs], win[:, k, 512 + h * 64:512 + (h + 1) * 64], hn[:, k, :], start=(k == 0),
                             stop=(k == NK - 1), r=("win", thn), w=(("ps", px),))
                hs2 = slice(2 * h2, 2 * h2 + 2)
                c.act(SZ[:, hs2, :], P[pz][0:64, :].rearrange("p (h t) -> p h t", t=TT), AF.Silu, r=(("ps", pz),), w=("SZ",))
                c.cp(XR[:, hs2, 3:3 + TT], P[px][0:64, :].rearrange("p (h t) -> p h t", t=TT), r=(("ps", px),), w=("XR",), eng="act")
            for q in range(4):
                pb = cm.bank()
                for k in range(NK):
                    c.mm(P[pb][:, 0:TT], win[:, k, 1024 + q * 128:1024 + (q + 1) * 128], hn[:, k, :], start=(k == 0), stop=(k == NK - 1),
                         r=("win", thn), w=(("ps", pb),))
                c.cp(BR[:, q, 3:3 + TT], P[pb][:, 0:TT], r=(("ps", pb),), w=("BR",), eng="act")
            pd = cm.bank()
            for k in range(NK):
                c.mm(P[pd][0:8, 0:TT], win[:, k, 1536:1544], hn[:, k, :], start=(k == 0), stop=(k == NK - 1),
                     r=("win", thn), w=(("ps", pd),))
            c.act(e_dt[:], P[pd][0:8, 0:TT], AF.Exp, r=(("ps", pd),), w=("edt",), bias=dtb[0:8, :])
            c.act(dt_[:], e_dt[:], AF.Ln, r=("edt", "one1"), w=("dt",), bias=one1[:])
            c.ts(dA[:], dt_[:], negA[:, 0:1], ALU.mult, r=("dt", "negA"), w=("dA",), s2=-1.0, op1=ALU.mult)
            c.s.add("dve", lambda e: e.tensor_tensor_scan(acum[:], reset[0:8, 0:TT], dA[:], 0.0, ALU.mult, ALU.add),
                    r=("dA",), w=("acum",))
            c.act(ea[:], acum[:], AF.Exp, r=("acum",), w=("ea",))
            aL = acum[:].rearrange("p (c l) -> p c l", l=64)[:, :, 63:64].broadcast_to([8, NC_, 64])
            c.tt(warg[:].rearrange("p (c l) -> p c l", l=64), aL, acum[:].rearrange("p (c l) -> p c l", l=64), ALU.subtract,
                 r=("acum",), w=("warg",))
            c.act(wend[:], warg[:], AF.Exp, r=("warg",), w=("wend",))
            c.tt(wend[:], wend[:], dt_[:], ALU.mult, r=("wend", "dt"), w=("wend",))
            for h in range(8):
                c.ts(acc[:, h, :], XR[:, h, 3:3 + TT], cwx[3][0:64, h:h + 1], ALU.mult, r=("XR",), w=(("acc", h),),
                     s2=cbx[0:64, h:h + 1], op1=ALU.add, eng="pool")
                for i in range(3):
                    c.stt(acc[:, h, :], XR[:, h, i:i + TT], cwx[i][0:64, h:h + 1], acc[:, h, :], ALU.mult, ALU.add,
                          r=("XR", ("acc", h)), w=(("acc", h),))
            for q in range(4):
                c.ts(accb[:, q, :], BR[:, q, 3:3 + TT], cwb[3][:, q:q + 1], ALU.mult, r=("BR",), w=(("accb", q),),
                     s2=cbb[:, q:q + 1], op1=ALU.add, eng="pool")
                for i in range(3):
                    c.stt(accb[:, q, :], BR[:, q, i:i + TT], cwb[i][:, q:q + 1], accb[:, q, :], ALU.mult, ALU.add,
                          r=("BR", ("accb", q)), w=(("accb", q),))
            acc_t = tuple(("acc", h) for h in range(8))
            accb_t = tuple(("accb", q) for q in range(4))
            c.act(XS[:], acc[:], AF.Silu, r=acc_t, w=("XS",))
            c.act(BCf[:], accb[:], AF.Silu, r=accb_t, w=("BCf",))
            c.cp(BCb[:], BCf[:], r=("BCf",), w=("BCb",), eng="pool")
            c.cp(XR[:, :, 0:3], XR[:, :, TT:TT + 3], r=("XR",), w=("XR",), eng="pool")
            c.cp(BR[:, :, 0:3], BR[:, :, TT:TT + 3], r=("BR",), w=("BR",), eng="pool")
            if DBG_STOP <= 1:
                cm.store_x(it, x_out)
                continue
            if os.environ.get("MK_PFPOS", "0") == "1":
                cm.prefetch(it + 1, g_pre, x_in, NT)
            for h2 in range(4):
                pe_, pa_ = cm.bank(), cm.bank()
                for i in range(2):
                    h = 2 * h2 + i
                    cs = slice(i * TT, (i + 1) * TT)
                    c.mm(P[pe_][:, cs], sel128[0:8, h * 128:(h + 1) * 128], ea[:], start=True, stop=True, r=("ea",), w=(("ps", pe_),))
                    c.mm(P[pa_][0:64, cs], selh[0:8, h * 64:(h + 1) * 64], acum[:], start=True, stop=True, r=("acum",), w=(("ps", pa_),))
                hs2 = slice(2 * h2, 2 * h2 + 2)
                g = h2 // 2
                pe3 = P[pe_][:, :].rearrange("p (h t) -> p h t", t=TT)
                c.tt(Ct[:, hs2, :], pe3, BCf[:, 2 + g, :].unsqueeze(1).broadcast_to([128, 2, TT]), ALU.mult,
                     r=(("ps", pe_), "BCf"), w=("Ct",))
                c.cp(eaL[:, hs2, :], P[pe_][:, :].rearrange("p (h c l) -> p h c l", h=2, l=64)[:, :, :, 63], r=(("ps", pe_),),
                     w=("eaL",), eng="act")
                c.cp(ACb[:, hs2, :], P[pa_][0:64, :].rearrange("p (h t) -> p h t", t=TT), r=(("ps", pa_),), w=("ACb",), eng="act")
            if DBG_STOP <= 2:
                cm.store_x(it, x_out)
                continue
            if os.environ.get("MK_PFPOS", "0") == "0":
                cm.prefetch(it + 1, g_pre, x_in, NT)
            for jc in range(NC_):
                cols = slice(jc * 64, (jc + 1) * 64)
                jb = jc % 2
                pt = cm.bank()
                for qi, src in enumerate((acum, dt_, wend)):
                    c.tr(P[pt][0:64, qi * 8:(qi + 1) * 8], src[0:8, cols], ident[0:8, 0:8], r=("acum", "dt", "wend"), w=(("ps", pt),))
                c.cp(sc[jb][:], P[pt][0:64, 0:24], r=(("ps", pt),), w=(("sc", jb),), eng="act")
                px_ = cm.bank()
                for h in range(8):
                    c.tr(P[px_][0:64, h * 64:(h + 1) * 64], XS[:, h, cols], ident[0:64, 0:64], r=("XS",), w=(("ps", px_),))
                px3 = P[px_][0:64, :].rearrange("p (h d) -> p h d", d=64)
                c.tt(xdtT[jb][:], px3, sc[jb][:, 8:16].unsqueeze(2).broadcast_to([64, 8, 64]), ALU.mult,
                     r=(("ps", px_), ("sc", jb)), w=(("xdtT", jb),))
                c.tt(xhT[jb][:], px3, sc[jb][:, 16:24].unsqueeze(2).broadcast_to([64, 8, 64]), ALU.mult,
                     r=(("ps", px_), ("sc", jb)), w=(("xhT", jb),))
                pbt = cm.bank()
                for g in range(2):
                    c.tr(P[pbt][0:64, g * 128:(g + 1) * 128], BCf[:, g, cols], ident[:, :], r=("BCf",), w=(("ps", pbt),))
                c.cp(BT[jb][:], P[pbt][0:64, 0:256].rearrange("p (g n) -> p g n", n=128), r=(("ps", pbt),), w=(("BT", jb),), eng="act")
                pcb = cm.bank()
                for g in range(2):
                    c.mm(P[pcb][0:64, g * 64:(g + 1) * 64], BCb[:, g, cols], BCb[:, 2 + g, cols], start=True, stop=True,
                         r=("BCb",), w=(("ps", pcb),))
                c.cp(CBs[jb][:], P[pcb][0:64, 0:128].rearrange("p (g l) -> p g l", l=64), r=(("ps", pcb),), w=(("CBs", jb),), eng="act")
                c.tt(arg[jb][:], ACb[:, :, cols], sc[jb][:, 0:8].unsqueeze(2).broadcast_to([64, 8, 64]), ALU.subtract,
                     r=("ACb", ("sc", jb)), w=(("arg", jb),))
                c.tt(arg[jb][:], arg[jb][:], negm[0:64, :].unsqueeze(1).broadcast_to([64, 8, 64]), ALU.add,
                     r=(("arg", jb),), w=(("arg", jb),), eng="pool")
                c.act(arg[jb][:], arg[jb][:], AF.Exp, r=(("arg", jb),), w=(("arg", jb),))
                for g in range(2):
                    c.tt(G[jb][:, 4 * g:4 * g + 4, :], arg[jb][:, 4 * g:4 * g + 4, :],
                         CBs[jb][:, g, :].unsqueeze(1).broadcast_to([64, 4, 64]), ALU.mult,
                         r=(("arg", jb), ("CBs", jb)), w=(("G", jb),))
                py = cm.bank()
                for h in range(8):
                    hs = slice(h * 64, (h + 1) * 64)
                    c.mm(P[py][0:64, hs], STb[:, h, :], Ct[:, h, cols], start=True, stop=False, r=("STb", "Ct"), w=(("ps", py),))
                    c.mm(P[py][0:64, hs], xdtT[jb][:, h, :], G[jb][:, h, :], start=False, stop=True,
                         r=(("xdtT", jb), ("G", jb)), w=(("ps", py),))
                c.cp(Y[:, :, cols], P[py][0:64, :].rearrange("p (h d) -> p h d", d=64), r=(("ps", py),), w=("Y",), eng="act")
                pst = cm.bank()
                for h in range(8):
                    c.mm(P[pst][:, h * 64:(h + 1) * 64], BT[jb][:, h // 4, :], xhT[jb][:, h, :], start=True, stop=True,
                         r=(("BT", jb), ("xhT", jb)), w=(("ps", pst),))
                c.tt(STs[:], STs[:], eaL[:, :, jc:jc + 1].broadcast_to([128, 8, 64]), ALU.mult, r=("STs", "eaL"), w=("STs",))
                c.tt(STs[:], STs[:], P[pst][:, :].rearrange("p (h d) -> p h d", d=64), ALU.add, r=("STs", ("ps", pst)), w=("STs",))
                c.cp(STb[:], STs[:], r=("STs",), w=("STb",), eng="act")
            if DBG_STOP <= 3:
                cm.store_x(it, x_out)
                continue
            if os.environ.get("MK_PFPOS", "0") == "2":
                cm.prefetch(it + 1, g_pre, x_in, NT)
            c.tt(acc[:], XS[:], Dbc[0:64, :].unsqueeze(2).broadcast_to([64, 8, TT]), ALU.mult, r=("XS",) + acc_t, w=acc_t, eng="pool")
            c.tt(Y[:], Y[:], acc[:], ALU.add, r=("Y",) + acc_t, w=("Y",))
            c.tt(Y[:], Y[:], SZ[:], ALU.mult, r=("Y", "SZ"), w=("Y",))
            if DBG_STOP <= 3.3:
                cm.store_x(it, x_out)
                continue
            sq = cm.sq[0:64, :, :]
            c.act(sq, Y[:], AF.Square, r=("Y",), w=("sq",))
            for g in range(2):
                pb = cm.bank()
                for i in range(4):
                    c.mm(P[pb][0:64, 0:TT], ones64[:], sq[:, 4 * g + i, :], start=(i == 0), stop=(i == 3), r=("sq", "ones64"),
                         w=(("ps", pb),))
                c.act(rsg[g][:], P[pb][0:64, 0:TT], AF.Ln, r=(("ps", pb), "eps5"), w=(("rsg", g),), bias=eps5[:], scale=1.0 / 256.0)
                c.act(rsg[g][:], rsg[g][:], AF.Exp, r=(("rsg", g),), w=(("rsg", g),), scale=-0.5)
                for i in range(4):
                    h = 4 * g + i
                    c.stt(yb[:, h, :], Y[:, h, :], nwm[0:64, h:h + 1], rsg[g][:], ALU.mult, ALU.mult, r=("Y", ("rsg", g)), w=(("yb", h),))
            if DBG_STOP <= 3.6:
                cm.store_x(it, x_out)
                continue
            for o in range(NK):
                pb = cm.bank()
                VAR = os.environ.get("MK_VAR", "")
                for cc in range(4):
                    if VAR in ("B", "C"):
                        break
                    c.mm(P[pb][:, 0:TT], woa[:, cc, o * 128:(o + 1) * 128], yab[:, cc, :], start=(cc == 0), stop=(VAR == "A" and cc == 3),
                         r=("woa", "yab"), w=(("ps", pb),))
                for h in range(8):
                    if VAR in ("A", "C"):
                        break
                    c.mm(P[pb][:, 0:TT], wob[:, h, o * 128:(o + 1) * 128], yb[:, h, :], start=(VAR == "B" and h == 0), stop=(h == 7),
                         r=("wob", ("yb", h)), w=(("ps", pb),))
                if VAR == "C":
                    v3 = os.environ.get("MK_VAR3", "act")
                    if v3 != "none":
                        c.cp(cm.yy[:, o, :], cm.xs[it % 2][:, o, :], r=(("x", it % 2),), w=("yy",), eng=v3)
                else:
                    c.cp(cm.yy[:, o, :], P[pb][:, 0:TT], r=(("ps", pb),), w=("yy",), eng="dve")
            if VAR != "C" or os.environ.get("MK_VAR2", "") != "D":
                cm.post_residual(it, g_post)
            cm.store_x(it, x_out)
        c.s.barrier()


def phase_rwkv(c, layer, j, x_in, ya_out, w_in_d, w2_d, a2_d, g2_d, cst):
    nc = c.nc
    TT = 128
    NT = T // TT
    NC_ = TT // 64
    NG = 2
    GH = 8 // NG
    GMAIN = [(0, 2), (2, 2)]
    GPRE = [(4, 1), (5, 1)]
    GSH = (6, 1)
    with ExitStack() as es:
        cm = Common(c, es, TT, lite=True)
        sb = cm.sb
        P = c.psum
        win = sb("rwwin", [128, NK, 1792], BF16)
        w2b = sb("w2b", [64, 512], BF16)
        a2b = sb("a2b", [64, 512], BF16)
        g2b = sb("g2b", [128, 512], BF16)
        load_weight_bf16(c, win, w_in_d, D, 1792, "win", NK)
        c.dma(w2b[:], w2_d, r=(), w=("w2b",), q="pool")
        c.dma(a2b[:], a2_d, r=(), w=("a2b",), q="pool")
        c.dma(g2b[:], g2_d, r=(), w=("g2b",), q="pool")
        g_pre = cst["norm_mix_pre"][layer]
        C8 = lambda nm: cst[nm][j][0:64, :]
        mu = [C8("rw_mu_r"), C8("rw_mu_k"), C8("rw_mu_v")]
        mul = cst["rw_mu_l"][j]
        w0, a0, kkc, kac, rkc, lnw, lnb = [C8(n) for n in ("rw_w0", "rw_a0", "rw_k_k", "rw_k_a", "rw_r_k", "rw_ln_w", "rw_ln_b")]
        ident = cst["ident"][0]
        m_strict, m_incl, m_low = cst["mask_strict"][0], cst["mask_incl"][0], cst["mask_low"][0]
        reset = cst["reset"][0]
        f32t = lambda nm: sb(nm, [64, 8, TT], F32)
        b16t = lambda nm: sb(nm, [64, 8, TT], BF16)
        ones64 = sb("ones64", [64, 64], BF16)
        lneps = sb("lneps", [64, 1], F32)
        RAW = [sb("RAW", [64, 8, TT + 1], F32) for _ in range(3)]
        RAWL = sb("RAWL", [128, 3, TT + 1], F32)
        X0, X1 = f32t("X0"), f32t("X1")
        XL = sb("XL", [128, 3, TT], F32)
        twd = sb("twd", [64, TT], BF16)
        adb = sb("adb", [64, TT], BF16)
        sgd = sb("sgd", [128, TT], BF16)
        LW, CL, E_, tE, KK, Kmod, Bsc = [f32t(n) for n in ("LW", "CL", "E", "tE", "KK", "Kmod", "Bsc")]
        sqh = b16t("sqh")
        Xv2 = [f32t("Xv") for _ in range(2)]
        Bh2 = [f32t("Bh") for _ in range(2)]
        Kh2 = [f32t("Kh") for _ in range(2)]
        Epos2 = [f32t("Epos") for _ in range(2)]
        BON2 = [f32t("BON") for _ in range(2)]
        GG2 = [f32t("GG") for _ in range(2)]
        At2 = [b16t("At") for _ in range(2)]
        Rt2 = [b16t("Rt") for _ in range(2)]
        Bt2 = [b16t("Bt") for _ in range(2)]
        Kt2 = [b16t("Kt") for _ in range(2)]
        YY = f32t("YY")
        sqh2 = b16t("sqh2")
        NR2 = f32t("NR2")
        YA = sqh2
        m64 = lambda nm, dt=BF16: sb(nm, [64, 8, 64], dt)
        Aab, AabT, Aak, Arb, Ark = [m64(n) for n in ("Aab", "AabT", "Aak", "Arb", "Ark")]
        Tm = [m64("Tm") for _ in range(2)]
        Ak = [m64("Ak") for _ in range(2)]
        AkT = [m64("AkT") for _ in range(2)]
        VT, BhT, KhT, XTs, UTs = [m64(n) for n in ("VT", "BhT", "KhT", "XTs", "UTs")]
        S0 = m64("S0", F32)
        S0b = m64("S0b")
        allg = lambda nm: tuple((nm, g) for g in range(NG))
        c.memset(ones64[:], 1.0, w=("ones64",))
        c.memset(lneps[:], 64e-5, w=("lneps",))
        c.memset(S0[:], 0.0, w=allg("S0"))
        c.memset(S0b[:], 0.0, w=allg("S0b"))
        for q in range(3):
            c.memset(RAW[q][:], 0.0, w=tuple(("RAW", q, g) for g in range(NG)))
        c.memset(RAWL[:], 0.0, w=("RAWL",))
        ya_v = ya_out[0].rearrange("c (two p) t -> p (c two) t", two=2)
        V3 = lambda ap, d=64: ap.rearrange("p (h d) -> p h d", d=d)

        def shared(it):
            hn, thn = cm.get_hn(it, g_pre, x_in)
            pb = cm.bank(GSH)
            for (li, c0, mrows) in ((0, 1536, 64), (1, 1600, 64), (2, 1664, 128)):
                for k in range(NK):
                    c.mm(P[pb][0:mrows, li * TT:(li + 1) * TT], win[:, k, c0:c0 + mrows], hn[:, k, :], start=(k == 0), stop=(k == NK - 1),
                         r=("win", thn), w=(("ps", pb),))
            c.cp(RAWL[0:64, 0:2, 1:1 + TT], V3(P[pb][0:64, 0:2 * TT], TT), r=(("ps", pb),), w=("RAWL",), eng="dve")
            c.cp(RAWL[:, 2, 1:1 + TT], P[pb][:, 2 * TT:3 * TT], r=(("ps", pb),), w=("RAWL",), eng="dve")
            c.tt(XL[:], RAWL[:, :, 0:TT], RAWL[:, :, 1:1 + TT], ALU.subtract, r=("RAWL",), w=("XL",), eng="pool")
            c.tt(XL[:], XL[:], mul[:, :].unsqueeze(2).broadcast_to([128, 3, TT]), ALU.mult, r=("XL",), w=("XL",))
            c.tt(XL[:], XL[:], RAWL[:, :, 1:1 + TT], ALU.add, r=("XL", "RAWL"), w=("XL",))
            c.cp(RAWL[:, :, 0:1], RAWL[:, :, TT:TT + 1], r=("RAWL",), w=("RAWL",), eng="pool")
            c.act(twd[:], XL[0:64, 0, :], AF.Tanh, r=("XL",), w=("twd",))
            c.act(adb[:], XL[0:64, 1, :], AF.Copy, r=("XL",), w=("adb",), scale=1.0)
            c.act(sgd[:], XL[:, 2, :], AF.Sigmoid, r=("XL",), w=("sgd",))

        def pre(it, g):
            b = it % 2
            GB = GPRE[g]
            hsl = slice(g * GH, (g + 1) * GH)
            HTg = GH * TT
            Xv, Bh, Kh, Epos, BON, GG = [t_[b][:, hsl, :] for t_ in (Xv2, Bh2, Kh2, Epos2, BON2, GG2)]
            At, Rt, Bt, Kt = [t_[b][:, hsl, :] for t_ in (At2, Rt2, Bt2, Kt2)]
            tXv, tBh, tKh, tEpos, tBON, tGG = [(n, b, g) for n in ("Xv", "Bh", "Kh", "Epos", "BON", "GG")]
            tAt, tRt, tBt, tKt = [(n, b, g) for n in ("At", "Rt", "Bt", "Kt")]
            hn, thn = cm.hn[b], ("hn", b)
            Xr, Xk = X0[:, hsl, :], X1[:, hsl, :]
            tXr, tXk = ("X0", g), ("X1", g)
            Xs, tXs = [Xr, Xk, Xv], [tXr, tXk, tXv]
            LWg, CLg, Eg, tEg, KKg, Kmodg, Bscg, sqhg = [t_[:, hsl, :] for t_ in (LW, CL, E_, tE, KK, Kmod, Bsc, sqh)]
            tLW, tCL, tEb, ttE, tKK, tKmod, tBsc, tsqh = [(n, g) for n in ("LW", "CL", "E", "tE", "KK", "Kmod", "Bsc", "sqh")]
            bcg = lambda ap: ap[:, hsl].unsqueeze(2).broadcast_to([64, GH, TT])
            fl = lambda ap: ap.rearrange("p h t -> p (h t)")
            for q in range(3):
                pb = cm.bank(GB)
                for i in range(GH):
                    h = g * GH + i
                    for k in range(NK):
                        c.mm(P[pb][0:64, i * TT:(i + 1) * TT], win[:, k, q * 512 + h * 64:q * 512 + (h + 1) * 64], hn[:, k, :],
                             start=(k == 0), stop=(k == NK - 1), r=("win", thn), w=(("ps", pb),))
                c.act(RAW[q][:, hsl, 1:1 + TT], V3(P[pb][0:64, 0:HTg], TT), AF.Copy, r=(("ps", pb),), w=(("RAW", q, g),), scale=1.0)
            for q in range(3):
                R_ = RAW[q]
                c.tt(tEg, R_[:, hsl, 0:TT], R_[:, hsl, 1:1 + TT], ALU.subtract, r=(("RAW", q, g),), w=(ttE,), eng="pool")
                c.tt(tEg, tEg, bcg(mu[q]), ALU.mult, r=(ttE,), w=(ttE,))
                c.tt(Xs[q], tEg, R_[:, hsl, 1:1 + TT], ALU.add, r=(ttE, ("RAW", q, g)), w=(tXs[q],), eng="pool")
                c.cp(R_[:, hsl, 0:1], R_[:, hsl, TT:TT + 1], r=(("RAW", q, g),), w=(("RAW", q, g),), eng="pool")
            AA = tEg
            pw = cm.bank(GB)
            for i in range(GH):
                h = g * GH + i
                c.mm(P[pw][0:64, i * TT:(i + 1) * TT], w2b[:, h * 64:(h + 1) * 64], twd[:], start=True, stop=True, r=("w2b", "twd"), w=(("ps", pw),))
            for i in range(GH):
                h = g * GH + i
                c.act(LW[:, h, :], P[pw][0:64, i * TT:(i + 1) * TT], AF.Sigmoid, r=(("ps", pw),), w=(tLW,), bias=w0[:, h:h + 1])
            pa = cm.bank(GB)
            for i in range(GH):
                h = g * GH + i
                c.mm(P[pa][0:64, i * TT:(i + 1) * TT], a2b[:, h * 64:(h + 1) * 64], adb[:], start=True, stop=True, r=("a2b", "adb"), w=(("ps", pa),))
            for i in range(GH):
                h = g * GH + i
                c.act(tE[:, h, :], P[pa][0:64, i * TT:(i + 1) * TT], AF.Sigmoid, r=(("ps", pa),), w=(ttE,), bias=a0[:, h:h + 1])
            pg = cm.bank(GB)
            for i in range(GH):
                h = g * GH + i
                c.mm(P[pg][0:64, i * TT:(i + 1) * TT], g2b[:, h * 64:(h + 1) * 64], sgd[:], start=True, stop=True, r=("g2b", "sgd"), w=(("ps", pg),))
            c.cp(GG, V3(P[pg][0:64, 0:HTg], TT), r=(("ps", pg),), w=(tGG,), eng="dve")
            c.act(LWg, LWg, AF.Copy, r=(tLW,), w=(tLW,), scale=-0.6065306597126334)
            c.s.add("dve", lambda e: e.tensor_tensor_scan(fl(CLg), reset[0:64, 0:HTg], fl(LWg), 0.0, ALU.mult, ALU.add),
                    r=(tLW,), w=(tCL,), cost=2 * HTg / 0.96 + 150)
            c.act(Epos, CLg, AF.Exp, r=(tCL,), w=(tEpos,))
            NR = Eg
            c.tt(KKg, Xk, bcg(kkc), ALU.mult, r=(tXk,), w=(tKK,))
            c.act(sqhg, KKg, AF.Square, r=(tKK,), w=(tsqh,))
            pb = cm.bank(GB)
            for i in range(GH):
                h = g * GH + i
                c.mm(P[pb][0:64, i * TT:(i + 1) * TT], ones64[:], sqh[:, h, :], start=True, stop=True, r=(tsqh, "ones64"), w=(("ps", pb),))
            c.act(NR, V3(P[pb][0:64, 0:HTg], TT), AF.Sqrt, r=(("ps", pb),), w=(tEb,))
            c.act(c.dummy2[g], c.dummy, AF.Exp, r=(), w=(("dummy", g),))
            c.ts(NR, NR, 1e-12, ALU.max, r=(tEb,), w=(tEb,))
            c.recip(NR, NR, r=(tEb,), w=(tEb,))
            c.tt(KKg, KKg, NR, ALU.mult, r=(tKK, tEb), w=(tKK,))
            c.stt(Kmodg, AA, 1.0, bcg(kac), ALU.subtract, ALU.mult, r=(ttE,), w=(tKmod,))
            c.stt(Kmodg, Kmodg, 1.0, Xk, ALU.add, ALU.mult, r=(tKmod, tXk), w=(tKmod,))
            c.tt(Bscg, KKg, AA, ALU.mult, r=(tKK, ttE), w=(tBsc,), eng="pool")
            c.tt(Rt, Xr, Epos, ALU.mult, r=(tXr, tEpos), w=(tRt,), eng="pool")
            c.act(Eg, CLg, AF.Exp, r=(tCL, tEb), w=(tEb,), scale=-1.0)
            c.tt(Bt, Bscg, Eg, ALU.mult, r=(tBsc, tEb), w=(tBt,))
            c.tt(Kt, Kmodg, Eg, ALU.mult, r=(tKmod, tEb), w=(tKt,), eng="pool")
            c.tt(Eg, CLg, LWg, ALU.subtract, r=(tCL, tLW, tEb), w=(tEb,))
            c.act(Eg, Eg, AF.Exp, r=(tEb,), w=(tEb,))
            c.stt(At, KKg, -1.0, Eg, ALU.mult, ALU.mult, r=(tKK, tEb), w=(tAt,))
            CL4 = CLg.rearrange("p h (c l) -> p h c l", l=64)
            c.tt(Eg.rearrange("p h (c l) -> p h c l", l=64), CL4[:, :, :, 63:64].broadcast_to([64, GH, NC_, 64]), CL4, ALU.subtract,
                 r=(tCL, tEb), w=(tEb,), eng="pool")
            c.act(Eg, Eg, AF.Exp, r=(tEb,), w=(tEb,))
            c.tt(Bh, Bscg, Eg, ALU.mult, r=(tBsc, tEb), w=(tBh,))
            c.tt(Kh, Kmodg, Eg, ALU.mult, r=(tKmod, tEb), w=(tKh,), eng="pool")
            c.tt(tEg, Xr, Kmodg, ALU.mult, r=(tXr, tKmod, ttE), w=(ttE,), eng="pool")
            c.tt(sqhg, tEg, bcg(rkc), ALU.mult, r=(ttE,), w=(tsqh,))
            pb = cm.bank(GB)
            for i in range(GH):
                h = g * GH + i
                c.mm(P[pb][0:64, i * TT:(i + 1) * TT], ones64[:], sqh[:, h, :], start=True, stop=True, r=(tsqh, "ones64"), w=(("ps", pb),))
            c.tt(BON, V3(P[pb][0:64, 0:HTg], TT), Xv, ALU.mult, r=(("ps", pb), tXv), w=(tBON,))

        def main(it, g):
            b = it % 2
            t0 = it * TT
            GA = GMAIN[g]
            hsl = slice(g * GH, (g + 1) * GH)
            HTg = GH * TT
            W = GH * 64
            Xv, Bh, Kh, Epos, BON, GG = [t_[b] for t_ in (Xv2, Bh2, Kh2, Epos2, BON2, GG2)]
            At, Rt, Bt, Kt = [t_[b] for t_ in (At2, Rt2, Bt2, Kt2)]
            tXv, tBh, tKh, tEpos, tBON, tGG = [(n, b, g) for n in ("Xv", "Bh", "Kh", "Epos", "BON", "GG")]
            tAt, tRt, tBt, tKt = [(n, b, g) for n in ("At", "Rt", "Bt", "Kt")]
            T_ = lambda nm: (nm, g)
            bcm = lambda ap: ap[:, hsl].unsqueeze(2).broadcast_to([64, GH, TT])
            mk = lambda m_: m_[0:64, 0:64].unsqueeze(1).broadcast_to([64, GH, 64])
            hs_of = lambda i: slice(i * 64, (i + 1) * 64)
            for jc in range(NC_):
                cols = slice(jc * 64, (jc + 1) * 64)
                for (src, stok, dst, tok) in ((Xv, tXv, VT, "VT"), (Bh, tBh, BhT, "BhT"), (Kh, tKh, KhT, "KhT")):
                    pb = cm.bank(GA)
                    for i in range(GH):
                        c.tr(P[pb][0:64, hs_of(i)], src[:, g * GH + i, cols], ident[0:64, 0:64], r=(stok,), w=(("ps", pb),))
                    c.act(dst[:, hsl, :], V3(P[pb][0:64, 0:W]), AF.Copy, r=(("ps", pb),), w=(T_(tok),), scale=1.0)
                for (lt, ltok, rt, rtok, dst, tok, msk) in ((Bt, tBt, At, tAt, Aab, "Aab", m_strict), (At, tAt, Bt, tBt, AabT, "AabT", m_low),
                                                            (Kt, tKt, At, tAt, Aak, "Aak", m_strict),
                                                            (Bt, tBt, Rt, tRt, Arb, "Arb", m_incl), (Kt, tKt, Rt, tRt, Ark, "Ark", m_incl)):
                    pb = cm.bank(GA)
                    for i in range(GH):
                        h = g * GH + i
                        c.mm(P[pb][0:64, hs_of(i)], lt[:, h, cols], rt[:, h, cols], start=True, stop=True, r=(ltok, rtok), w=(("ps", pb),))
                    c.tt(dst[:, hsl, :], V3(P[pb][0:64, 0:W]), mk(msk), ALU.mult, r=(("ps", pb),), w=(T_(tok),))
                c.tt(Tm[0][:, hsl, :], Aab[:, hsl, :], mk(ident), ALU.add, r=(T_("Aab"),), w=(("Tm", 0, g),), eng="pool")
                A_c, AT_c, tokA, tokAT = Aab, AabT, T_("Aab"), T_("AabT")
                tcur = 0
                for lvl, kpow in enumerate((1, 2, 4, 8, 16, 32)):
                    nb = lvl % 2
                    if kpow >= 2:
                        pz = cm.bank(GA)
                        for i in range(GH):
                            h = g * GH + i
                            c.mm(P[pz][0:64, hs_of(i)], AT_c[:, h, :], Tm[tcur][:, h, :], start=True, stop=True,
                                 r=(tokAT, ("Tm", tcur, g)), w=(("ps", pz),))
                        c.tt(Tm[1 - tcur][:, hsl, :], V3(P[pz][0:64, 0:W]), Tm[tcur][:, hsl, :], ALU.add,
                             r=(("ps", pz), ("Tm", tcur, g)), w=(("Tm", 1 - tcur, g),))
                        tcur = 1 - tcur
                    if kpow <= 16:
                        py = cm.bank(GA)
                        for i in range(GH):
                            h = g * GH + i
                            c.mm(P[py][0:64, hs_of(i)], A_c[:, h, :], AT_c[:, h, :], start=True, stop=True, r=(tokA, tokAT), w=(("ps", py),))
                        if kpow <= 8:
                            px_ = cm.bank(GA)
                            for i in range(GH):
                                h = g * GH + i
                                c.mm(P[px_][0:64, hs_of(i)], AT_c[:, h, :], A_c[:, h, :], start=True, stop=True, r=(tokA, tokAT), w=(("ps", px_),))
                            c.act(Ak[nb][:, hsl, :], V3(P[px_][0:64, 0:W]), AF.Copy, r=(("ps", px_),), w=(("Ak", nb, g),), scale=1.0)
                        c.cp(AkT[nb][:, hsl, :], V3(P[py][0:64, 0:W]), r=(("ps", py),), w=(("AkT", nb, g),), eng="dve")
                        A_c, AT_c, tokA, tokAT = Ak[nb], AkT[nb], ("Ak", nb, g), ("AkT", nb, g)
                Tf, tokT = Tm[tcur], ("Tm", tcur, g)
                pb = cm.bank(GA)
                for i in range(GH):
                    h = g * GH + i
                    c.mm(P[pb][0:64, hs_of(i)], At[:, h, cols], S0b[:, h, :], start=True, stop=False, r=(tAt, T_("S0b")), w=(("ps", pb),))
                    c.mm(P[pb][0:64, hs_of(i)], Aak[:, h, :], VT[:, h, :], start=False, stop=True, r=(T_("Aak"), T_("VT")), w=(("ps", pb),))
                c.act(XTs[:, hsl, :], V3(P[pb][0:64, 0:W]), AF.Copy, r=(("ps", pb),), w=(T_("XTs"),), scale=1.0)
                pb = cm.bank(GA)
                for i in range(GH):
                    h = g * GH + i
                    c.mm(P[pb][0:64, hs_of(i)], Tf[:, h, :], XTs[:, h, :], start=True, stop=True, r=(tokT, T_("XTs")), w=(("ps", pb),))
                c.cp(UTs[:, hsl, :], V3(P[pb][0:64, 0:W]), r=(("ps", pb),), w=(T_("UTs"),), eng="dve")
                pb = cm.bank(GA)
                for i in range(GH):
                    h = g * GH + i
                    c.mm(P[pb][0:64, hs_of(i)], S0b[:, h, :], Rt[:, h, cols], start=True, stop=False, r=(T_("S0b"), tRt), w=(("ps", pb),))
                    c.mm(P[pb][0:64, hs_of(i)], UTs[:, h, :], Arb[:, h, :], start=False, stop=False, r=(T_("UTs"), T_("Arb")), w=(("ps", pb),))
                    c.mm(P[pb][0:64, hs_of(i)], VT[:, h, :], Ark[:, h, :], start=False, stop=True, r=(T_("VT"), T_("Ark")), w=(("ps", pb),))
                c.act(YY[:, hsl, cols], V3(P[pb][0:64, 0:W]), AF.Copy, r=(("ps", pb),), w=(T_("YY"),), scale=1.0)
                pb = cm.bank(GA)
                for i in range(GH):
                    h = g * GH + i
                    c.mm(P[pb][0:64, hs_of(i)], BhT[:, h, :], UTs[:, h, :], start=True, stop=False, r=(T_("BhT"), T_("UTs")), w=(("ps", pb),))
                    c.mm(P[pb][0:64, hs_of(i)], KhT[:, h, :], VT[:, h, :], start=False, stop=True, r=(T_("KhT"), T_("VT")), w=(("ps", pb),))
                c.tt(S0[:, hsl, :], S0[:, hsl, :], Epos[:, hsl, jc * 64 + 63:jc * 64 + 64].broadcast_to([64, GH, 64]), ALU.mult,
                     r=(T_("S0"), tEpos), w=(T_("S0"),))
                c.tt(S0[:, hsl, :], S0[:, hsl, :], V3(P[pb][0:64, 0:W]), ALU.add, r=(T_("S0"), ("ps", pb)), w=(T_("S0"),))
                c.act(S0b[:, hsl, :], S0[:, hsl, :], AF.Copy, r=(T_("S0"),), w=(T_("S0b"),), scale=1.0)
            YYg, sq2g, NR2g = YY[:, hsl, :], sqh2[:, hsl, :], NR2[:, hsl, :]
            c.act(sq2g, YYg, AF.Copy, r=(T_("YY"),), w=(T_("sqh2"),), scale=1.0)
            pb = cm.bank(GA)
            for i in range(GH):
                h = g * GH + i
                c.mm(P[pb][0:64, i * TT:(i + 1) * TT], ones64[:], sqh2[:, h, :], start=True, stop=True, r=(T_("sqh2"), "ones64"), w=(("ps", pb),))
            c.stt(YYg, V3(P[pb][0:64, 0:HTg], TT), -1.0 / 64.0, YYg, ALU.mult, ALU.add, r=(("ps", pb), T_("YY")), w=(T_("YY"),))
            c.act(sq2g, YYg, AF.Square, r=(T_("YY"),), w=(T_("sqh2"),))
            pb = cm.bank(GA)
            for i in range(GH):
                h = g * GH + i
                c.mm(P[pb][0:64, i * TT:(i + 1) * TT], ones64[:], sqh2[:, h, :], start=True, stop=True, r=(T_("sqh2"), "ones64"), w=(("ps", pb),))
            c.act(NR2g, V3(P[pb][0:64, 0:HTg], TT), AF.Ln, r=(("ps", pb), "lneps"), w=(T_("NR2"),), bias=lneps[:], scale=1.0 / 64.0)
            c.act(NR2g, NR2g, AF.Exp, r=(T_("NR2"),), w=(T_("NR2"),), scale=-0.5)
            c.tt(YYg, YYg, NR2g, ALU.mult, r=(T_("YY"), T_("NR2")), w=(T_("YY"),))
            c.tt(YYg, YYg, bcm(lnw), ALU.mult, r=(T_("YY"),), w=(T_("YY"),), eng="pool")
            c.tt(YYg, YYg, bcm(lnb), ALU.add, r=(T_("YY"),), w=(T_("YY"),), eng="pool")
            c.tt(YYg, YYg, BON[:, hsl, :], ALU.add, r=(T_("YY"), tBON), w=(T_("YY"),))
            c.tt(sq2g, YYg, GG[:, hsl, :], ALU.mult, r=(T_("YY"), tGG), w=(T_("sqh2"),))
            c.dma(ya_v[:, hsl, t0:t0 + TT], sq2g, r=(T_("sqh2"),), w=(),
                  wacc=tuple(("dram", ya_out[1], q) for q in range(t0 // 64, (t0 + TT) // 64)))

        S = c.s
        shared(0)
        cm.prefetch(1, g_pre, x_in, NT)
        for g in range(NG):
            pre(0, g)
        noil = os.environ.get("MK_NOIL", "0") == "1"
        pr = float(os.environ.get("MK_PRIO", "-300"))
        for it in range(NT):
            if it + 1 < NT:
                shared(it + 1)
                cm.prefetch(it + 2, g_pre, x_in, NT)
            thr = [S.capture(main, it, g) for g in range(NG)]
            if it + 1 < NT:
                thr += [S.capture(pre, it + 1, g) for g in range(NG)]
            if noil:
                for t_ in thr:
                    S.merge(t_, [])
            else:
                S.merge_n(thr, prio=[0.0, 0.0, pr, pr][:len(thr)])
        c.s.barrier()


def pack_consts(inputs):
    cols = []
    index = {}

    def add(name, arr2d):
        off = sum(a.shape[1] for a in cols)
        cols.append(np.ascontiguousarray(arr2d, dtype=np.float32))
        index[name] = (off, arr2d.shape[1])

    for nm in ("norm_mix_pre", "norm_mix_post", "norm_xa_pre", "norm_xa_post", "norm_mem", "norm_ff_pre", "norm_ff_post"):
        a = inputs[nm]
        for l in range(a.shape[0]):
            add((nm, l), a[l].reshape(NK, 128).T)
    a = inputs["ml_norm_w"]
    for l in range(a.shape[0]):
        add(("ml_norm_w", l), a[l].reshape(8, 128).T)
    a = inputs["ml_b_gates"]
    for l in range(a.shape[0]):
        z = np.zeros((128, 1), np.float32)
        z[0:8, 0] = a[l][0:8]
        add(("ml_b_i", l), z)
        z = np.zeros((128, 1), np.float32)
        z[0:8, 0] = a[l][8:16]
        add(("ml_b_f", l), z)
    def hl(v512):
        z = np.zeros((128, 8), np.float32)
        z[0:64] = np.asarray(v512).reshape(8, 64).T
        return z
    for l in range(inputs["rw_mu"].shape[0]):
        mu_ = inputs["rw_mu"][l]
        add(("rw_mu_r", l), hl(mu_[0:512]))
        add(("rw_mu_k", l), hl(mu_[512:1024]))
        add(("rw_mu_v", l), hl(mu_[1024:1536]))
        z = np.zeros((128, 3), np.float32)
        z[0:64, 0] = mu_[1536:1600]
        z[0:64, 1] = mu_[1600:1664]
        z[:, 2] = mu_[1664:1792]
        add(("rw_mu_l", l), z)
        for nm in ("rw_w0", "rw_a0", "rw_k_k", "rw_k_a", "rw_ln_w", "rw_ln_b"):
            add((nm, l), hl(inputs[nm][l]))
        add(("rw_r_k", l), hl(inputs["rw_r_k"][l].reshape(512)))
    for l in range(inputs["mb_conv_w"].shape[0]):
        cw = inputs["mb_conv_w"][l]
        cb = inputs["mb_conv_b"][l]
        for i in range(4):
            z = np.zeros((128, 8), np.float32)
            z[0:64] = cw[i, 0:512].reshape(8, 64).T
            add(("mb_cwx%d" % i, l), z)
            add(("mb_cwb%d" % i, l), cw[i, 512:1024].reshape(4, 128).T)
        z = np.zeros((128, 8), np.float32)
        z[0:64] = cb[0:512].reshape(8, 64).T
        add(("mb_cbx", l), z)
        add(("mb_cbb", l), cb[512:1024].reshape(4, 128).T)
        for nm in ("mb_dt_bias", "mb_A_log"):
            z = np.zeros((128, 1), np.float32)
            z[0:8, 0] = inputs[nm][l]
            add((nm, l), z)
        add(("mb_D", l), np.broadcast_to(inputs["mb_D"][l][None, :], (128, 8)))
        z = np.zeros((128, 8), np.float32)
        z[0:64] = inputs["mb_norm_w"][l].reshape(8, 64).T
        add(("mb_norm_w", l), z)
    p = np.arange(128)[:, None]
    t = np.arange(64)[None, :]
    add(("neg_mask", 0), np.where((p % 64) <= t, 0.0, -30000.0).astype(np.float32))
    sel128 = np.zeros((128, 8 * 128), np.float32)
    for h in range(8):
        sel128[h, h * 128:(h + 1) * 128] = 1.0
    add(("sel128", 0), sel128)
    add(("mask_incl", 0), ((p % 64) <= t).astype(np.float32))
    add(("mask_strict", 0), ((p % 64) < t).astype(np.float32))
    add(("mask_low", 0), ((p % 64) > t).astype(np.float32))
    add(("ident", 0), np.eye(128, dtype=np.float32))
    selm = np.zeros((128, 4 * 128), np.float32)
    for h in range(8):
        hp = h // 2
        selm[h, hp * 128 + (h % 2) * 64: hp * 128 + (h % 2) * 64 + 64] = 1.0
    selh = np.zeros((128, 8 * 64), np.float32)
    for h in range(8):
        selh[h, h * 64:(h + 1) * 64] = 1.0
    add(("selh", 0), selh)
    rst = np.ones((128, 1024), np.float32)
    rst[:, ::64] = 0.0
    add(("reset", 0), rst)
    return np.concatenate(cols, axis=1), index


def build(phases, consts_np, cindex):
    nc = bass.Bass("TRN2", target_bir_lowering=False)
    c = Ctx(nc)
    dr = {}

    def din(name, shape):
        dr[name] = nc.dram_tensor(name, list(shape), F32, kind="ExternalInput")
        return dr[name]

    ncst = consts_np.shape[1]
    x0 = din("xT", [NK, 128, T])
    cst_d = din("consts", [128, ncst])
    out = nc.dram_tensor("outT", [NK, 128, T], F32, kind="ExternalOutput")
    with ExitStack() as es:
        cst_sb = es.enter_context(nc.sbuf_tensor("cst", [128, ncst], F32))
        ones = es.enter_context(nc.sbuf_tensor("ones", [128, 128], BF16))
        epst = es.enter_context(nc.sbuf_tensor("epst", [128, 1], F32))
        c.psum = [es.enter_context(nc.psum_tensor("ps%d" % i, [128, 512], F32)) for i in range(8)]
        c.ones = ones[:]
        c.eps_ap = epst[:]
        dmy = es.enter_context(nc.sbuf_tensor("dmy", [128, 1], F32))
        c.dummy = dmy[:]
        c.memset(dmy[:], 0.0, w=("dummy",))
        dmy2 = es.enter_context(nc.sbuf_tensor("dmy2", [128, 4], F32))
        c.dummy2 = [dmy2[:, i:i + 1] for i in range(4)]
        c.memset(dmy2[:], 0.0, w=tuple(("dummy", i) for i in range(4)))
        c.dma(cst_sb[:], cst_d.ap(), r=(), w=("cst",))
        c.memset(ones[:], 1.0, w=("ones",))
        c.memset(epst[:], EPS, w=("eps",))
        c.s.barrier()
        cst = {}
        for (nm, l), (off, n) in cindex.items():
            cst.setdefault(nm, {})[l] = cst_sb[:, off:off + n]
        cur = (x0.ap(), "xT")
        for pi_, ph in enumerate(phases):
            last = pi_ == len(phases) - 1
            if last:
                dst = (out.ap(), "outT")
            elif ph[0] == "rwkv":
                dst = None
            else:
                dst = (nc.dram_tensor("scr%d" % pi_, [NK, 128, T], F32, kind="Internal").ap(), "scr%d" % pi_)
            kind = ph[0]
            if kind == "mlp":
                l = ph[1]
                wu = din("ff_up%d" % l, [D, DFF])
                wdn = din("ff_down%d" % l, [DFF, D])
                phase_mlp(c, l, cur, dst, wu.ap(), wdn.ap(), cst)
            elif kind == "rwkv":
                l = ph[1]
                wi = din("ev_w_in_rw", [D, 1792])
                w2_ = din("rw_w2", [64, 512])
                a2_ = din("rw_a2", [64, 512])
                g2_ = din("rw_g2", [128, 512])
                if len(ph) > 2 and ph[2] == "out":
                    ya_t = (nc.dram_tensor("ya_out", [4, 128, T], BF16, kind="ExternalOutput").ap(), "ya_out")
                else:
                    ya_t = (nc.dram_tensor("ya_scr%d" % l, [4, 128, T], BF16, kind="Internal").ap(), "ya_scr%d" % l)
                    dr["ya_scr"] = ya_t
                phase_rwkv(c, l, l // 2, cur, ya_t, wi.ap(), w2_.ap(), a2_.ap(), g2_.ap(), cst)
                dst = cur
            elif kind == "mamba":
                l = ph[1]
                ya_t = dr.get("ya_scr")
                if ya_t is None:
                    ya_t = (nc.dram_tensor("ya_in", [4, 128, T], BF16, kind="ExternalInput").ap(), "ya_in")
                wi = din("ev_w_in_mb", [D, 1544])
                woa_ = din("ev_w_out_a", [512, D])
                wob_ = din("ev_w_out_b", [8, 64, D])
                phase_mamba(c, l, l // 2, cur, dst, ya_t, wi.ap(), woa_.ap(), wob_.ap(), cst)
            elif kind == "mlstm":
                l = ph[1]
                wi = din("ml_w_in", [D, 3088])
                wo_ = din("ml_w_out", [D, D])
                phase_mlstm(c, l, l // 2, cur, dst, wi.ap(), wo_.ap(), cst)
            elif kind == "xattn":
                l = ph[1]
                if "memT" not in dr:
                    din("memT", [NK, 128, NMEM])
                ws = [din("xa_w%s%d" % (nm, l), [D, D]).ap() for nm in "qkvo"]
                phase_xattn(c, l, cur, dst, dr["memT"].ap(), ws[0], ws[1], ws[2], ws[3], cst)
            cur = dst
        fin = [("dram", "outT", it) for it in range(64)] + [("dram", "ya_out", it) for it in range(64)]
        c.s.add("sp", lambda e: e.nop(), r=fin)
        c.s.emit(nc, es)
    return nc, c


def phase_inputs(phases, inputs, b):
    m = {}
    for ph in phases:
        if ph[0] == "mlp":
            l = ph[1]
            m["ff_up%d" % l] = np.ascontiguousarray(inputs["ff_up"][l])
            m["ff_down%d" % l] = np.ascontiguousarray(inputs["ff_down"][l])
        elif ph[0] == "rwkv":
            jj = ph[1] // 2
            m["ev_w_in_rw"] = np.ascontiguousarray(inputs["ev_w_in"][jj][:, 0:1792])
            m["rw_w2"] = np.ascontiguousarray(inputs["rw_w2"][jj])
            m["rw_a2"] = np.ascontiguousarray(inputs["rw_a2"][jj])
            m["rw_g2"] = np.ascontiguousarray(inputs["rw_g2"][jj])
        elif ph[0] == "mamba":
            jj = ph[1] // 2
            m["ev_w_in_mb"] = np.ascontiguousarray(inputs["ev_w_in"][jj][:, 1792:3336])
            m["ev_w_out_a"] = np.ascontiguousarray(inputs["ev_w_out"][jj][0:512])
            m["ev_w_out_b"] = np.ascontiguousarray(inputs["ev_w_out"][jj][512:1024].reshape(8, 64, D))
        elif ph[0] == "mlstm":
            jj = ph[1] // 2
            m["ml_w_in"] = np.ascontiguousarray(inputs["ml_w_in"][jj])
            m["ml_w_out"] = np.ascontiguousarray(inputs["ml_w_out"][jj])
        elif ph[0] == "xattn":
            l = ph[1]
            m["memT"] = np.ascontiguousarray(inputs["mem"][b].T.reshape(NK, 128, NMEM))
            xa = {"q": inputs["xa_wq"], "k": inputs["xa_wk"], "v": inputs["xa_wv"], "o": inputs["xa_wo"]}
            for nm in "qkvo":
                m["xa_w%s%d" % (nm, l)] = np.ascontiguousarray(xa[nm][l])
    return m


FULL_PHASES = [("rwkv", 0), ("mamba", 0), ("xattn", 0), ("mlp", 0), ("mlstm", 1), ("xattn", 1), ("mlp", 1)]
_CACHE = {}


def kernel(**inputs):
    inputs = {k: np.asarray(v) for k, v in inputs.items()}
    consts, cindex = pack_consts(inputs)
    key = consts.shape
    if key not in _CACHE:
        _CACHE[key] = build(FULL_PHASES, consts, cindex)[0]
    nc = _CACHE[key]
    B = inputs["x"].shape[0]
    shared = phase_inputs(FULL_PHASES, inputs, 0)
    in_maps = []
    for b in range(B):
        m = dict(shared)
        m["xT"] = np.ascontiguousarray(inputs["x"][b].T.reshape(NK, 128, T))
        m["memT"] = np.ascontiguousarray(inputs["mem"][b].T.reshape(NK, 128, NMEM))
        m["consts"] = consts
        in_maps.append(m)
    res = run_bass_kernel_spmd(nc, in_maps, core_ids=list(range(B)))
    out = np.empty((B, T, D), np.float32)
    for b in range(B):
        out[b] = np.asarray(res.results[b]["outT"]).reshape(D, T).T
    return out
```
